# Optimizing a Trainium2 kernel written in Bass

```python
import jax
import jax.numpy as jnp
from jax import lax
import numpy as np

D_MODEL = 2048
BATCH = 4
SEQ = 2048
DEPTH = 4

HEAD_DIM = 128
ROPE_THETA = 10000.0
Q_BLOCK = 128
NEG_INF = -1e30
FOX_HEADS = 6
NSA_HEADS = 4
CMP_LEN = 32
CMP_STRIDE = 16
CMP_HIDDEN = 256
SLC_LEN = 64
SLC_TOPN = 16
WIN = 512
SLC_FORCE = 1e4
DSA_HEADS = 6
DSA_KV_RANK = 256
IDX_HEADS = 16
IDX_DIM = 64
IDX_TOPK_MAX = 256
D_FF = ((8 * D_MODEL + 3 * 256 - 1) // (3 * 256)) * 256
PLE_DIM = 256
ALPHA = (2 * DEPTH) ** 0.25
BETA = (8 * DEPTH) ** -0.25

SPLITS = (
    ('fox_q', FOX_HEADS * HEAD_DIM), ('fox_k', FOX_HEADS * HEAD_DIM), ('fox_v', FOX_HEADS * HEAD_DIM), ('fox_f', FOX_HEADS),
    ('nsa_q', NSA_HEADS * HEAD_DIM), ('nsa_kc', HEAD_DIM), ('nsa_vc', HEAD_DIM), ('nsa_ks', HEAD_DIM), ('nsa_vs', HEAD_DIM),
    ('nsa_kw', HEAD_DIM), ('nsa_vw', HEAD_DIM), ('nsa_g', 3 * NSA_HEADS),
    ('dsa_q', DSA_HEADS * HEAD_DIM), ('dsa_ckv', DSA_KV_RANK), ('idx_q', IDX_HEADS * IDX_DIM), ('idx_k', IDX_DIM), ('idx_w', IDX_HEADS),
    ('gate', 3 * D_MODEL),
)
IN_COLS = sum(w for _, w in SPLITS)

kernel_name = 'hybrid_fox_nsa_dsa_deepnorm'


def split_cols(z):
    offs = [int(o) for o in np.cumsum([w for _, w in SPLITS])[:-1]]
    parts = jnp.split(z, offs, axis=-1)
    return {name: part for (name, _), part in zip(SPLITS, parts)}


def heads(t, n):
    return t.reshape(t.shape[0], t.shape[1], n, -1)


def layer_norm(x, g, b, eps=1e-5):
    xf = x.astype(jnp.float32)
    mu = jnp.mean(xf, axis=-1, keepdims=True)
    var = jnp.mean(jnp.square(xf - mu), axis=-1, keepdims=True)
    return ((xf - mu) * lax.rsqrt(var + eps)).astype(x.dtype) * g + b


def rms_norm(x, g, eps=1e-6):
    xf = x.astype(jnp.float32)
    return (xf * lax.rsqrt(jnp.mean(jnp.square(xf), axis=-1, keepdims=True) + eps)).astype(x.dtype) * g


def rope_tables(n, dim):
    inv = 1.0 / (ROPE_THETA ** (jnp.arange(0, dim, 2, dtype=jnp.float32) / dim))
    ang = jnp.arange(n, dtype=jnp.float32)[:, None] * inv[None, :]
    return jnp.cos(ang), jnp.sin(ang)


def apply_rope(x, cos, sin):
    x1, x2 = jnp.split(x, 2, axis=-1)
    c = cos[:, None, :].astype(x.dtype)
    s = sin[:, None, :].astype(x.dtype)
    return jnp.concatenate([x1 * c - x2 * s, x1 * s + x2 * c], axis=-1)


def rope1(t, cos, sin):
    return apply_rope(t[:, :, None], cos, sin)[:, :, 0]


def to_blocks(a):
    B, S = a.shape[:2]
    return jnp.moveaxis(a.reshape((B, S // Q_BLOCK, Q_BLOCK) + a.shape[2:]), 1, 0)


def from_blocks(a):
    nb, B, Q = a.shape[:3]
    return jnp.moveaxis(a, 0, 1).reshape((B, nb * Q) + a.shape[3:])


def fox_attention(q, k, v, f_logit, f_bias):
    B, S, H, dh = q.shape
    pos = jnp.arange(S)
    scale = dh ** -0.5
    logf = jax.nn.log_sigmoid(f_logit.astype(jnp.float32) + f_bias.astype(jnp.float32))
    cum = jnp.cumsum(logf, axis=1)
    cum_k = jnp.transpose(cum, (0, 2, 1))

    def block(args):
        qb, cq, qpos = args
        s = jnp.einsum('bqhd,bkhd->bhqk', qb, k).astype(jnp.float32) * scale
        s = s + jnp.transpose(cq, (0, 2, 1))[..., None] - cum_k[:, :, None, :]
        s = jnp.where(pos[None, None, None, :] <= qpos[None, None, :, None], s, NEG_INF)
        pr = jax.nn.softmax(s, axis=-1).astype(v.dtype)
        return jnp.einsum('bhqk,bkhd->bqhd', pr, v)

    out = lax.map(block, (to_blocks(q), to_blocks(cum), pos.reshape(-1, Q_BLOCK)))
    return from_blocks(out)


def nsa_attention(q, kc_raw, vc_raw, ks, vs, kw, vw, g_logit, pe_k, pe_v, wk1, wk2, wv1, wv2, cos, sin):
    B, S, H, dh = q.shape
    pos = jnp.arange(S)
    scale = dh ** -0.5
    n_cmp = (S - CMP_LEN) // CMP_STRIDE + 1
    c_start = jnp.arange(n_cmp) * CMP_STRIDE
    c_end = c_start + CMP_LEN - 1
    tok = c_start[:, None] + jnp.arange(CMP_LEN)[None, :]

    def compress(raw, pe, w1, w2):
        blk = (raw[:, tok] + pe).reshape(B, n_cmp, CMP_LEN * dh)
        return jax.nn.gelu(blk @ w1) @ w2

    k_cmp = rope1(compress(kc_raw, pe_k, wk1, wk2), cos[c_end], sin[c_end])
    v_cmp = compress(vc_raw, pe_v, wv1, wv2)
    cmask = c_end[None, :] <= pos[:, None]
    s_c = jnp.einsum('bshd,bcd->bhsc', q, k_cmp).astype(jnp.float32) * scale
    p_c = jnp.where(cmask, jax.nn.softmax(jnp.where(cmask, s_c, NEG_INF), axis=-1), 0.0)
    o_cmp = jnp.einsum('bhsc,bcd->bshd', p_c.astype(v_cmp.dtype), v_cmp)
    n_slc = S // SLC_LEN
    s_start = jnp.arange(n_slc) * SLC_LEN
    overlap = jnp.maximum(jnp.minimum(c_end[:, None], s_start[None, :] + SLC_LEN - 1)
                          - jnp.maximum(c_start[:, None], s_start[None, :]) + 1, 0).astype(jnp.float32) / CMP_LEN
    imp = jnp.einsum('bhsc,cj->bsj', p_c, overlap)
    blk_id = jnp.arange(n_slc)[None, :]
    cur = (pos // SLC_LEN)[:, None]
    forced = (blk_id == 0) | (blk_id == cur) | (blk_id == cur - 1)
    imp = jnp.where(forced, SLC_FORCE, jnp.where(s_start[None, :] <= pos[:, None], imp, -SLC_FORCE))
    n_sel = min(SLC_TOPN, n_slc)
    _, sel = lax.top_k(imp, n_sel)
    ks_blk = ks.reshape(B, n_slc, SLC_LEN, dh)
    vs_blk = vs.reshape(B, n_slc, SLC_LEN, dh)
    kw_pad = jnp.pad(kw, ((0, 0), (WIN, 0), (0, 0)))
    vw_pad = jnp.pad(vw, ((0, 0), (WIN, 0), (0, 0)))
    band = WIN + Q_BLOCK
    gather = jax.vmap(lambda kb, ib: kb[ib])

    def block(args):
        qb, selb, qpos, start = args
        gk = gather(ks_blk, selb).reshape(B, Q_BLOCK, n_sel * SLC_LEN, dh)
        gv = gather(vs_blk, selb).reshape(B, Q_BLOCK, n_sel * SLC_LEN, dh)
        kpos = (selb[..., None] * SLC_LEN + jnp.arange(SLC_LEN)).reshape(B, Q_BLOCK, n_sel * SLC_LEN)
        s = jnp.einsum('bqhd,bqkd->bhqk', qb, gk).astype(jnp.float32) * scale
        s = jnp.where((kpos <= qpos[None, :, None])[:, None], s, NEG_INF)
        o_s = jnp.einsum('bhqk,bqkd->bqhd', jax.nn.softmax(s, axis=-1).astype(gv.dtype), gv)
        kwb = lax.dynamic_slice_in_dim(kw_pad, start, band, axis=1)
        vwb = lax.dynamic_slice_in_dim(vw_pad, start, band, axis=1)
        wpos = start - WIN + jnp.arange(band)
        dist = qpos[:, None] - wpos[None, :]
        wmask = (dist >= 0) & (dist < WIN) & (wpos[None, :] >= 0)
        s_w = jnp.einsum('bqhd,bkd->bhqk', qb, kwb).astype(jnp.float32) * scale
        s_w = jnp.where(wmask[None, None], s_w, NEG_INF)
        o_w = jnp.einsum('bhqk,bkd->bqhd', jax.nn.softmax(s_w, axis=-1).astype(vwb.dtype), vwb)
        return o_s, o_w

    nb = S // Q_BLOCK
    o_slc, o_win = lax.map(block, (to_blocks(q), to_blocks(sel), pos.reshape(nb, Q_BLOCK), jnp.arange(nb) * Q_BLOCK))
    g = jax.nn.sigmoid(g_logit).reshape(B, S, H, 3)
    return g[..., 0:1] * o_cmp + g[..., 1:2] * from_blocks(o_slc) + g[..., 2:3] * from_blocks(o_win)


def dsa_attention(q, k, v, iq, ik, iw):
    B, S, H, dh = q.shape
    pos = jnp.arange(S)
    scale = dh ** -0.5
    k_top = min(IDX_TOPK_MAX, S // 4)
    iw = iw.astype(jnp.float32) * (IDX_HEADS ** -0.5 * IDX_DIM ** -0.5)
    gather = jax.vmap(lambda kb, ib: kb[ib])

    def block(args):
        qb, iqb, iwb, qpos = args
        sc = jax.nn.relu(jnp.einsum('bqhd,bkd->bqhk', iqb, ik).astype(jnp.float32))
        sc = jnp.einsum('bqhk,bqh->bqk', sc, iwb)
        sc = jnp.where(pos[None, None, :] <= qpos[None, :, None], sc, NEG_INF)
        _, idx = lax.top_k(sc, k_top)
        gk = gather(k, idx)
        gv = gather(v, idx)
        s = jnp.einsum('bqhd,bqkd->bhqk', qb, gk).astype(jnp.float32) * scale
        s = jnp.where((idx <= qpos[None, :, None])[:, None], s, NEG_INF)
        return jnp.einsum('bhqk,bqkd->bqhd', jax.nn.softmax(s, axis=-1).astype(gv.dtype), gv)

    out = lax.map(block, (to_blocks(q), to_blocks(iq), to_blocks(iw), pos.reshape(-1, Q_BLOCK)))
    return from_blocks(out)


def setup_inputs(seed: int = 0) -> dict:
    key = jax.random.key(seed)
    ks = jax.random.split(key, 24)
    L, D = DEPTH, D_MODEL
    cw = CMP_LEN * HEAD_DIM

    def nrm(k, shape, s):
        return jax.random.normal(k, shape, jnp.float32) * s

    return {
        'x': nrm(ks[0], (BATCH, SEQ, D), 1.0),
        'p': nrm(ks[1], (DEPTH, BATCH, SEQ, PLE_DIM), 1.0),
        'w_in': nrm(ks[2], (L, D, IN_COLS), D ** -0.5),
        'fox_f_bias': 3.0 + nrm(ks[3], (L, FOX_HEADS), 0.5),
        'nsa_pe_k': nrm(ks[4], (L, CMP_LEN, HEAD_DIM), 0.1),
        'nsa_pe_v': nrm(ks[5], (L, CMP_LEN, HEAD_DIM), 0.1),
        'nsa_cmp_k1': nrm(ks[6], (L, cw, CMP_HIDDEN), cw ** -0.5),
        'nsa_cmp_k2': nrm(ks[7], (L, CMP_HIDDEN, HEAD_DIM), CMP_HIDDEN ** -0.5),
        'nsa_cmp_v1': nrm(ks[8], (L, cw, CMP_HIDDEN), cw ** -0.5),
        'nsa_cmp_v2': nrm(ks[9], (L, CMP_HIDDEN, HEAD_DIM), CMP_HIDDEN ** -0.5),
        'dsa_kv_norm': 1.0 + nrm(ks[10], (L, DSA_KV_RANK), 0.02),
        'dsa_kv_up': nrm(ks[11], (L, DSA_KV_RANK, 2 * HEAD_DIM), DSA_KV_RANK ** -0.5),
        'w_br_fox': nrm(ks[12], (L, FOX_HEADS * HEAD_DIM, D), (FOX_HEADS * HEAD_DIM) ** -0.5),
        'w_br_nsa': nrm(ks[13], (L, NSA_HEADS * HEAD_DIM, D), (NSA_HEADS * HEAD_DIM) ** -0.5),
        'w_br_dsa': nrm(ks[14], (L, DSA_HEADS * HEAD_DIM, D), (DSA_HEADS * HEAD_DIM) ** -0.5),
        'w_out': nrm(ks[15], (L, D, D), BETA * D ** -0.5),
        'ln1_g': 1.0 + nrm(ks[16], (L, D), 0.02),
        'ln1_b': nrm(ks[17], (L, D), 0.02),
        'w_ffn_in': nrm(ks[18], (L, D, 2 * D_FF), BETA * D ** -0.5),
        'w_ffn_out': nrm(ks[19], (L, D_FF, D), BETA * D_FF ** -0.5),
        'ln2_g': 1.0 + nrm(ks[20], (L, D), 0.02),
        'ln2_b': nrm(ks[21], (L, D), 0.02),
        'w_ple_in': nrm(ks[22], (L, PLE_DIM, D), BETA * PLE_DIM ** -0.5),
        'w_ple_gate': nrm(ks[23], (L, D, D), D ** -0.5),
    }


def reference(x, p, w_in, fox_f_bias, nsa_pe_k, nsa_pe_v, nsa_cmp_k1, nsa_cmp_k2, nsa_cmp_v1, nsa_cmp_v2,
              dsa_kv_norm, dsa_kv_up, w_br_fox, w_br_nsa, w_br_dsa, w_out, ln1_g, ln1_b,
              w_ffn_in, w_ffn_out, ln2_g, ln2_b, w_ple_in, w_ple_gate):
    B, S, _ = x.shape
    cos, sin = rope_tables(S, HEAD_DIM)
    cos_i, sin_i = rope_tables(S, IDX_DIM)
    h = x
    for i in range(DEPTH):
        z = split_cols(h @ w_in[i])
        o_fox = fox_attention(heads(z['fox_q'], FOX_HEADS), heads(z['fox_k'], FOX_HEADS),
                              heads(z['fox_v'], FOX_HEADS), z['fox_f'], fox_f_bias[i])
        o_nsa = nsa_attention(apply_rope(heads(z['nsa_q'], NSA_HEADS), cos, sin),
                              z['nsa_kc'], z['nsa_vc'], rope1(z['nsa_ks'], cos, sin), z['nsa_vs'],
                              rope1(z['nsa_kw'], cos, sin), z['nsa_vw'], z['nsa_g'],
                              nsa_pe_k[i], nsa_pe_v[i], nsa_cmp_k1[i], nsa_cmp_k2[i], nsa_cmp_v1[i], nsa_cmp_v2[i],
                              cos, sin)
        k_d, v_d = jnp.split(rms_norm(z['dsa_ckv'], dsa_kv_norm[i]) @ dsa_kv_up[i], 2, axis=-1)
        o_dsa = dsa_attention(apply_rope(heads(z['dsa_q'], DSA_HEADS), cos, sin),
                              rope1(k_d, cos, sin), v_d,
                              apply_rope(heads(z['idx_q'], IDX_HEADS), cos_i, sin_i),
                              rope1(z['idx_k'], cos_i, sin_i), z['idx_w'])
        g_fox, g_nsa, g_dsa = jnp.split(jax.nn.sigmoid(z['gate']), 3, axis=-1)
        mixed = (g_fox * (o_fox.reshape(B, S, -1) @ w_br_fox[i])
                 + g_nsa * (o_nsa.reshape(B, S, -1) @ w_br_nsa[i])
                 + g_dsa * (o_dsa.reshape(B, S, -1) @ w_br_dsa[i]))
        h = layer_norm(ALPHA * h + mixed @ w_out[i], ln1_g[i], ln1_b[i])
        a, b = jnp.split(h @ w_ffn_in[i], 2, axis=-1)
        h = layer_norm(ALPHA * h + (jax.nn.silu(a) * b) @ w_ffn_out[i], ln2_g[i], ln2_b[i])
        h = h + (p[i] @ w_ple_in[i]) * jax.nn.sigmoid(h @ w_ple_gate[i])
    return h
```

```python
import numpy as np
import ml_dtypes
from contextlib import ExitStack
import concourse.bass as bass
import concourse.mybir as mybir
from concourse.bass_utils import run_bass_kernel_spmd

F32 = mybir.dt.float32
BF16 = mybir.dt.bfloat16
AF = mybir.ActivationFunctionType
ALU = mybir.AluOpType

D = 2048
S = 2048
NT = 16
DEPTH = 4
DFF = 5632
INC = 11874
PLE = 256
ALPHA = float((2 * DEPTH) ** 0.25)
SC = float(128 ** -0.5)
NEG = -30000.0
OFF = dict(fox_q=0, fox_k=768, fox_v=1536, fox_f=2304, nsa_q=2310, nsa_kc=2822, nsa_vc=2950, nsa_ks=3078,
           nsa_vs=3206, nsa_kw=3334, nsa_vw=3462, nsa_g=3590, dsa_q=3602, dsa_ckv=4370, idx_q=4626,
           idx_k=5650, idx_w=5714, gate=5730)
WSHAPES = dict(w_in=(2048, INC), nsa_cmp_k1=(4096, 256), nsa_cmp_k2=(256, 128), nsa_cmp_v1=(4096, 256),
               nsa_cmp_v2=(256, 128), dsa_kv_up=(256, 256), w_br_fox=(768, 2048), w_br_nsa=(512, 2048),
               w_br_dsa=(768, 2048), w_out=(2048, 2048), w_ffn_in=(2048, 2 * DFF), w_ffn_out=(DFF, 2048),
               w_ple_in=(256, 2048), w_ple_gate=(2048, 2048))
WORDER = ["w_in", "nsa_cmp_k1", "nsa_cmp_k2", "nsa_cmp_v1", "nsa_cmp_v2", "dsa_kv_up", "w_br_fox", "w_br_nsa",
          "w_br_dsa", "w_out", "w_ffn_in", "w_ffn_out", "w_ple_in", "w_ple_gate"]
SMALL = dict(fox_f_bias=(6,), nsa_pe_k=(32, 128), nsa_pe_v=(32, 128), dsa_kv_norm=(256,), ln1_g=(2048,),
             ln1_b=(2048,), ln2_g=(2048,), ln2_b=(2048,))
ENGS = ("pe", "act", "dve", "pool", "sp")


def _lineno():
    import sys
    f = sys._getframe(2)
    out = []
    for _ in range(4):
        if f is None:
            break
        out.append(f.f_lineno)
        f = f.f_back
    return out


class Op:
    __slots__ = ("eng", "fn", "deps", "signal", "tick", "is_dma", "dsem", "dcount", "dprev", "hard", "line")

    def __init__(self, eng, fn, is_dma=False):
        self.eng = eng
        self.fn = fn
        self.deps = set()
        self.signal = False
        self.tick = 0
        self.is_dma = is_dma
        self.dsem = None
        self.dcount = 0
        self.dprev = 0
        self.hard = False


class Res:
    __slots__ = ("last_w", "readers")

    def __init__(self):
        self.last_w = None
        self.readers = []


class Prog:
    N_DMA_SEMS = 14

    def __init__(self):
        self.ops = []
        self.res = {}
        self.last = {e: None for e in ENGS}
        self.dmas_since_bar = []

    def _r(self, key):
        r = self.res.get(key)
        if r is None:
            r = self.res[key] = Res()
        return r

    def add(self, eng, fn, reads=(), writes=(), is_dma=False, nobar=False):
        op = Op(eng, fn, is_dma)
        op.line = _lineno()
        for k in reads:
            r = self._r(k)
            if r.last_w is not None:
                op.deps.add(r.last_w)
        for k in writes:
            r = self._r(k)
            if r.last_w is not None:
                op.deps.add(r.last_w)
            op.deps.update(r.readers)
        for k in reads:
            self._r(k).readers.append(op)
        for k in writes:
            r = self._r(k)
            r.last_w = op
            r.readers = []
        op.deps.discard(op)
        self.ops.append(op)
        self.last[eng] = op
        if is_dma and not nobar:
            self.dmas_since_bar.append(op)
        return op

    def dma(self, q, out, in_, reads=(), writes=(), nobar=False):
        return self.add(q, lambda e: e.dma_start(out=out, in_=in_), reads, writes, is_dma=True, nobar=nobar)

    def barrier(self):
        lasts = [op for op in self.last.values() if op is not None and not op.is_dma]
        dmas = list(self.dmas_since_bar)
        self.dmas_since_bar = []
        for e in ("pe", "act", "dve", "sp"):
            op = Op(e, lambda eng: eng.nop())
            op.hard = True
            op.deps.update(lasts)
            op.deps.update(dmas)
            self.ops.append(op)
            self.last[e] = op
        self.res = {k: v for k, v in self.res.items() if isinstance(k, tuple) and k and k[0] in ("ps", "dram", "wb")}

    def emit(self, nc, ctx, final_wait_ops=()):
        streams = {e: [] for e in ENGS}
        for op in self.ops:
            streams[op.eng].append(op)
        for op in self.ops:
            for d in op.deps:
                if d.is_dma:
                    continue
                if d.eng == op.eng and d.eng == "pe" and not op.hard:
                    continue
                d.signal = True
        for e in ENGS:
            t = 0
            for op in streams[e]:
                if not op.is_dma and op.signal:
                    t += 1
                    op.tick = t
        esem = {e: ctx.enter_context(nc.semaphore("s_" + e)) for e in ENGS}
        dsems = {}
        for q in ENGS:
            if any(op.is_dma for op in streams[q]):
                dsems[q] = [ctx.enter_context(nc.semaphore("d_%s_%d" % (q, i))) for i in range(self.N_DMA_SEMS)]
        for q, sl in dsems.items():
            cnt = [0] * len(sl)
            i = 0
            for op in streams[q]:
                if op.is_dma:
                    j = i % len(sl)
                    op.dsem = (q, j)
                    op.dprev = cnt[j]
                    cnt[j] += 16
                    op.dcount = cnt[j]
                    i += 1
        block = ctx.enter_context(nc.Block())

        def run_stream(ename, eng):
            known = {}
            for op in streams[ename]:
                waits = {}
                for d in op.deps:
                    if d.is_dma:
                        key = ("d",) + d.dsem
                        val = d.dcount
                    else:
                        if d.eng == ename and ename == "pe" and not op.hard:
                            continue
                        key = ("e", d.eng)
                        val = d.tick
                    if waits.get(key, 0) < val:
                        waits[key] = val
                if op.is_dma and op.dprev > 0:
                    key = ("d",) + op.dsem
                    if waits.get(key, 0) < op.dprev:
                        waits[key] = op.dprev
                for key, val in waits.items():
                    if known.get(key, 0) >= val:
                        continue
                    known[key] = val
                    sem = esem[key[1]] if key[0] == "e" else dsems[key[1]][key[2]]
                    eng.wait_ge(sem, val)
                try:
                    ins = op.fn(eng)
                except Exception:
                    print("EMIT FAILED for op recorded at lines", getattr(op, "line", None), flush=True)
                    raise
                if op.is_dma:
                    ins.then_inc(dsems[op.dsem[0]][op.dsem[1]], 16)
                elif op.signal:
                    ins.then_inc(esem[ename], 1)
            if ename == "sp":
                for op in final_wait_ops:
                    eng.wait_ge(dsems[op.dsem[0]][op.dsem[1]], op.dcount)

        @block.tensor
        def _(e):
            run_stream("pe", e)

        @block.scalar
        def _(e):
            run_stream("act", e)

        @block.vector
        def _(e):
            run_stream("dve", e)

        @block.gpsimd
        def _(e):
            run_stream("pool", e)

        @block.sync
        def _(e):
            run_stream("sp", e)


class Arena:
    def __init__(self, t, nwords):
        self.t = t
        self.nwords = nwords
        self.off = 0
        self.n = 0

    def mark(self):
        return self.off

    def release(self, m):
        self.off = m

    def _alloc(self, nbytes):
        nb = (nbytes + 63) // 64 * 64
        o = self.off
        self.off += nb
        assert self.off <= self.nwords * 4, ("arena overflow", self.off, self.nwords * 4)
        return o

    def f32(self, shape):
        n = int(np.prod(shape[1:]))
        o = self._alloc(n * 4)
        v = self.t[:, o // 4: o // 4 + n]
        if len(shape) == 3:
            v = v.rearrange("p (a b) -> p a b", b=shape[2])
        elif len(shape) == 4:
            v = v.rearrange("p (a b c) -> p a b c", b=shape[2], c=shape[3])
        return v

    def bf(self, shape):
        n = int(np.prod(shape[1:]))
        n2 = (n + 1) // 2
        o = self._alloc(n2 * 4)
        v = self.t[:, o // 4: o // 4 + n2].bitcast(BF16)[:, 0:n]
        if len(shape) == 3:
            v = v.rearrange("p (a b) -> p a b", b=shape[2])
        elif len(shape) == 4:
            v = v.rearrange("p (a b c) -> p a b c", b=shape[2], c=shape[3])
        return v


def _rope_np(n, dim):
    inv = (1.0 / (np.float32(10000.0) ** (np.arange(0, dim, 2, dtype=np.float32) / np.float32(dim)))).astype(np.float32)
    ang = np.arange(n, dtype=np.float32)[:, None] * inv[None, :]
    return np.cos(ang).astype(np.float32), np.sin(ang).astype(np.float32)


def make_consts():
    c = {}
    cos, sin = _rope_np(S, 128)
    c["ropeC"] = np.ascontiguousarray(np.concatenate([cos, cos], 1).T)
    c["ropeS"] = np.ascontiguousarray(np.concatenate([-sin, sin], 1).T)
    ci, si = _rope_np(S, 64)
    c["ropeCi"] = np.ascontiguousarray(np.concatenate([ci, ci], 1).T)
    c["ropeSi"] = np.ascontiguousarray(np.concatenate([-si, si], 1).T)
    sp = np.arange(128)[:, None]
    tp = np.arange(128)[None, :]
    c["maskC"] = np.where(sp <= tp, 0.0, NEG).astype(ml_dtypes.bfloat16)
    c["maskW"] = np.where(sp > tp, 0.0, NEG).astype(ml_dtypes.bfloat16)
    cc = np.arange(127)[:, None]
    tt = np.arange(S)[None, :]
    cm = np.zeros((128, S), np.float32)
    cm[:127] = np.where(16 * cc + 31 <= tt, 0.0, NEG)
    c["maskCmp"] = cm.astype(ml_dtypes.bfloat16)
    c_start = np.arange(127) * 16
    c_end = c_start + 31
    s_start = np.arange(32) * 64
    ov = np.maximum(np.minimum(c_end[:, None], s_start[None, :] + 63) - np.maximum(c_start[:, None], s_start[None, :]) + 1, 0) / 32.0
    oe = np.zeros((128, 33), np.float32)
    oe[:127, 0] = 1.0
    oe[:127, 1:] = ov
    c["ovl"] = oe.astype(ml_dtypes.bfloat16)
    pos = np.arange(S)
    blk = np.arange(32)[None, :]
    cur = (pos // 64)[:, None]
    forced = (blk == 0) | (blk == cur) | (blk == cur - 1)
    vis = s_start[None, :] <= pos[:, None]
    mul = (vis & ~forced).astype(np.float32)
    add = np.where(forced, 1e4, np.where(vis, 0.0, -1e4)).astype(np.float32)
    c["selMul"] = np.ascontiguousarray(mul.reshape(16, 128, 32).transpose(1, 0, 2))
    c["selAdd"] = np.ascontiguousarray(add.reshape(16, 128, 32).transpose(1, 0, 2))
    ex = np.zeros((32, 16, 128), np.float32)
    for kt in range(16):
        for s_ in range(128):
            ex[2 * kt + s_ // 64, kt, s_] = 1.0
    c["expand"] = ex.astype(ml_dtypes.bfloat16)
    c["negC"] = np.where(tp.T >= sp.T, 0.0, -1e30).astype(np.float32)
    c["negC"] = np.where(np.arange(128)[None, :] <= np.arange(128)[:, None], 0.0, -1e30).astype(np.float32)
    return c


CONST_SHAPES = dict(ropeC=([128, S], F32), ropeS=([128, S], F32), ropeCi=([64, S], F32), ropeSi=([64, S], F32),
                    maskC=([128, 128], BF16), maskW=([128, 128], BF16), maskCmp=([128, S], BF16),
                    ovl=([128, 33], BF16), selMul=([128, 16, 32], F32), selAdd=([128, 16, 32], F32),
                    expand=([32, 16, 128], BF16), negC=([128, 128], F32))


def build(depth=DEPTH, dbg=()):
    nc = bass.Bass("TRN2", target_bir_lowering=False)
    ctx = ExitStack()
    P = Prog()
    dt_in = {}
    x_d = nc.dram_tensor("x", [S, D], F32, kind="ExternalInput").ap()
    p_d = nc.dram_tensor("p", [DEPTH, S, PLE], F32, kind="ExternalInput").ap()
    wsrc = {n: nc.dram_tensor(n, [DEPTH] + list(s), F32, kind="ExternalInput").ap() for n, s in WSHAPES.items()}
    small = {n: nc.dram_tensor(n, [DEPTH] + list(s), F32, kind="ExternalInput").ap() for n, s in SMALL.items()}
    cst = {n: nc.dram_tensor("c_" + n, s, d, kind="ExternalInput").ap() for n, (s, d) in CONST_SHAPES.items()}
    y_d = nc.dram_tensor("y", [S, D], F32, kind="ExternalOutput").ap()
    dbg_d = {}
    for name, shape, dty in dbg:
        dbg_d[name] = nc.dram_tensor("dbg_" + name, list(shape), dty, kind="ExternalOutput").ap()
    wb = {n: nc.dram_tensor("wb_" + n, [depth] + list(s), BF16, kind="Internal").ap() for n, s in WSHAPES.items()}
    h_dram = nc.dram_tensor("h_dram", [S, D], F32, kind="Internal").ap()
    r_dram = nc.dram_tensor("r_dram", [S, D], F32, kind="Internal").ap()
    oT_dram = nc.dram_tensor("oT_dram", [D, S], BF16, kind="Internal").ap()
    mixT_dram = nc.dram_tensor("mixT_dram", [D, S], BF16, kind="Internal").ap()

    ARENA_WORDS = 32 * 1024
    hT = ctx.enter_context(nc.sbuf_tensor("hT", [128, 16, S], BF16))
    arena_t = ctx.enter_context(nc.sbuf_tensor("arena", [128, ARENA_WORDS], F32))
    A = Arena(arena_t, ARENA_WORDS)
    ident = ctx.enter_context(nc.sbuf_tensor("ident", [128, 128], BF16))
    identf = ctx.enter_context(nc.sbuf_tensor("identf", [128, 128], F32))
    maskC = ctx.enter_context(nc.sbuf_tensor("maskC", [128, 128], BF16))
    maskW = ctx.enter_context(nc.sbuf_tensor("maskW", [128, 128], BF16))
    maskCmp = ctx.enter_context(nc.sbuf_tensor("maskCmp", [128, S], BF16))
    ovl = ctx.enter_context(nc.sbuf_tensor("ovl", [128, 33], BF16))
    selMul = ctx.enter_context(nc.sbuf_tensor("selMul", [128, 16, 32], F32))
    selAdd = ctx.enter_context(nc.sbuf_tensor("selAdd", [128, 16, 32], F32))
    expand = ctx.enter_context(nc.sbuf_tensor("expand", [32, 16, 128], BF16))
    negC = ctx.enter_context(nc.sbuf_tensor("negC", [128, 128], F32))
    onesb = ctx.enter_context(nc.sbuf_tensor("onesb", [128, 16], BF16))
    ps = [ctx.enter_context(nc.psum_tensor("ps%d" % i, [128, 512], F32)) for i in range(8)]

    def PS(i):
        return ("ps", i)

    cnt = {"ev": 0, "ps": 0, "w": 0}

    def mm(out, lhsT, rhs, start, stop, reads, pskey):
        P.add("pe", lambda e: e.matmul(out, lhsT, rhs, start=start, stop=stop), reads=reads, writes=[pskey])

    def tr(out, in_, idn, reads, pskey):
        P.add("pe", lambda e: e.transpose(out=out, in_=in_, identity=idn), reads=reads, writes=[pskey])

    def act(out, in_, func, reads, writes, bias=0.0, scale=1.0, accum=None):
        if accum is None:
            P.add("act", lambda e: e.activation(out=out, in_=in_, func=func, bias=bias, scale=scale), reads=reads, writes=writes)
        else:
            P.add("act", lambda e: e.activation(out=out, in_=in_, func=func, bias=bias, scale=scale, accum_out=accum), reads=reads, writes=writes)

    def tt(out, in0, in1, op, reads, writes):
        P.add("dve", lambda e: e.tensor_tensor(out=out, in0=in0, in1=in1, op=op), reads=reads, writes=writes)

    def ts(out, in0, s1, s2, op0, op1, reads, writes):
        if s2 is None:
            P.add("dve", lambda e: e.tensor_scalar(out=out, in0=in0, scalar1=s1, scalar2=None, op0=op0), reads=reads, writes=writes)
        else:
            P.add("dve", lambda e: e.tensor_scalar(out=out, in0=in0, scalar1=s1, scalar2=s2, op0=op0, op1=op1), reads=reads, writes=writes)

    def stt(out, in0, scalar, in1, op0, op1, reads, writes):
        P.add("dve", lambda e: e.scalar_tensor_tensor(out=out, in0=in0, scalar=scalar, in1=in1, op0=op0, op1=op1), reads=reads, writes=writes)

    def vcopy(out, in_, reads, writes):
        P.add("dve", lambda e: e.tensor_copy(out=out, in_=in_), reads=reads, writes=writes)

    def acopy(out, in_, reads, writes):
        act(out, in_, AF.Copy, reads, writes)

    def evac(out, in_, reads, writes):
        cnt["ev"] += 1
        if cnt["ev"] % 2:
            acopy(out, in_, reads, writes)
        else:
            vcopy(out, in_, reads, writes)

    def memset(ap, val, writes, eng="dve"):
        P.add(eng, lambda e: e.memset(ap, val), writes=writes)

    def nextps(lo=0, n=4):
        cnt["ps"] += 1
        return lo + cnt["ps"] % n

    def hkeys(t0, n):
        return [("hT", t) for t in range(t0, t0 + n)]

    def wkeys(name, l):
        return [("wb", name, l, c) for c in range(nchunks[name])]

    for t_, n_ in ((maskC, "maskC"), (maskW, "maskW"), (maskCmp, "maskCmp"), (ovl, "ovl"), (selMul, "selMul"),
                   (selAdd, "selAdd"), (expand, "expand"), (negC, "negC")):
        P.dma("sp", t_[:], cst[n_], writes=[n_])
    memset(ident[:], 1.0, ["ident"], eng="pool")
    P.add("pool", lambda e: e.affine_select(out=ident[:], in_=ident[:], pattern=[[-1, 128]], compare_op=ALU.is_equal,
                                            fill=0.0, base=0, channel_multiplier=1), reads=["ident"], writes=["ident"])
    memset(identf[:], 1.0, ["identf"], eng="pool")
    P.add("pool", lambda e: e.affine_select(out=identf[:], in_=identf[:], pattern=[[-1, 128]], compare_op=ALU.is_equal,
                                            fill=0.0, base=0, channel_multiplier=1), reads=["identf"], writes=["identf"])
    memset(onesb[:], 1.0, ["onesb"], eng="pool")

    nchunks = {}
    for n, (R_, C_) in WSHAPES.items():
        rows = max(1, min(R_, (1 << 20) // C_))
        rows = min(rows, 128) if R_ % 128 == 0 else rows
        nchunks[n] = (R_ + rows - 1) // rows
        nchunks[n + "_rows"] = rows

    def cast_layer(l):
        for n in WORDER:
            R_, C_ = WSHAPES[n]
            rows = nchunks[n + "_rows"]
            for c in range(nchunks[n]):
                r0 = c * rows
                r1 = min(R_, r0 + rows)
                P.dma("pool", wb[n][l, r0:r1, :], wsrc[n][l, r0:r1, :], writes=[("wb", n, l, c)], nobar=True)

    wslots = {}

    def new_wslots(nbytes, n=2):
        wslots["v"] = [A.bf([128, nbytes // 2]) for _ in range(n)]
        wslots["i"] = 0

    def wslot():
        i = wslots["i"] % len(wslots["v"])
        wslots["i"] += 1
        return wslots["v"][i], ("wslot", i)

    def load_w_cols(src2d, c0, ncols, nk, keys, dst=None, dkey=None, coff=0):
        if dst is None:
            buf, dkey = wslot()
            dst = buf[:, 0:nk * ncols].rearrange("p (k c) -> p k c", c=ncols)
            P.dma("sp", dst, src2d[:, c0:c0 + ncols].rearrange("(k p) c -> p k c", p=128), reads=keys, writes=[dkey])
        else:
            P.dma("sp", dst[:, :, coff:coff + ncols], src2d[:, c0:c0 + ncols].rearrange("(k p) c -> p k c", p=128),
                  reads=keys, writes=[dkey])
        return dst, dkey

    def rope128(psap, pk, dst, dkey, tb, tabs, n=512, pos=None):
        C_, S_, tk = tabs
        if pos is None:
            cs = C_[:, tb * 512: tb * 512 + n]
            ss_lo = S_[0:64, tb * 512: tb * 512 + n]
            ss_hi = S_[64:128, tb * 512: tb * 512 + n]
        else:
            cs, ss_lo, ss_hi = pos
        ta = ropetmp["a"][cnt["ev"] % 2]
        tbm = ropetmp["b"][cnt["ev"] % 2]
        ka = ("ropeA", cnt["ev"] % 2)
        kb = ("ropeB", cnt["ev"] % 2)
        cnt["ev"] += 1
        tt(ta[:, 0:n], psap, cs, ALU.mult, [pk, tk], [ka])
        tt(tbm[0:64, 0:n], psap[64:128, :], ss_lo, ALU.mult, [pk, tk], [kb])
        tt(tbm[64:128, 0:n], psap[0:64, :], ss_hi, ALU.mult, [pk, tk], [kb])
        tt(dst, ta[:, 0:n], tbm[:, 0:n], ALU.add, [ka, kb], [dkey])

    def rope64(psap, pk, dsts, dkey, tb, tabs, n=512):
        C_, S_, tk = tabs
        ta = ropetmp["a"][cnt["ev"] % 2]
        tbm = ropetmp["b"][cnt["ev"] % 2]
        ka = ("ropeA", cnt["ev"] % 2)
        kb = ("ropeB", cnt["ev"] % 2)
        cnt["ev"] += 1
        sl = slice(tb * 512, tb * 512 + n)
        tt(ta[0:64, 0:n], psap, C_[0:64, sl], ALU.mult, [pk, tk], [ka])
        tt(tbm[0:32, 0:n], psap[32:64, :], S_[0:32, sl], ALU.mult, [pk, tk], [kb])
        tt(tbm[32:64, 0:n], psap[0:32, :], S_[32:64, sl], ALU.mult, [pk, tk], [kb])
        for dst in dsts:
            tt(dst, ta[0:64, 0:n], tbm[0:64, 0:n], ALU.add, [ka, kb], [dkey])

    ropetmp = {}

    def alloc_ropetmp():
        ropetmp["a"] = [A.f32([128, 512]) for _ in range(2)]
        ropetmp["b"] = [A.f32([128, 512]) for _ in range(2)]

    def proj_F(wt, wk, c0, M, epi, actT=None, akey="hT", nk=16, base_ps=0):
        src = hT if actT is None else actT
        for tb in range(4):
            b = nextps()
            for kc in range(nk):
                mm(ps[b][0:M, :], wt[:, kc, c0:c0 + M], src[:, kc, tb * 512:(tb + 1) * 512], kc == 0, kc == nk - 1,
                   [wk] + (hkeys(tb * 4, 4) if akey == "hT" else [akey]), PS(b))
            epi(ps[b][0:M, :], PS(b), tb)

    def proj_T(wt, wk, c0, N, epi, actT=None, akey="hT", nk=16, group=None):
        src = hT if actT is None else actT
        if group is None:
            group = max(1, 512 // N)
        for t0 in range(0, NT, group):
            b = nextps()
            g = min(group, NT - t0)
            for j in range(g):
                tt_ = t0 + j
                for kc in range(nk):
                    mm(ps[b][:, j * N:(j + 1) * N], src[:, kc, tt_ * 128:(tt_ + 1) * 128], wt[:, kc, c0:c0 + N],
                       kc == 0, kc == nk - 1, [wk] + (hkeys(tt_, 1) if akey == "hT" else [akey]), PS(b))
            epi(ps[b][:, 0:g * N], PS(b), t0, g)

    def attn_qtile(heads, qT_of, ktiles, ncol, pT, skey_banks, obanks, exp_bias=None):
        G = len(heads)
        ob = obanks
        nkt = len(ktiles)
        for ki, kt in enumerate(ktiles):
            sb = skey_banks[ki % 2]
            ns = kt["ns"]
            kT, kkey = kt["kT"]
            for hi, h in enumerate(heads):
                q, qk = qT_of(h)
                nm = len(kt["masks"])
                mm(ps[sb][0:ns, hi * 128:(hi + 1) * 128], kT, q, hi == 0, nm == 0 and hi == G - 1, [kkey, qk], PS(sb))
                for mi, (ml, mr, mk) in enumerate(kt["masks"]):
                    mm(ps[sb][0:ns, hi * 128:(hi + 1) * 128], ml, mr, False, mi == nm - 1 and hi == G - 1, mk, PS(sb))
            pt, ptk = pT[ki % 2]
            if exp_bias is None:
                act(pt[0:ns, 0:G * 128], ps[sb][0:ns, 0:G * 128], AF.Exp, [PS(sb)], [ptk], scale=SC)
            else:
                for hi, h in enumerate(heads):
                    bap, bk = exp_bias(h, kt)
                    act(pt[0:ns, hi * 128:(hi + 1) * 128], ps[sb][0:ns, hi * 128:(hi + 1) * 128], AF.Exp,
                        [PS(sb), bk], [ptk], scale=SC, bias=bap)
            V, vk = kt["V"]
            for hi, h in enumerate(heads):
                mm(ps[ob][:, hi * ncol:(hi + 1) * ncol], pt[0:ns, hi * 128:(hi + 1) * 128], V, ki == 0 and hi == 0,
                   ki == nkt - 1 and hi == G - 1, [ptk] + list(vk), PS(ob))
        return [(ps[ob][:, hi * ncol:(hi + 1) * ncol], PS(ob)) for hi in range(G)]

    def transposes_to(dstT, dkey, src_tok, skey, nblk, tbank):
        pb = ps[tbank][:].bitcast(BF16)
        for j in range(nblk):
            tr(pb[:, j * 128:(j + 1) * 128], src_tok[:, j * 128:(j + 1) * 128], ident[:], [skey, "ident"], PS(tbank))
        for j in range(nblk):
            evac(dstT(j), pb[:, j * 128:(j + 1) * 128], [PS(tbank)], [dkey])

    def load_tables128():
        C_ = A.f32([128, S])
        S_ = A.f32([128, S])
        P.dma("sp", C_, cst["ropeC"], writes=["tabC"])
        P.dma("sp", S_, cst["ropeS"], writes=["tabC"])
        return (C_, S_, "tabC")

    def load_tables64():
        C_ = A.f32([128, S])
        S_ = A.f32([128, S])
        P.dma("sp", C_[0:64, :], cst["ropeCi"], writes=["tabI"])
        P.dma("sp", S_[0:64, :], cst["ropeSi"], writes=["tabI"])
        return (C_, S_, "tabI")

    def phase_fox(l):
        A.release(0)
        win = wb["w_in"][l]
        wk_in = wkeys("w_in", l)
        new_wslots(16 * 384 * 2)
        wf = A.bf([128, 16, 8])
        lf = A.f32([128, S])
        cp = A.f32([128, S])
        onesr = A.f32([128, S])
        negb = A.f32([128, 2])
        ctok = A.f32([128, 96])
        bsh = A.f32([128, 96])
        rbd = A.f32([128, 96])
        bias = A.f32([128, 6, 16, 16])
        qTh = A.bf([128, S])
        kTh = A.bf([128, S])
        vh = A.bf([128, 16, 130])
        oTh = [A.bf([128, S]) for _ in range(2)]
        pts = [(A.bf([128, 128]), ("pT", i)) for i in range(2)]
        otok = [A.bf([128, 128]) for _ in range(2)]
        rec = A.f32([128, 2])
        memset(onesr[0:8, :], 1.0, ["onesr"])
        memset(vh[:, :, 128:129], 1.0, ["vh1"])
        P.dma("sp", wf[:, :, 0:6], win[:, OFF["fox_f"]:OFF["fox_f"] + 6].rearrange("(k p) c -> p k c", p=128),
              reads=wk_in, writes=["wf"])
        P.dma("sp", negb[0:6, 0:1], small["fox_f_bias"][l:l + 1, :].rearrange("o h -> h o"), writes=["negb"])
        ts(negb[0:6, 1:2], negb[0:6, 0:1], -1.0, None, ALU.mult, None, ["negb"], ["negb2"])
        for tb in range(4):
            b = nextps()
            for kc in range(16):
                mm(ps[b][0:6, :], wf[:, kc, 0:6], hT[:, kc, tb * 512:(tb + 1) * 512], kc == 0, kc == 15, ["wf"] + hkeys(tb * 4, 4), PS(b))
            sl = slice(tb * 512, (tb + 1) * 512)
            act(lf[0:6, sl], ps[b][0:6, :], AF.Exp, [PS(b), "negb2"], [("lf", tb)], bias=negb[0:6, 1:2], scale=-1.0)
            act(lf[0:6, sl], lf[0:6, sl], AF.Ln, [("lf", tb)], [("lf", tb)], bias=1.0, scale=1.0)
        P.add("dve", lambda e: e.tensor_tensor_scan(out=cp[0:6, :], data0=onesr[0:6, :], data1=lf[0:6, :], initial=0.0,
                                                    op0=ALU.mult, op1=ALU.add),
              reads=[("lf", t) for t in range(4)] + ["onesr"], writes=["cp"])
        b = nextps()
        for kt in range(16):
            tr(ps[b][:, kt * 6:(kt + 1) * 6], cp[0:6, kt * 128:(kt + 1) * 128], identf[0:6, 0:6], ["cp", "identf"], PS(b))
        vcopy(ctok[:, :], ps[b][:, 0:96], [PS(b)], ["ctok"])
        for h in range(6):
            ts(rbd[0:6, h * 16:(h + 1) * 16], cp[0:6, 127:S:128], identf[0:6, h:h + 1], None, ALU.mult, None,
               ["cp", "identf"], ["rbd"])
        b = nextps()
        mm(ps[b][:, 0:96], onesr[0:6, 0:128], rbd[0:6, 0:96], True, True, ["onesr", "rbd"], PS(b))
        vcopy(bsh[:, :], ps[b][:, 0:96], [PS(b)], ["bsh"])
        ctv = ctok[:, :].rearrange("p (k h) -> p h k", h=6)
        for h in range(6):
            for qt in range(16):
                ts(bias[:, h, qt, :], ctv[:, h, :], bsh[:, h * 16 + qt:h * 16 + qt + 1], None, ALU.subtract, None,
                   ["ctok", "bsh"], [("bias", h)])
        for h in range(6):
            wt, wk = wslot()
            wt = wt[:, 0:16 * 384].rearrange("p (k c) -> p k c", c=384)
            for j, seg in enumerate(("fox_q", "fox_k", "fox_v")):
                c0 = OFF[seg] + h * 128
                P.dma("sp", wt[:, :, j * 128:(j + 1) * 128], win[:, c0:c0 + 128].rearrange("(k p) c -> p k c", p=128),
                      reads=wk_in, writes=[wk])
            proj_F(wt, wk, 0, 128, lambda pa, pk, tb: evac(qTh[:, tb * 512:(tb + 1) * 512], pa, [pk], ["qTh"]))
            proj_F(wt, wk, 128, 128, lambda pa, pk, tb: evac(kTh[:, tb * 512:(tb + 1) * 512], pa, [pk], ["kTh"]))
            proj_T(wt, wk, 256, 128, lambda pa, pk, t0, g: evac(vh[:, t0:t0 + g, 0:128], pa.rearrange("p (g c) -> p g c", c=128), [pk], ["vh"]))
            oT = oTh[h % 2]
            ok_ = ("oTh", h % 2)
            for qt in range(16):
                kts = []
                for kt in range(qt + 1):
                    masks = []
                    if kt == qt:
                        masks.append((ident[:], maskC[:], ["ident", "maskC"]))
                    kts.append(dict(kT=(kTh[:, kt * 128:(kt + 1) * 128], "kTh"), ns=128,
                                    V=(vh[:, kt, 0:129], ["vh", "vh1"]), masks=masks, kt=kt))
                res = attn_qtile([h], lambda hh: (qTh[:, qt * 128:(qt + 1) * 128], "qTh"), kts, 129, pts, (4, 5),
                                 2 + qt % 2, exp_bias=lambda hh, k, qt=qt: (bias[:, hh, qt, k["kt"]:k["kt"] + 1], ("bias", hh)))
                oap, opk = res[0]
                ri = qt % 2
                P.add("dve", lambda e, ri=ri, oap=oap: e.reciprocal(out=rec[:, ri:ri + 1], in_=oap[:, 128:129]),
                      reads=[opk, "vh1"], writes=[("rec", ri)])
                ts(otok[ri][:, :], oap[:, 0:128], rec[:, ri:ri + 1], None, ALU.mult, None, [opk, ("rec", ri)], [("otok", ri)])
                transposes_to(lambda j, qt=qt, oT=oT: oT[:, qt * 128:(qt + 1) * 128], ok_, otok[ri], ("otok", ri), 1, 6 + qt % 2)
            P.dma("sp", oT_dram[h * 128:(h + 1) * 128, :], oT[:, :], reads=[ok_], writes=[("dram", "oT", h)])

    def phase_nsa(l):
        P.barrier()
        A.release(0)
        win = wb["w_in"][l]
        wk_in = wkeys("w_in", l)
        qTn = A.bf([128, 4, S])
        ksT = A.bf([128, S])
        kwT = A.bf([128, S])
        vs = A.bf([128, 16, 130])
        vw = A.bf([128, 16, 130])
        kcmpT = A.bf([128, 128])
        vcx = A.bf([128, 162])
        gsig = A.f32([128, 16, 12])
        oTn = A.bf([128, 4, S])
        m_stage = A.mark()
        tabs = load_tables128()
        alloc_ropetmp()
        new_wslots(16 * 512 * 2)
        rawT = [A.bf([128, S]) for _ in range(2)]
        peT = A.bf([128, 2, 32])
        pe32 = A.f32([32, 2, 128])
        w2 = A.bf([128, 2, 2, 128])
        h1x = A.f32([128, 256])
        h1a = A.f32([128, 256])
        gT = A.bf([128, 2, 2, 128])
        memset(vs[:, :, 128:129], 1.0, ["vs1"])
        memset(vw[:, :, 128:129], 1.0, ["vw1"])
        vcopy(vcx[:, 128:161], ovl[:, :], ["ovl"], ["vcx1"])
        wt, wk = load_w_cols(win, OFF["nsa_g"], 12, 16, wk_in)
        proj_T(wt, wk, 0, 12, lambda pa, pk, t0, g: act(gsig[:, t0:t0 + g, :], pa.rearrange("p (g c) -> p g c", c=12), AF.Sigmoid, [pk], ["gsig"]), group=16)
        wt, wk = load_w_cols(win, OFF["nsa_q"], 512, 16, wk_in)
        for h in range(4):
            proj_F(wt, wk, h * 128, 128, lambda pa, pk, tb, h=h: rope128(pa, pk, qTn[:, h, tb * 512:(tb + 1) * 512], "qTn", tb, tabs))
        wt, wk = load_w_cols(win, OFF["nsa_kc"], 512, 16, wk_in)
        proj_F(wt, wk, 0, 128, lambda pa, pk, tb: evac(rawT[0][:, tb * 512:(tb + 1) * 512], pa, [pk], [("rawT", 0)]))
        proj_F(wt, wk, 128, 128, lambda pa, pk, tb: evac(rawT[1][:, tb * 512:(tb + 1) * 512], pa, [pk], [("rawT", 1)]))
        proj_F(wt, wk, 256, 128, lambda pa, pk, tb: rope128(pa, pk, ksT[:, tb * 512:(tb + 1) * 512], "ksT", tb, tabs))
        proj_T(wt, wk, 384, 128, lambda pa, pk, t0, g: evac(vs[:, t0:t0 + g, 0:128], pa.rearrange("p (g c) -> p g c", c=128), [pk], ["vs"]))
        wt, wk = load_w_cols(win, OFF["nsa_kw"], 256, 16, wk_in)
        proj_F(wt, wk, 0, 128, lambda pa, pk, tb: rope128(pa, pk, kwT[:, tb * 512:(tb + 1) * 512], "kwT", tb, tabs))
        proj_T(wt, wk, 128, 128, lambda pa, pk, t0, g: evac(vw[:, t0:t0 + g, 0:128], pa.rearrange("p (g c) -> p g c", c=128), [pk], ["vw"]))
        for which, (pen, w1n, w2n) in enumerate((("nsa_pe_k", "nsa_cmp_k1", "nsa_cmp_k2"), ("nsa_pe_v", "nsa_cmp_v1", "nsa_cmp_v2"))):
            P.dma("sp", pe32[0:32, which, :], small[pen][l], writes=[("pe32", which)])
            b = nextps()
            tr(ps[b][:, 0:32], pe32[0:32, which, :], identf[0:32, 0:32], [("pe32", which), "identf"], PS(b))
            vcopy(peT[:, which, :], ps[b][:, 0:32], [PS(b)], [("peT", which)])
            P.dma("sp", w2[:, which, :, :], wb[w2n][l].rearrange("(k p) c -> p k c", p=128), reads=wkeys(w2n, l), writes=[("w2", which)])
            buf, wk1 = wslot()
            w1 = buf[:, 0:32 * 256].rearrange("p (l c) -> p l c", c=256)
            P.dma("sp", w1, wb[w1n][l].rearrange("(l p) c -> p l c", p=128), reads=wkeys(w1n, l), writes=[wk1])
            b = nextps()
            for j in range(2):
                for li in range(32):
                    mm(ps[b][:, j * 128:j * 128 + 127], w1[:, li, j * 128:(j + 1) * 128], rawT[which][:, li:li + 16 * 126 + 1:16],
                       li == 0, False, [wk1, ("rawT", which)], PS(b))
                    mm(ps[b][:, j * 128:j * 128 + 127], w1[:, li, j * 128:(j + 1) * 128],
                       peT[:, which, li:li + 1].to_broadcast([128, 127]), False, li == 31, [wk1, ("peT", which)], PS(b))
            hx = h1x[:, :].rearrange("p (j c) -> p j c", c=128)[:, :, 0:127]
            ha = h1a[:, :].rearrange("p (j c) -> p j c", c=128)[:, :, 0:127]
            pv = ps[b][:, 0:256].rearrange("p (j c) -> p j c", c=128)[:, :, 0:127]
            acopy(hx, pv, [PS(b)], ["h1x"])
            tt(ha, hx, hx, ALU.mult, ["h1x"], ["h1a"])
            ts(ha, ha, 0.044715, 1.0, ALU.mult, ALU.add, ["h1a"], ["h1a"])
            tt(ha, ha, hx, ALU.mult, ["h1a", "h1x"], ["h1a"])
            act(ha, ha, AF.Tanh, ["h1a"], ["h1a"], scale=0.7978845608028654)
            stt(ha, ha, 1.0, hx, ALU.add, ALU.mult, ["h1a", "h1x"], ["h1a"])
            ts(gT[:, which, :, 0:127], ha, 0.5, None, ALU.mult, None, ["h1a"], [("gT", which)])
            b = nextps()
            if which == 0:
                for j in range(2):
                    mm(ps[b][:, 0:127], w2[:, 0, j, :], gT[:, 0, j, 0:127], j == 0, j == 1, [("w2", 0), ("gT", 0)], PS(b))
                C_, S_, tk = tabs
                rope128(ps[b][:, 0:127], PS(b), kcmpT[:, 0:127], "kcmpT", 0, tabs, n=127,
                        pos=(C_[:, 31:S:16], S_[0:64, 31:S:16], S_[64:128, 31:S:16]))
            else:
                for j in range(2):
                    mm(ps[b][0:127, 0:128], gT[:, 1, j, 0:127], w2[:, 1, j, :], j == 0, j == 1, [("w2", 1), ("gT", 1)], PS(b))
                evac(vcx[0:127, 0:128], ps[b][0:127, 0:128], [PS(b)], ["vcx"])
        P.barrier()
        A.release(m_stage)
        pts = [(A.bf([128, 512]), ("pT", i)) for i in range(2)]
        oacc = A.f32([128, 4, 128])
        onb = A.bf([128, 4, 128])
        rec = A.f32([128, 4])
        coef = A.f32([128, 4])
        imp = A.f32([128, 32])
        imp2 = A.f32([128, 32])
        m8 = A.f32([128, 16])
        negsel = A.bf([128, 32])
        negselT = A.bf([32, 128])
        vkeys = ["vcx", "vcx1"]
        for qt in range(16):
            qsl = slice(qt * 128, (qt + 1) * 128)

            def qof(h, qsl=qsl):
                return (qTn[:, h, qsl], "qTn")

            gq = gsig[:, qt, :].rearrange("p (h c) -> p c h", c=3)
            kts = [dict(kT=(kcmpT[:, 0:127], "kcmpT"), ns=127, V=(vcx[0:127, 0:161], ["vcx", "vcx1"]),
                        masks=[(ident[0:127, 0:127], maskCmp[0:127, qsl], ["ident", "maskCmp"])])]
            for gi, hs in enumerate(((0, 1, 2), (3,))):
                res = attn_qtile(list(hs), qof, kts, 161, pts, (4, 5), 2 + gi)
                for hi, h in enumerate(hs):
                    oap, opk = res[hi]
                    ts(rec[:, h:h + 1], oap[:, 128:129], 1e-30, None, ALU.max, None, [opk, "vcx1"], [("rec", h)])
                    P.add("dve", lambda e, h=h: e.reciprocal(out=rec[:, h:h + 1], in_=rec[:, h:h + 1]), reads=[("rec", h)], writes=[("rec", h)])
                    tt(coef[:, h:h + 1], rec[:, h:h + 1], gq[:, 0, h:h + 1], ALU.mult, [("rec", h), "gsig"], [("coef", h)])
                    ts(oacc[:, h, :], oap[:, 0:128], coef[:, h:h + 1], None, ALU.mult, None, [opk, ("coef", h)], [("oacc", h)])
                    if h == 0:
                        ts(imp[:, :], oap[:, 129:161], rec[:, h:h + 1], None, ALU.mult, None, [opk, ("rec", h)], ["imp"])
                    else:
                        stt(imp[:, :], oap[:, 129:161], rec[:, h:h + 1], imp[:, :], ALU.mult, ALU.add, [opk, ("rec", h), "imp"], ["imp"])
            tt(imp2[:, :], imp[:, :], selMul[:, qt, :], ALU.mult, ["imp", "selMul"], ["imp2"])
            tt(imp2[:, :], imp2[:, :], selAdd[:, qt, :], ALU.add, ["imp2", "selAdd"], ["imp2"])
            P.add("dve", lambda e: e.max(out=m8[:, 0:8], in_=imp2[:, :]), reads=["imp2"], writes=["m8"])
            P.add("dve", lambda e: e.match_replace(out=imp[:, :], in_to_replace=m8[:, 0:8], in_values=imp2[:, :], imm_value=-1e30),
                  reads=["imp2", "m8"], writes=["imp"])
            P.add("dve", lambda e: e.max(out=m8[:, 8:16], in_=imp[:, :]), reads=["imp"], writes=["m8b"])
            ts(negsel[:, :], imp2[:, :], m8[:, 15:16], NEG, ALU.is_lt, ALU.mult, ["imp2", "m8b"], ["negsel"])
            pb = ps[6][:].bitcast(BF16)
            tr(pb[0:32, 0:128], negsel[:, :], ident[:], ["negsel", "ident"], PS(6))
            vcopy(negselT[0:32, :], pb[0:32, 0:128], [PS(6)], ["negselT"])
            for br in range(2):
                kts = []
                if br == 0:
                    for kt in range(qt + 1):
                        masks = [(expand[:, kt, :], negselT[0:32, :], ["expand", "negselT"])]
                        if kt == qt:
                            masks.append((ident[:], maskC[:], ["ident", "maskC"]))
                        kts.append(dict(kT=(ksT[:, kt * 128:(kt + 1) * 128], "ksT"), ns=128, V=(vs[:, kt, 0:129], ["vs", "vs1"]), masks=masks))
                else:
                    for kt in range(max(0, qt - 4), qt + 1):
                        masks = []
                        if kt == qt:
                            masks.append((ident[:], maskC[:], ["ident", "maskC"]))
                        if kt == qt - 4:
                            masks.append((ident[:], maskW[:], ["ident", "maskW"]))
                        kts.append(dict(kT=(kwT[:, kt * 128:(kt + 1) * 128], "kwT"), ns=128, V=(vw[:, kt, 0:129], ["vw", "vw1"]), masks=masks))
                for gi, hs in enumerate(((0, 1, 2), (3,))):
                    res = attn_qtile(list(hs), qof, kts, 129, pts, (4, 5), (0, 1, 2, 3)[(br * 2 + gi) % 4])
                    for hi, h in enumerate(hs):
                        oap, opk = res[hi]
                        P.add("dve", lambda e, h=h, oap=oap: e.reciprocal(out=rec[:, h:h + 1], in_=oap[:, 128:129]),
                              reads=[opk, "vs1", "vw1"], writes=[("rec", h)])
                        tt(coef[:, h:h + 1], rec[:, h:h + 1], gq[:, 1 + br, h:h + 1], ALU.mult, [("rec", h), "gsig"], [("coef", h)])
                        if br == 0:
                            stt(oacc[:, h, :], oap[:, 0:128], coef[:, h:h + 1], oacc[:, h, :], ALU.mult, ALU.add,
                                [opk, ("coef", h), ("oacc", h)], [("oacc", h)])
                        else:
                            stt(onb[:, h, :], oap[:, 0:128], coef[:, h:h + 1], oacc[:, h, :], ALU.mult, ALU.add,
                                [opk, ("coef", h), ("oacc", h)], [("onb", h)])
            pb7 = ps[7][:].bitcast(BF16)
            for h in range(4):
                tr(pb7[:, h * 128:(h + 1) * 128], onb[:, h, :], ident[:], [("onb", h), "ident"], PS(7))
            evac(oTn[:, :, qsl], pb7[:, 0:512].rearrange("p (h c) -> p h c", c=128), [PS(7)], ["oTn"])
        for h in range(4):
            P.dma("sp", oT_dram[(6 + h) * 128:(7 + h) * 128, :], oTn[:, h, :], reads=["oTn"], writes=[("dram", "oT", 6 + h)])

    def phase_dsa(l):
        P.barrier()
        A.release(0)
        win = wb["w_in"][l]
        wk_in = wkeys("w_in", l)
        qTd = A.bf([128, 6, S])
        kdT = A.bf([128, S])
        vd = A.bf([128, 16, 130])
        ikT = A.bf([128, S])
        iw = A.f32([128, 16, 16])
        m0 = A.mark()
        tabs = load_tables128()
        alloc_ropetmp()
        new_wslots(16 * 512 * 2)
        ckvnT = A.bf([128, 2, S])
        ckvn = [A.bf([128, 256]) for _ in range(2)]
        gB = A.f32([128, 256])
        ssq = A.f32([128, 4])
        junk = A.f32([128, 256])
        kvup = A.bf([128, 2, 256])
        memset(vd[:, :, 128:129], 1.0, ["vd1"])
        P.dma("sp", gB[:, :], small["dsa_kv_norm"][l:l + 1, :].to_broadcast([128, 256]), writes=["gB"])
        P.dma("sp", kvup[:, :, :], wb["dsa_kv_up"][l].rearrange("(k p) c -> p k c", p=128), reads=wkeys("dsa_kv_up", l), writes=["kvup"])
        wt, wk = load_w_cols(win, OFF["dsa_q"], 512, 16, wk_in)
        for h in range(4):
            proj_F(wt, wk, h * 128, 128, lambda pa, pk, tb, h=h: rope128(pa, pk, qTd[:, h, tb * 512:(tb + 1) * 512], "qTd", tb, tabs))
        wt, wk = load_w_cols(win, OFF["dsa_q"] + 512, 512, 16, wk_in)
        for h in range(2):
            proj_F(wt, wk, h * 128, 128, lambda pa, pk, tb, h=h: rope128(pa, pk, qTd[:, 4 + h, tb * 512:(tb + 1) * 512], "qTd", tb, tabs))

        def ckv_epi(pa, pk, t0, g):
            for j in range(g):
                t_ = t0 + j
                i2 = t_ % 2
                pj = pa[:, j * 256:(j + 1) * 256]
                act(junk[:, :], pj, AF.Square, [pk], ["junk", ("ssq", i2)], accum=ssq[:, i2:i2 + 1])
                ts(ssq[:, i2:i2 + 1], ssq[:, i2:i2 + 1], 1.0 / 256.0, 1e-6, ALU.mult, ALU.add, [("ssq", i2)], [("ssq", i2)])
                act(ssq[:, i2:i2 + 1], ssq[:, i2:i2 + 1], AF.Sqrt, [("ssq", i2)], [("ssq", i2)])
                P.add("dve", lambda e, i2=i2: e.reciprocal(out=ssq[:, i2:i2 + 1], in_=ssq[:, i2:i2 + 1]), reads=[("ssq", i2)], writes=[("ssq", i2)])
                stt(ckvn[i2][:, :], pj, ssq[:, i2:i2 + 1], gB[:, :], ALU.mult, ALU.mult, [pk, ("ssq", i2), "gB"], [("ckvn", i2)])
                pb = ps[6 + i2][:].bitcast(BF16)
                for c in range(2):
                    tr(pb[:, c * 128:(c + 1) * 128], ckvn[i2][:, c * 128:(c + 1) * 128], ident[:], [("ckvn", i2), "ident"], PS(6 + i2))
                evac(ckvnT[:, :, t_ * 128:(t_ + 1) * 128], pb[:, 0:256].rearrange("p (c t) -> p c t", t=128), [PS(6 + i2)], ["ckvnT"])

        proj_T(wt, wk, 256, 256, ckv_epi, group=2)
        proj_F(kvup, "kvup", 0, 128, lambda pa, pk, tb: rope128(pa, pk, kdT[:, tb * 512:(tb + 1) * 512], "kdT", tb, tabs),
               actT=ckvnT, akey="ckvnT", nk=2)
        proj_T(kvup, "kvup", 128, 128, lambda pa, pk, t0, g: evac(vd[:, t0:t0 + g, 0:128], pa.rearrange("p (g c) -> p g c", c=128), [pk], ["vd"]),
               actT=ckvnT, akey="ckvnT", nk=2)
        P.barrier()
        A.release(m0)
        iqT = A.bf([128, 8, S])
        m0 = A.mark()
        tabsi = load_tables64()
        alloc_ropetmp()
        new_wslots(16 * 512 * 2)
        wt, wk = load_w_cols(win, OFF["idx_k"], 80, 16, wk_in)
        proj_F(wt, wk, 0, 64, lambda pa, pk, tb: rope64(pa, pk, [ikT[0:64, tb * 512:(tb + 1) * 512], ikT[64:128, tb * 512:(tb + 1) * 512]], "ikT", tb, tabsi))
        proj_T(wt, wk, 64, 16, lambda pa, pk, t0, g: act(iw[:, t0:t0 + g, :], pa.rearrange("p (g c) -> p g c", c=16), AF.Copy, [pk], ["iw"], scale=1.0 / 32.0), group=16)
        for half in range(2):
            wt, wk = load_w_cols(win, OFF["idx_q"] + half * 512, 512, 16, wk_in)
            for hh in range(8):
                h = half * 8 + hh
                proj_F(wt, wk, hh * 64, 64, lambda pa, pk, tb, h=h: rope64(
                    pa, pk, [iqT[64 * (h % 2):64 * (h % 2) + 64, h // 2, tb * 512:(tb + 1) * 512]], "iqT", tb, tabsi))
        P.barrier()
        A.release(m0)
        scA = A.f32([128, S])
        scB = A.f32([128, S])
        rel = [A.f32([128, 512]) for _ in range(2)]
        negm = [A.bf([128, S]) for _ in range(2)]
        m8 = A.f32([128, 8])
        pts = [(A.bf([128, 384]), ("pT", i)) for i in range(2)]
        odb = A.bf([128, 6, 128])
        oTd = [A.bf([128, 6, 128]) for _ in range(2)]
        rec = A.f32([128, 6])
        for qt in range(16):
            qsl = slice(qt * 128, (qt + 1) * 128)
            nk_ = (qt + 1) * 128
            nm = negm[qt % 2]
            nmk = ("negm", qt % 2)
            if qt >= 2:
                for kb in range(0, nk_, 512):
                    n = min(512, nk_ - kb)
                    for h in range(16):
                        b = nextps()
                        pb_ = 64 * (h % 2)
                        mm(ps[b][:, 0:n], iqT[pb_:pb_ + 64, h // 2, qsl], ikT[pb_:pb_ + 64, kb:kb + n], True, True, ["iqT", "ikT"], PS(b))
                        r_ = rel[h % 2]
                        act(r_[:, 0:n], ps[b][:, 0:n], AF.Relu, [PS(b)], [("rel", h % 2)])
                        if h == 0:
                            ts(scA[:, kb:kb + n], r_[:, 0:n], iw[:, qt, h:h + 1], None, ALU.mult, None, [("rel", h % 2), "iw"], [("scA", kb)])
                        else:
                            stt(scA[:, kb:kb + n], r_[:, 0:n], iw[:, qt, h:h + 1], scA[:, kb:kb + n], ALU.mult, ALU.add,
                                [("rel", h % 2), "iw", ("scA", kb)], [("scA", kb)])
                sck = [("scA", kb) for kb in range(0, nk_, 512)]
                tt(scA[:, qt * 128:nk_], scA[:, qt * 128:nk_], negC[:, :], ALU.add, sck + ["negC"], sck)
                P.add("dve", lambda e, nk_=nk_: e.max(out=m8[:, :], in_=scA[:, 0:nk_]), reads=sck, writes=["m8"])
                P.add("dve", lambda e, nk_=nk_: e.match_replace(out=scB[:, 0:nk_], in_to_replace=m8[:, :], in_values=scA[:, 0:nk_], imm_value=-1e30),
                      reads=sck + ["m8"], writes=["scB"])
                for r in range(31):
                    P.add("dve", lambda e, nk_=nk_: e.max(out=m8[:, :], in_=scB[:, 0:nk_]), reads=["scB"], writes=["m8"])
                    if r < 30:
                        P.add("dve", lambda e, nk_=nk_: e.match_replace(out=scB[:, 0:nk_], in_to_replace=m8[:, :], in_values=scB[:, 0:nk_], imm_value=-1e30),
                              reads=["scB", "m8"], writes=["scB"])
                ts(nm[:, 0:nk_], scA[:, 0:nk_], m8[:, 7:8], NEG, ALU.is_lt, ALU.mult, sck + ["m8"], [nmk])
            kts = []
            for kt in range(qt + 1):
                masks = []
                if qt >= 2:
                    masks.append((nm[:, kt * 128:(kt + 1) * 128], ident[:], [nmk, "ident"]))
                if kt == qt:
                    masks.append((ident[:], maskC[:], ["ident", "maskC"]))
                kts.append(dict(kT=(kdT[:, kt * 128:(kt + 1) * 128], "kdT"), ns=128, V=(vd[:, kt, 0:129], ["vd", "vd1"]), masks=masks))
            for gi, hs in enumerate(((0, 1, 2), (3, 4, 5))):
                res = attn_qtile(list(hs), lambda h, qsl=qsl: (qTd[:, h, qsl], "qTd"), kts, 129, pts, (4, 5), 2 + gi)
                for hi, h in enumerate(hs):
                    oap, opk = res[hi]
                    P.add("dve", lambda e, h=h, oap=oap: e.reciprocal(out=rec[:, h:h + 1], in_=oap[:, 128:129]), reads=[opk, "vd1"], writes=[("rec", h)])
                    ts(odb[:, h, :], oap[:, 0:128], rec[:, h:h + 1], None, ALU.mult, None, [opk, ("rec", h)], [("odb", h)])
            ot = oTd[qt % 2]
            otk = ("oTd", qt % 2)
            for half in range(2):
                bnk = 6 + half
                pb = ps[bnk][:].bitcast(BF16)
                for j in range(3):
                    h = half * 3 + j
                    tr(pb[:, j * 128:(j + 1) * 128], odb[:, h, :], ident[:], [("odb", h), "ident"], PS(bnk))
                evac(ot[:, half * 3:half * 3 + 3, :], pb[:, 0:384].rearrange("p (h c) -> p h c", c=128), [PS(bnk)], [otk])
            P.dma("sp", oT_dram[10 * 128:16 * 128, qsl].rearrange("(h d) t -> d h t", d=128), ot[:, :, :], reads=[otk],
                  writes=[("dram", "oT", 10 + qt * 0)])

    def phase_merge(l):
        P.barrier()
        A.release(0)
        win = wb["w_in"][l]
        wk_in = wkeys("w_in", l)
        oT = A.bf([128, 16, S])
        new_wslots(16 * 512 * 2)
        sg = [A.f32([128, 512]) for _ in range(3)]
        macc = [A.f32([128, 512]) for _ in range(2)]
        mix = [A.bf([128, S]) for _ in range(2)]
        for h in range(16):
            P.dma("sp", oT[:, h, :], oT_dram[h * 128:(h + 1) * 128, :], reads=[("dram", "oT", min(h, 10))], writes=["oT"])
        brs = (("w_br_fox", 0, 6), ("w_br_nsa", 6, 4), ("w_br_dsa", 10, 6))
        for c in range(16):
            buf, wk = wslot()
            wt = buf[:, 0:16 * 512].rearrange("p (k c) -> p k c", c=512)
            for bi, (wn, h0, nh) in enumerate(brs):
                P.dma("sp", wt[:, h0:h0 + nh, 0:128], wb[wn][l][:, c * 128:(c + 1) * 128].rearrange("(k p) c -> p k c", p=128),
                      reads=wkeys(wn, l), writes=[wk])
                g0 = OFF["gate"] + bi * 2048 + c * 128
                P.dma("sp", wt[:, :, 128 * (bi + 1):128 * (bi + 2)], win[:, g0:g0 + 128].rearrange("(k p) c -> p k c", p=128),
                      reads=wk_in, writes=[wk])
            mx = mix[c % 2]
            mxk = ("mix", c % 2)
            for tb in range(4):
                tsl = slice(tb * 512, (tb + 1) * 512)
                mk = ("macc", tb % 2)
                ma = macc[tb % 2]
                for bi, (wn, h0, nh) in enumerate(brs):
                    bg = nextps(0, 8)
                    for kc in range(16):
                        mm(ps[bg][:, :], wt[:, kc, 128 * (bi + 1):128 * (bi + 2)], hT[:, kc, tsl], kc == 0, kc == 15, [wk] + hkeys(tb * 4, 4), PS(bg))
                    act(sg[bi][:, :], ps[bg][:, :], AF.Sigmoid, [PS(bg)], [("sg", bi)])
                    bo = nextps(0, 8)
                    for j in range(nh):
                        mm(ps[bo][:, :], wt[:, h0 + j, 0:128], oT[:, h0 + j, tsl], j == 0, j == nh - 1, [wk, "oT"], PS(bo))
                    if bi == 0:
                        tt(ma[:, :], ps[bo][:, :], sg[bi][:, :], ALU.mult, [PS(bo), ("sg", bi)], [mk])
                    else:
                        tt(sg[bi][:, :], ps[bo][:, :], sg[bi][:, :], ALU.mult, [PS(bo), ("sg", bi)], [("sg", bi)])
                        if bi == 1:
                            tt(ma[:, :], ma[:, :], sg[bi][:, :], ALU.add, [mk, ("sg", bi)], [mk])
                        else:
                            tt(mx[:, tsl], ma[:, :], sg[bi][:, :], ALU.add, [mk, ("sg", bi)], [mxk])
            P.dma("sp", mixT_dram[c * 128:(c + 1) * 128, :], mx[:, :], reads=[mxk], writes=[("dram", "mixT", c)])

    def gemm_resid(actT, akey, nk, wsrc2d, wkl, ncg, cgw, tiles, hsrc):
        hin = [A.f32([128, cgw]) for _ in range(2)]
        rout = [A.f32([128, cgw]) for _ in range(2)]
        k_ = 0
        nxt = load_w_cols(wsrc2d, 0, cgw, nk, wkl)
        for cg in range(ncg):
            wt, wk = nxt
            if cg + 1 < ncg:
                nxt = load_w_cols(wsrc2d, (cg + 1) * cgw, cgw, nk, wkl)
            for t_ in tiles:
                i2 = k_ % 2
                k_ += 1
                P.dma("sp", hin[i2][:, :], hsrc[t_ * 128:(t_ + 1) * 128, cg * cgw:(cg + 1) * cgw], reads=[("dram", "h", t_)], writes=[("hin", i2)])
                b = nextps(0, 8)
                for kc in range(nk):
                    mm(ps[b][:, 0:cgw], actT(kc, t_), wt[:, kc, :], kc == 0, kc == nk - 1, [wk, akey], PS(b))
                stt(rout[i2][:, :], hin[i2][:, :], ALPHA, ps[b][:, 0:cgw], ALU.mult, ALU.add, [("hin", i2), PS(b)], [("rout", i2)])
                P.dma("sp", r_dram[t_ * 128:(t_ + 1) * 128, cg * cgw:(cg + 1) * cgw], rout[i2][:, :], reads=[("rout", i2)], writes=[("dram", "r", t_)])

    def layernorm_pass(l, gname, bname, tiles, hdst, write_y=False):
        gB = A.f32([128, D])
        bB = A.f32([128, D])
        rt = [A.f32([128, D]) for _ in range(2)]
        hb = [A.bf([128, D]) for _ in range(2)]
        st = A.f32([128, 2, 4, 6])
        mv = A.f32([128, 2, 2])
        P.dma("sp", gB[:, :], small[gname][l:l + 1, :].to_broadcast([128, D]), writes=["lnG"])
        P.dma("sp", bB[:, :], small[bname][l:l + 1, :].to_broadcast([128, D]), writes=["lnB"])
        for k_, t_ in enumerate(tiles):
            i2 = k_ % 2
            r_ = rt[i2]
            rk = ("rt", i2)
            P.dma("sp", r_[:, :], r_dram[t_ * 128:(t_ + 1) * 128, :], reads=[("dram", "r", t_)], writes=[rk])
            for c in range(4):
                P.add("dve", lambda e, c=c, r_=r_, i2=i2: e.bn_stats(out=st[:, i2, c, :], in_=r_[:, c * 512:(c + 1) * 512]), reads=[rk], writes=[("st", i2)])
            P.add("dve", lambda e, i2=i2: e.bn_aggr(out=mv[:, i2, :], in_=st[:, i2, :, :].rearrange("p a b -> p (a b)")), reads=[("st", i2)], writes=[("mv", i2)])
            ts(mv[:, i2, 1:2], mv[:, i2, 1:2], 1e-5, None, ALU.add, None, [("mv", i2)], [("mv", i2)])
            act(mv[:, i2, 1:2], mv[:, i2, 1:2], AF.Sqrt, [("mv", i2)], [("mv", i2)])
            P.add("dve", lambda e, i2=i2: e.reciprocal(out=mv[:, i2, 1:2], in_=mv[:, i2, 1:2]), reads=[("mv", i2)], writes=[("mv", i2)])
            ts(r_[:, :], r_[:, :], mv[:, i2, 0:1], mv[:, i2, 1:2], ALU.subtract, ALU.mult, [rk, ("mv", i2)], [rk])
            tt(r_[:, :], r_[:, :], gB[:, :], ALU.mult, [rk, "lnG"], [rk])
            tt(r_[:, :], r_[:, :], bB[:, :], ALU.add, [rk, "lnB"], [rk])
            P.dma("sp", hdst[t_ * 128:(t_ + 1) * 128, :], r_[:, :], reads=[rk], writes=[("dram", "h", t_)])
            act(hb[i2][:, :], r_[:, :], AF.Copy, [rk], [("hb", i2)])
            for q4 in range(4):
                bnk = 4 + (k_ * 4 + q4) % 4
                pb = ps[bnk][:].bitcast(BF16)
                for j in range(4):
                    kc = q4 * 4 + j
                    tr(pb[:, j * 128:(j + 1) * 128], hb[i2][:, kc * 128:(kc + 1) * 128], ident[:], [("hb", i2), "ident"], PS(bnk))
                evac(hT[:, q4 * 4:q4 * 4 + 4, t_ * 128:(t_ + 1) * 128], pb[:, 0:512].rearrange("p (k t) -> p k t", t=128), [PS(bnk)], [("hT", t_)])

    def phase_outproj(l):
        P.barrier()
        A.release(0)
        mixT = A.bf([128, 16, S])
        new_wslots(16 * 512 * 2)
        for c in range(16):
            P.dma("sp", mixT[:, c, :], mixT_dram[c * 128:(c + 1) * 128, :], reads=[("dram", "mixT", c)], writes=["mixT"])
        gemm_resid(lambda kc, t_: mixT[:, kc, t_ * 128:(t_ + 1) * 128], "mixT", 16, wb["w_out"][l], wkeys("w_out", l), 4, 512, range(NT), h_dram)
        P.barrier()
        A.release(0)
        layernorm_pass(l, "ln1_g", "ln1_b", range(NT), h_dram)

    def phase_ffn(l):
        w1 = wb["w_ffn_in"][l]
        w2 = wb["w_ffn_out"][l]
        k1 = wkeys("w_ffn_in", l)
        k2 = wkeys("w_ffn_out", l)
        for tb in range(4):
            P.barrier()
            A.release(0)
            uT = A.bf([128, 44, 512])
            new_wslots(44 * 256 * 2)
            sa = [A.f32([128, 512]) for _ in range(2)]
            tsl = slice(tb * 512, (tb + 1) * 512)

            def load_ab(c):
                buf, wk = wslot()
                wt = buf[:, 0:16 * 256].rearrange("p (k c) -> p k c", c=256)
                P.dma("sp", wt[:, :, 0:128], w1[:, c * 128:(c + 1) * 128].rearrange("(k p) c -> p k c", p=128), reads=k1, writes=[wk])
                P.dma("sp", wt[:, :, 128:256], w1[:, DFF + c * 128:DFF + (c + 1) * 128].rearrange("(k p) c -> p k c", p=128), reads=k1, writes=[wk])
                return wt, wk

            nxt = load_ab(0)
            for c in range(44):
                wt, wk = nxt
                if c + 1 < 44:
                    nxt = load_ab(c + 1)
                ba = nextps(0, 8)
                for kc in range(16):
                    mm(ps[ba][:, :], wt[:, kc, 0:128], hT[:, kc, tsl], kc == 0, kc == 15, [wk] + hkeys(tb * 4, 4), PS(ba))
                bb = nextps(0, 8)
                for kc in range(16):
                    mm(ps[bb][:, :], wt[:, kc, 128:256], hT[:, kc, tsl], kc == 0, kc == 15, [wk] + hkeys(tb * 4, 4), PS(bb))
                s_ = sa[c % 2]
                act(s_[:, :], ps[ba][:, :], AF.Silu, [PS(ba)], [("sa", c % 2)])
                tt(uT[:, c, :], s_[:, :], ps[bb][:, :], ALU.mult, [("sa", c % 2), PS(bb)], [("uT", c)])
            P.add("dve", lambda e: e.memset(sa[0][:, 0:1], 0.0), reads=[("uT", c) for c in range(44)] + [("sa", 0)], writes=["uTall", ("sa", 0)])
            gemm_resid(lambda kc, t_: uT[:, kc, (t_ % 4) * 128:(t_ % 4 + 1) * 128], "uTall", 44, w2, k2, 8, 256,
                       range(tb * 4, tb * 4 + 4), h_dram)
        P.barrier()
        A.release(0)
        layernorm_pass(l, "ln2_g", "ln2_b", range(NT), h_dram)

    def to_hT_tile(src, skey, t_, hbuf, hkey, k_):
        act(hbuf[:, :], src, AF.Copy, [skey], [hkey])
        for q4 in range(4):
            bnk = 4 + (k_ * 4 + q4) % 4
            pb = ps[bnk][:].bitcast(BF16)
            for j in range(4):
                kc = q4 * 4 + j
                tr(pb[:, j * 128:(j + 1) * 128], hbuf[:, kc * 128:(kc + 1) * 128], ident[:], [hkey, "ident"], PS(bnk))
            evac(hT[:, q4 * 4:q4 * 4 + 4, t_ * 128:(t_ + 1) * 128], pb[:, 0:512].rearrange("p (k t) -> p k t", t=128),
                 [PS(bnk)], [("hT", t_)])

    def phase_ple(l, last):
        P.barrier()
        A.release(0)
        new_wslots(16 * 512 * 2)
        pT = A.bf([128, 2, S])
        pin = [A.f32([128, 256]) for _ in range(2)]
        pbf = [A.bf([128, 256]) for _ in range(2)]
        wpl = A.bf([128, 2, D])
        hin = [A.f32([128, 512]) for _ in range(2)]
        sg = [A.f32([128, 512]) for _ in range(2)]
        P.dma("sp", wpl[:, :, :], wb["w_ple_in"][l].rearrange("(k p) c -> p k c", p=128), reads=wkeys("w_ple_in", l), writes=["wpl"])
        for t_ in range(NT):
            i2 = t_ % 2
            P.dma("sp", pin[i2][:, :], p_d[l, t_ * 128:(t_ + 1) * 128, :], writes=[("pin", i2)])
            vcopy(pbf[i2][:, :], pin[i2][:, :], [("pin", i2)], [("pbf", i2)])
            pb = ps[6 + i2][:].bitcast(BF16)
            for c in range(2):
                tr(pb[:, c * 128:(c + 1) * 128], pbf[i2][:, c * 128:(c + 1) * 128], ident[:], [("pbf", i2), "ident"], PS(6 + i2))
            evac(pT[:, :, t_ * 128:(t_ + 1) * 128], pb[:, 0:256].rearrange("p (c t) -> p c t", t=128), [PS(6 + i2)], ["pT"])
        wg = wb["w_ple_gate"][l]
        kg = wkeys("w_ple_gate", l)
        k_ = 0
        nxt = load_w_cols(wg, 0, 512, 16, kg)
        for cg in range(4):
            wt, wk = nxt
            if cg + 1 < 4:
                nxt = load_w_cols(wg, (cg + 1) * 512, 512, 16, kg)
            for t_ in range(NT):
                i2 = k_ % 2
                k_ += 1
                P.dma("sp", hin[i2][:, :], h_dram[t_ * 128:(t_ + 1) * 128, cg * 512:(cg + 1) * 512], reads=[("dram", "h", t_)], writes=[("hin", i2)])
                bg = nextps(0, 4)
                for kc in range(16):
                    mm(ps[bg][:, :], hT[:, kc, t_ * 128:(t_ + 1) * 128], wt[:, kc, :], kc == 0, kc == 15, [wk, ("hT", t_)], PS(bg))
                be = nextps(0, 4)
                for kc in range(2):
                    mm(ps[be][:, :], pT[:, kc, t_ * 128:(t_ + 1) * 128], wpl[:, kc, cg * 512:(cg + 1) * 512], kc == 0, kc == 1, ["pT", "wpl"], PS(be))
                act(sg[i2][:, :], ps[bg][:, :], AF.Sigmoid, [PS(bg)], [("sg", i2)])
                tt(sg[i2][:, :], sg[i2][:, :], ps[be][:, :], ALU.mult, [("sg", i2), PS(be)], [("sg", i2)])
                tt(hin[i2][:, :], hin[i2][:, :], sg[i2][:, :], ALU.add, [("hin", i2), ("sg", i2)], [("hin", i2)])
                if last:
                    finals.append(P.dma("sp", y_d[t_ * 128:(t_ + 1) * 128, cg * 512:(cg + 1) * 512], hin[i2][:, :],
                                        reads=[("hin", i2)], writes=[("dram", "y", t_, cg)]))
                else:
                    P.dma("sp", r_dram[t_ * 128:(t_ + 1) * 128, cg * 512:(cg + 1) * 512], hin[i2][:, :], reads=[("hin", i2)],
                          writes=[("dram", "r", t_)])
        if not last:
            P.barrier()
            A.release(0)
            rt = [A.f32([128, D]) for _ in range(2)]
            hb = [A.bf([128, D]) for _ in range(2)]
            for t_ in range(NT):
                i2 = t_ % 2
                P.dma("sp", rt[i2][:, :], r_dram[t_ * 128:(t_ + 1) * 128, :], reads=[("dram", "r", t_)], writes=[("rt", i2)])
                P.dma("sp", h_dram[t_ * 128:(t_ + 1) * 128, :], rt[i2][:, :], reads=[("rt", i2)], writes=[("dram", "h", t_)])
                to_hT_tile(rt[i2][:, :], ("rt", i2), t_, hb[i2], ("hb", i2), t_)

    finals = []

    cast_layer(0)
    A.release(0)
    xin = [A.f32([128, D]) for _ in range(2)]
    xb = [A.bf([128, D]) for _ in range(2)]
    for t_ in range(NT):
        i2 = t_ % 2
        P.dma("sp", xin[i2][:, :], x_d[t_ * 128:(t_ + 1) * 128, :], writes=[("xin", i2)])
        P.dma("sp", h_dram[t_ * 128:(t_ + 1) * 128, :], xin[i2][:, :], reads=[("xin", i2)], writes=[("dram", "h", t_)])
        to_hT_tile(xin[i2][:, :], ("xin", i2), t_, xb[i2], ("xb", i2), t_)

    def dbg_dump(name, src_ap, keys):
        if name in dbg_d:
            P.barrier()
            finals.append(P.dma("sp", dbg_d[name], src_ap, reads=keys, writes=[("dram", "dbg", name)]))

    for l in range(depth):
        if l + 1 < depth:
            cast_layer(l + 1)
        phase_fox(l)
        phase_nsa(l)
        phase_dsa(l)
        if l == 0:
            dbg_dump("oT", oT_dram, [("dram", "oT", h) for h in range(11)])
        phase_merge(l)
        if l == 0:
            dbg_dump("mixT", mixT_dram, [("dram", "mixT", c) for c in range(16)])
        phase_outproj(l)
        if l == 0:
            dbg_dump("h1", h_dram, [("dram", "h", t) for t in range(NT)])
        phase_ffn(l)
        if l == 0:
            dbg_dump("h2", h_dram, [("dram", "h", t) for t in range(NT)])
        phase_ple(l, l == depth - 1)
    P.emit(nc, ctx, final_wait_ops=finals)
    return nc, ctx


_CACHE = {}


def kernel(**inputs):
    n_cores = 8
    if "nc" not in _CACHE:
        _CACHE["nc"] = build(DEPTH)
    nc, _ctx = _CACHE["nc"]
    consts = make_consts()
    shared = {}
    for n in WSHAPES:
        shared[n] = np.ascontiguousarray(np.asarray(inputs[n], dtype=np.float32))
    for n in SMALL:
        shared[n] = np.ascontiguousarray(np.asarray(inputs[n], dtype=np.float32))
    for n, v in consts.items():
        shared["c_" + n] = np.ascontiguousarray(v)
    x = np.asarray(inputs["x"], dtype=np.float32)
    p = np.asarray(inputs["p"], dtype=np.float32)
    in_maps = []
    for c in range(n_cores):
        b = c % 4
        m = dict(shared)
        m["x"] = np.ascontiguousarray(x[b])
        m["p"] = np.ascontiguousarray(p[:, b])
        in_maps.append(m)
    res = run_bass_kernel_spmd(nc, in_maps, core_ids=list(range(n_cores)))
    out = np.stack([np.asarray(res.results[b]["y"], dtype=np.float32) for b in range(4)], axis=0)
    return out
```

```python
import numpy as np
import ml_dtypes
from contextlib import ExitStack
import concourse.bass as bass
import concourse.mybir as mybir
from concourse.bass_utils import run_bass_kernel_spmd

F32 = mybir.dt.float32
BF16 = mybir.dt.bfloat16
AF = mybir.ActivationFunctionType
ALU = mybir.AluOpType

D = 2048
S = 2048
NT = 16
DEPTH = 4
DFF = 5632
INC = 11874
PLE = 256
ALPHA = float((2 * DEPTH) ** 0.25)
SC = float(128 ** -0.5)
NEG = -30000.0
OFF = dict(fox_q=0, fox_k=768, fox_v=1536, fox_f=2304, nsa_q=2310, nsa_kc=2822, nsa_vc=2950, nsa_ks=3078,
           nsa_vs=3206, nsa_kw=3334, nsa_vw=3462, nsa_g=3590, dsa_q=3602, dsa_ckv=4370, idx_q=4626,
           idx_k=5650, idx_w=5714, gate=5730)
WSHAPES = dict(w_in=(2048, INC), nsa_cmp_k1=(4096, 256), nsa_cmp_k2=(256, 128), nsa_cmp_v1=(4096, 256),
               nsa_cmp_v2=(256, 128), dsa_kv_up=(256, 256), w_br_fox=(768, 2048), w_br_nsa=(512, 2048),
               w_br_dsa=(768, 2048), w_out=(2048, 2048), w_ffn_in=(2048, 2 * DFF), w_ffn_out=(DFF, 2048),
               w_ple_in=(256, 2048), w_ple_gate=(2048, 2048))
WORDER = ["w_in", "nsa_cmp_k1", "nsa_cmp_k2", "nsa_cmp_v1", "nsa_cmp_v2", "dsa_kv_up", "w_br_fox", "w_br_nsa",
          "w_br_dsa", "w_out", "w_ffn_in", "w_ffn_out", "w_ple_in", "w_ple_gate"]
SMALL = dict(fox_f_bias=(6,), nsa_pe_k=(32, 128), nsa_pe_v=(32, 128), dsa_kv_norm=(256,), ln1_g=(2048,),
             ln1_b=(2048,), ln2_g=(2048,), ln2_b=(2048,))
ENGS = ("pe", "act", "dve", "pool", "sp")


def _lineno():
    import sys
    f = sys._getframe(2)
    out = []
    for _ in range(4):
        if f is None:
            break
        out.append(f.f_lineno)
        f = f.f_back
    return out


class Op:
    __slots__ = ("eng", "fn", "deps", "signal", "tick", "is_dma", "dsem", "dcount", "dprev", "hard", "line")

    def __init__(self, eng, fn, is_dma=False):
        self.eng = eng
        self.fn = fn
        self.deps = set()
        self.signal = False
        self.tick = 0
        self.is_dma = is_dma
        self.dsem = None
        self.dcount = 0
        self.dprev = 0
        self.hard = False


class Res:
    __slots__ = ("last_w", "readers")

    def __init__(self):
        self.last_w = None
        self.readers = []


class Prog:
    N_DMA_SEMS = 14

    def __init__(self):
        self.ops = []
        self.res = {}
        self.last = {e: None for e in ENGS}
        self.dmas_since_bar = []

    def _r(self, key):
        r = self.res.get(key)
        if r is None:
            r = self.res[key] = Res()
        return r

    def add(self, eng, fn, reads=(), writes=(), is_dma=False, nobar=False):
        op = Op(eng, fn, is_dma)
        op.line = _lineno()
        for k in reads:
            r = self._r(k)
            if r.last_w is not None:
                op.deps.add(r.last_w)
        for k in writes:
            r = self._r(k)
            if r.last_w is not None:
                op.deps.add(r.last_w)
            op.deps.update(r.readers)
        for k in reads:
            self._r(k).readers.append(op)
        for k in writes:
            r = self._r(k)
            r.last_w = op
            r.readers = []
        op.deps.discard(op)
        self.ops.append(op)
        self.last[eng] = op
        if is_dma and not nobar:
            self.dmas_since_bar.append(op)
        return op

    def dma(self, q, out, in_, reads=(), writes=(), nobar=False):
        return self.add(q, lambda e: e.dma_start(out=out, in_=in_), reads, writes, is_dma=True, nobar=nobar)

    def barrier(self):
        lasts = [op for op in self.last.values() if op is not None and not op.is_dma]
        dmas = list(self.dmas_since_bar)
        self.dmas_since_bar = []
        for e in ("pe", "act", "dve", "sp"):
            op = Op(e, lambda eng: eng.nop())
            op.hard = True
            op.deps.update(lasts)
            op.deps.update(dmas)
            self.ops.append(op)
            self.last[e] = op
        self.res = {k: v for k, v in self.res.items() if isinstance(k, tuple) and k and k[0] in ("ps", "dram", "wb")}

    def emit(self, nc, ctx, final_wait_ops=()):
        streams = {e: [] for e in ENGS}
        for op in self.ops:
            streams[op.eng].append(op)
        for op in self.ops:
            for d in op.deps:
                if d.is_dma:
                    continue
                if d.eng == op.eng and d.eng == "pe" and not op.hard:
                    continue
                d.signal = True
        for e in ENGS:
            t = 0
            for op in streams[e]:
                if not op.is_dma and op.signal:
                    t += 1
                    op.tick = t
        esem = {e: ctx.enter_context(nc.semaphore("s_" + e)) for e in ENGS}
        dsems = {}
        for q in ENGS:
            if any(op.is_dma for op in streams[q]):
                dsems[q] = [ctx.enter_context(nc.semaphore("d_%s_%d" % (q, i))) for i in range(self.N_DMA_SEMS)]
        for q, sl in dsems.items():
            cnt = [0] * len(sl)
            i = 0
            for op in streams[q]:
                if op.is_dma:
                    j = i % len(sl)
                    op.dsem = (q, j)
                    op.dprev = cnt[j]
                    cnt[j] += 16
                    op.dcount = cnt[j]
                    i += 1
        block = ctx.enter_context(nc.Block())

        def run_stream(ename, eng):
            known = {}
            for op in streams[ename]:
                waits = {}
                for d in op.deps:
                    if d.is_dma:
                        key = ("d",) + d.dsem
                        val = d.dcount
                    else:
                        if d.eng == ename and ename == "pe" and not op.hard:
                            continue
                        key = ("e", d.eng)
                        val = d.tick
                    if waits.get(key, 0) < val:
                        waits[key] = val
                if op.is_dma and op.dprev > 0:
                    key = ("d",) + op.dsem
                    if waits.get(key, 0) < op.dprev:
                        waits[key] = op.dprev
                for key, val in waits.items():
                    if known.get(key, 0) >= val:
                        continue
                    known[key] = val
                    sem = esem[key[1]] if key[0] == "e" else dsems[key[1]][key[2]]
                    eng.wait_ge(sem, val)
                try:
                    ins = op.fn(eng)
                except Exception:
                    print("EMIT FAILED for op recorded at lines", getattr(op, "line", None), flush=True)
                    raise
                if op.is_dma:
                    ins.then_inc(dsems[op.dsem[0]][op.dsem[1]], 16)
                elif op.signal:
                    ins.then_inc(esem[ename], 1)
            if ename == "sp":
                for op in final_wait_ops:
                    eng.wait_ge(dsems[op.dsem[0]][op.dsem[1]], op.dcount)

        @block.tensor
        def _(e):
            run_stream("pe", e)

        @block.scalar
        def _(e):
            run_stream("act", e)

        @block.vector
        def _(e):
            run_stream("dve", e)

        @block.gpsimd
        def _(e):
            run_stream("pool", e)

        @block.sync
        def _(e):
            run_stream("sp", e)


class Arena:
    def __init__(self, t, nwords):
        self.t = t
        self.nwords = nwords
        self.off = 0
        self.n = 0

    def mark(self):
        return self.off

    def release(self, m):
        self.off = m

    def _alloc(self, nbytes):
        nb = (nbytes + 63) // 64 * 64
        o = self.off
        self.off += nb
        assert self.off <= self.nwords * 4, ("arena overflow", self.off, self.nwords * 4)
        return o

    def f32(self, shape):
        n = int(np.prod(shape[1:]))
        o = self._alloc(n * 4)
        v = self.t[:, o // 4: o // 4 + n]
        if len(shape) == 3:
            v = v.rearrange("p (a b) -> p a b", b=shape[2])
        elif len(shape) == 4:
            v = v.rearrange("p (a b c) -> p a b c", b=shape[2], c=shape[3])
        return v

    def bf(self, shape):
        n = int(np.prod(shape[1:]))
        n2 = (n + 1) // 2
        o = self._alloc(n2 * 4)
        v = self.t[:, o // 4: o // 4 + n2].bitcast(BF16)[:, 0:n]
        if len(shape) == 3:
            v = v.rearrange("p (a b) -> p a b", b=shape[2])
        elif len(shape) == 4:
            v = v.rearrange("p (a b c) -> p a b c", b=shape[2], c=shape[3])
        return v


def _rope_np(n, dim):
    inv = (1.0 / (np.float32(10000.0) ** (np.arange(0, dim, 2, dtype=np.float32) / np.float32(dim)))).astype(np.float32)
    ang = np.arange(n, dtype=np.float32)[:, None] * inv[None, :]
    return np.cos(ang).astype(np.float32), np.sin(ang).astype(np.float32)


def make_consts():
    c = {}
    cos, sin = _rope_np(S, 128)
    c["ropeC"] = np.ascontiguousarray(np.concatenate([cos, cos], 1).T)
    c["ropeS"] = np.ascontiguousarray(np.concatenate([-sin, sin], 1).T)
    ci, si = _rope_np(S, 64)
    c["ropeCi"] = np.ascontiguousarray(np.concatenate([ci, ci], 1).T)
    c["ropeSi"] = np.ascontiguousarray(np.concatenate([-si, si], 1).T)
    sp = np.arange(128)[:, None]
    tp = np.arange(128)[None, :]
    c["maskC"] = np.where(sp <= tp, 0.0, NEG).astype(ml_dtypes.bfloat16)
    c["maskW"] = np.where(sp > tp, 0.0, NEG).astype(ml_dtypes.bfloat16)
    cc = np.arange(127)[:, None]
    tt = np.arange(S)[None, :]
    cm = np.zeros((128, S), np.float32)
    cm[:127] = np.where(16 * cc + 31 <= tt, 0.0, NEG)
    c["maskCmp"] = cm.astype(ml_dtypes.bfloat16)
    c_start = np.arange(127) * 16
    c_end = c_start + 31
    s_start = np.arange(32) * 64
    ov = np.maximum(np.minimum(c_end[:, None], s_start[None, :] + 63) - np.maximum(c_start[:, None], s_start[None, :]) + 1, 0) / 32.0
    oe = np.zeros((128, 33), np.float32)
    oe[:127, 0] = 1.0
    oe[:127, 1:] = ov
    c["ovl"] = oe.astype(ml_dtypes.bfloat16)
    pos = np.arange(S)
    blk = np.arange(32)[None, :]
    cur = (pos // 64)[:, None]
    forced = (blk == 0) | (blk == cur) | (blk == cur - 1)
    vis = s_start[None, :] <= pos[:, None]
    mul = (vis & ~forced).astype(np.float32)
    add = np.where(forced, 1e4, np.where(vis, 0.0, -1e4)).astype(np.float32)
    c["selMul"] = np.ascontiguousarray(mul.reshape(16, 128, 32).transpose(1, 0, 2))
    c["selAdd"] = np.ascontiguousarray(add.reshape(16, 128, 32).transpose(1, 0, 2))
    ex = np.zeros((32, 16, 128), np.float32)
    for kt in range(16):
        for s_ in range(128):
            ex[2 * kt + s_ // 64, kt, s_] = 1.0
    c["expand"] = ex.astype(ml_dtypes.bfloat16)
    c["negC"] = np.where(tp.T >= sp.T, 0.0, -1e30).astype(np.float32)
    c["negC"] = np.where(np.arange(128)[None, :] <= np.arange(128)[:, None], 0.0, -1e30).astype(np.float32)
    return c


CONST_SHAPES = dict(ropeC=([128, S], F32), ropeS=([128, S], F32), ropeCi=([64, S], F32), ropeSi=([64, S], F32),
                    maskC=([128, 128], BF16), maskW=([128, 128], BF16), maskCmp=([128, S], BF16),
                    ovl=([128, 33], BF16), selMul=([128, 16, 32], F32), selAdd=([128, 16, 32], F32),
                    expand=([32, 16, 128], BF16), negC=([128, 128], F32))


def build(depth=DEPTH, dbg=()):
    nc = bass.Bass("TRN2", target_bir_lowering=False)
    ctx = ExitStack()
    P = Prog()
    dt_in = {}
    x_d = nc.dram_tensor("x", [S, D], F32, kind="ExternalInput").ap()
    p_d = nc.dram_tensor("p", [DEPTH, S, PLE], F32, kind="ExternalInput").ap()
    wsrc = {n: nc.dram_tensor(n, [DEPTH] + list(s), F32, kind="ExternalInput").ap() for n, s in WSHAPES.items()}
    small = {n: nc.dram_tensor(n, [DEPTH] + list(s), F32, kind="ExternalInput").ap() for n, s in SMALL.items()}
    cst = {n: nc.dram_tensor("c_" + n, s, d, kind="ExternalInput").ap() for n, (s, d) in CONST_SHAPES.items()}
    y_d = nc.dram_tensor("y", [S, D], F32, kind="ExternalOutput").ap()
    dbg_d = {}
    for name, shape, dty in dbg:
        dbg_d[name] = nc.dram_tensor("dbg_" + name, list(shape), dty, kind="ExternalOutput").ap()
    wb = {n: nc.dram_tensor("wb_" + n, [depth] + list(s), BF16, kind="Internal").ap() for n, s in WSHAPES.items()}
    h_dram = nc.dram_tensor("h_dram", [S, D], F32, kind="Internal").ap()
    r_dram = nc.dram_tensor("r_dram", [S, D], F32, kind="Internal").ap()
    oT_dram = nc.dram_tensor("oT_dram", [D, S], BF16, kind="Internal").ap()
    mixT_dram = nc.dram_tensor("mixT_dram", [D, S], BF16, kind="Internal").ap()

    ARENA_WORDS = 32 * 1024
    hT = ctx.enter_context(nc.sbuf_tensor("hT", [128, 16, S], BF16))
    arena_t = ctx.enter_context(nc.sbuf_tensor("arena", [128, ARENA_WORDS], F32))
    A = Arena(arena_t, ARENA_WORDS)
    ident = ctx.enter_context(nc.sbuf_tensor("ident", [128, 128], BF16))
    identf = ctx.enter_context(nc.sbuf_tensor("identf", [128, 128], F32))
    maskC = ctx.enter_context(nc.sbuf_tensor("maskC", [128, 128], BF16))
    maskW = ctx.enter_context(nc.sbuf_tensor("maskW", [128, 128], BF16))
    maskCmp = ctx.enter_context(nc.sbuf_tensor("maskCmp", [128, S], BF16))
    ovl = ctx.enter_context(nc.sbuf_tensor("ovl", [128, 33], BF16))
    selMul = ctx.enter_context(nc.sbuf_tensor("selMul", [128, 16, 32], F32))
    selAdd = ctx.enter_context(nc.sbuf_tensor("selAdd", [128, 16, 32], F32))
    expand = ctx.enter_context(nc.sbuf_tensor("expand", [32, 16, 128], BF16))
    negC = ctx.enter_context(nc.sbuf_tensor("negC", [128, 128], F32))
    onesb = ctx.enter_context(nc.sbuf_tensor("onesb", [128, 16], BF16))
    ps = [ctx.enter_context(nc.psum_tensor("ps%d" % i, [128, 512], F32)) for i in range(8)]

    def PS(i):
        return ("ps", i)

    cnt = {"ev": 0, "ps": 0, "w": 0}

    def mm(out, lhsT, rhs, start, stop, reads, pskey):
        P.add("pe", lambda e: e.matmul(out, lhsT, rhs, start=start, stop=stop), reads=reads, writes=[pskey])

    def tr(out, in_, idn, reads, pskey):
        P.add("pe", lambda e: e.transpose(out=out, in_=in_, identity=idn), reads=reads, writes=[pskey])

    def act(out, in_, func, reads, writes, bias=0.0, scale=1.0, accum=None):
        if accum is None:
            P.add("act", lambda e: e.activation(out=out, in_=in_, func=func, bias=bias, scale=scale), reads=reads, writes=writes)
        else:
            P.add("act", lambda e: e.activation(out=out, in_=in_, func=func, bias=bias, scale=scale, accum_out=accum), reads=reads, writes=writes)

    def tt(out, in0, in1, op, reads, writes):
        P.add("dve", lambda e: e.tensor_tensor(out=out, in0=in0, in1=in1, op=op), reads=reads, writes=writes)

    def ts(out, in0, s1, s2, op0, op1, reads, writes):
        if s2 is None:
            P.add("dve", lambda e: e.tensor_scalar(out=out, in0=in0, scalar1=s1, scalar2=None, op0=op0), reads=reads, writes=writes)
        else:
            P.add("dve", lambda e: e.tensor_scalar(out=out, in0=in0, scalar1=s1, scalar2=s2, op0=op0, op1=op1), reads=reads, writes=writes)

    def stt(out, in0, scalar, in1, op0, op1, reads, writes):
        P.add("dve", lambda e: e.scalar_tensor_tensor(out=out, in0=in0, scalar=scalar, in1=in1, op0=op0, op1=op1), reads=reads, writes=writes)

    def vcopy(out, in_, reads, writes):
        P.add("dve", lambda e: e.tensor_copy(out=out, in_=in_), reads=reads, writes=writes)

    def acopy(out, in_, reads, writes):
        act(out, in_, AF.Copy, reads, writes)

    def evac(out, in_, reads, writes):
        cnt["ev"] += 1
        if cnt["ev"] % 2:
            acopy(out, in_, reads, writes)
        else:
            vcopy(out, in_, reads, writes)

    def memset(ap, val, writes, eng="dve"):
        P.add(eng, lambda e: e.memset(ap, val), writes=writes)

    def nextps(lo=0, n=4):
        cnt["ps"] += 1
        return lo + cnt["ps"] % n

    def hkeys(t0, n):
        return [("hT", t) for t in range(t0, t0 + n)]

    def wkeys(name, l):
        return [("wb", name, l, c) for c in range(nchunks[name])]

    for t_, n_ in ((maskC, "maskC"), (maskW, "maskW"), (maskCmp, "maskCmp"), (ovl, "ovl"), (selMul, "selMul"),
                   (selAdd, "selAdd"), (expand, "expand"), (negC, "negC")):
        P.dma("sp", t_[:], cst[n_], writes=[n_])
    memset(ident[:], 1.0, ["ident"], eng="pool")
    P.add("pool", lambda e: e.affine_select(out=ident[:], in_=ident[:], pattern=[[-1, 128]], compare_op=ALU.is_equal,
                                            fill=0.0, base=0, channel_multiplier=1), reads=["ident"], writes=["ident"])
    memset(identf[:], 1.0, ["identf"], eng="pool")
    P.add("pool", lambda e: e.affine_select(out=identf[:], in_=identf[:], pattern=[[-1, 128]], compare_op=ALU.is_equal,
                                            fill=0.0, base=0, channel_multiplier=1), reads=["identf"], writes=["identf"])
    memset(onesb[:], 1.0, ["onesb"], eng="pool")

    CW = 4096
    nchunks = {}
    for n, (R_, C_) in WSHAPES.items():
        m_ = R_ * C_ // 128
        nchunks[n] = (m_ + CW - 1) // CW
    cast_q = []

    def cast_layer(l):
        for n in WORDER:
            R_, C_ = WSHAPES[n]
            m_ = R_ * C_ // 128
            src = wsrc[n][l].rearrange("(p a) c -> p (a c)", p=128)
            dst = wb[n][l].rearrange("(p a) c -> p (a c)", p=128)
            for c in range(nchunks[n]):
                c0 = c * CW
                c1 = min(m_, c0 + CW)
                cast_q.append((dst[:, c0:c1], src[:, c0:c1], ("wb", n, l, c)))

    def pump_cast(k, dep_keys=()):
        for _ in range(min(k, len(cast_q))):
            d_, s_, key = cast_q.pop(0)
            P.dma("pool", d_, s_, reads=list(dep_keys), writes=[key], nobar=True)

    wslots = {}

    def new_wslots(nbytes, n=2):
        wslots["v"] = [A.bf([128, nbytes // 2]) for _ in range(n)]
        wslots["i"] = 0

    def wslot():
        i = wslots["i"] % len(wslots["v"])
        wslots["i"] += 1
        return wslots["v"][i], ("wslot", i)

    def load_w_cols(src2d, c0, ncols, nk, keys, dst=None, dkey=None, coff=0):
        if dst is None:
            buf, dkey = wslot()
            dst = buf[:, 0:nk * ncols].rearrange("p (k c) -> p k c", c=ncols)
            P.dma("sp", dst, src2d[:, c0:c0 + ncols].rearrange("(k p) c -> p k c", p=128), reads=keys, writes=[dkey])
        else:
            P.dma("sp", dst[:, :, coff:coff + ncols], src2d[:, c0:c0 + ncols].rearrange("(k p) c -> p k c", p=128),
                  reads=keys, writes=[dkey])
        return dst, dkey

    def rope128(psap, pk, dst, dkey, tb, tabs, n=512, pos=None):
        C_, S_, tk = tabs
        if pos is None:
            cs = C_[:, tb * 512: tb * 512 + n]
            ss_lo = S_[0:64, tb * 512: tb * 512 + n]
            ss_hi = S_[64:128, tb * 512: tb * 512 + n]
        else:
            cs, ss_lo, ss_hi = pos
        ta = ropetmp["a"][cnt["ev"] % 2]
        tbm = ropetmp["b"][cnt["ev"] % 2]
        ka = ("ropeA", cnt["ev"] % 2)
        kb = ("ropeB", cnt["ev"] % 2)
        cnt["ev"] += 1
        tt(ta[:, 0:n], psap, cs, ALU.mult, [pk, tk], [ka])
        tt(tbm[0:64, 0:n], psap[64:128, :], ss_lo, ALU.mult, [pk, tk], [kb])
        tt(tbm[64:128, 0:n], psap[0:64, :], ss_hi, ALU.mult, [pk, tk], [kb])
        tt(dst, ta[:, 0:n], tbm[:, 0:n], ALU.add, [ka, kb], [dkey])

    def rope64(psap, pk, dsts, dkey, tb, tabs, n=512):
        C_, S_, tk = tabs
        ta = ropetmp["a"][cnt["ev"] % 2]
        tbm = ropetmp["b"][cnt["ev"] % 2]
        ka = ("ropeA", cnt["ev"] % 2)
        kb = ("ropeB", cnt["ev"] % 2)
        cnt["ev"] += 1
        sl = slice(tb * 512, tb * 512 + n)
        tt(ta[0:64, 0:n], psap, C_[0:64, sl], ALU.mult, [pk, tk], [ka])
        tt(tbm[0:32, 0:n], psap[32:64, :], S_[0:32, sl], ALU.mult, [pk, tk], [kb])
        tt(tbm[32:64, 0:n], psap[0:32, :], S_[32:64, sl], ALU.mult, [pk, tk], [kb])
        for dst in dsts:
            tt(dst, ta[0:64, 0:n], tbm[0:64, 0:n], ALU.add, [ka, kb], [dkey])

    ropetmp = {}

    def alloc_ropetmp():
        ropetmp["a"] = [A.f32([128, 512]) for _ in range(2)]
        ropetmp["b"] = [A.f32([128, 512]) for _ in range(2)]

    def proj_F(wt, wk, c0, M, epi, actT=None, akey="hT", nk=16, base_ps=0):
        src = hT if actT is None else actT
        for tb in range(4):
            b = nextps()
            for kc in range(nk):
                mm(ps[b][0:M, :], wt[:, kc, c0:c0 + M], src[:, kc, tb * 512:(tb + 1) * 512], kc == 0, kc == nk - 1,
                   [wk] + (hkeys(tb * 4, 4) if akey == "hT" else [akey]), PS(b))
            epi(ps[b][0:M, :], PS(b), tb)

    def proj_T(wt, wk, c0, N, epi, actT=None, akey="hT", nk=16, group=None):
        src = hT if actT is None else actT
        if group is None:
            group = max(1, 512 // N)
        for t0 in range(0, NT, group):
            b = nextps()
            g = min(group, NT - t0)
            for j in range(g):
                tt_ = t0 + j
                for kc in range(nk):
                    mm(ps[b][:, j * N:(j + 1) * N], src[:, kc, tt_ * 128:(tt_ + 1) * 128], wt[:, kc, c0:c0 + N],
                       kc == 0, kc == nk - 1, [wk] + (hkeys(tt_, 1) if akey == "hT" else [akey]), PS(b))
            epi(ps[b][:, 0:g * N], PS(b), t0, g)

    def attn_qtile(heads, qT_of, ktiles, ncol, pT, skey_banks, obanks, exp_bias=None):
        G = len(heads)
        ob = obanks
        nkt = len(ktiles)
        for ki, kt in enumerate(ktiles):
            sb = skey_banks[ki % 2]
            ns = kt["ns"]
            kT, kkey = kt["kT"]
            for hi, h in enumerate(heads):
                q, qk = qT_of(h)
                nm = len(kt["masks"])
                mm(ps[sb][0:ns, hi * 128:(hi + 1) * 128], kT, q, hi == 0, nm == 0 and hi == G - 1, [kkey, qk], PS(sb))
                for mi, (ml, mr, mk) in enumerate(kt["masks"]):
                    mm(ps[sb][0:ns, hi * 128:(hi + 1) * 128], ml, mr, False, mi == nm - 1 and hi == G - 1, mk, PS(sb))
            pt, ptk = pT[ki % 2]
            if exp_bias is None:
                act(pt[0:ns, 0:G * 128], ps[sb][0:ns, 0:G * 128], AF.Exp, [PS(sb)], [ptk], scale=SC)
            else:
                for hi, h in enumerate(heads):
                    bap, bk = exp_bias(h, kt)
                    act(pt[0:ns, hi * 128:(hi + 1) * 128], ps[sb][0:ns, hi * 128:(hi + 1) * 128], AF.Exp,
                        [PS(sb), bk], [ptk], scale=SC, bias=bap)
            V, vk = kt["V"]
            for hi, h in enumerate(heads):
                mm(ps[ob][:, hi * ncol:(hi + 1) * ncol], pt[0:ns, hi * 128:(hi + 1) * 128], V, ki == 0 and hi == 0,
                   ki == nkt - 1 and hi == G - 1, [ptk] + list(vk), PS(ob))
        return [(ps[ob][:, hi * ncol:(hi + 1) * ncol], PS(ob)) for hi in range(G)]

    def transposes_to(dstT, dkey, src_tok, skey, nblk, tbank):
        pb = ps[tbank][:].bitcast(BF16)
        for j in range(nblk):
            tr(pb[:, j * 128:(j + 1) * 128], src_tok[:, j * 128:(j + 1) * 128], ident[:], [skey, "ident"], PS(tbank))
        for j in range(nblk):
            evac(dstT(j), pb[:, j * 128:(j + 1) * 128], [PS(tbank)], [dkey])

    def load_tables128():
        C_ = A.f32([128, S])
        S_ = A.f32([128, S])
        P.dma("sp", C_, cst["ropeC"], writes=["tabC"])
        P.dma("sp", S_, cst["ropeS"], writes=["tabC"])
        return (C_, S_, "tabC")

    def load_tables64():
        C_ = A.f32([128, S])
        S_ = A.f32([128, S])
        P.dma("sp", C_[0:64, :], cst["ropeCi"], writes=["tabI"])
        P.dma("sp", S_[0:64, :], cst["ropeSi"], writes=["tabI"])
        return (C_, S_, "tabI")

    def phase_fox(l):
        A.release(0)
        win = wb["w_in"][l]
        wk_in = wkeys("w_in", l)
        new_wslots(16 * 384 * 2)
        wf = A.bf([128, 16, 8])
        lf = A.f32([128, S])
        cp = A.f32([128, S])
        onesr = A.f32([128, S])
        negb = A.f32([128, 2])
        ctok = A.f32([128, 96])
        bsh = A.f32([128, 96])
        rbd = A.f32([128, 96])
        bias = A.f32([128, 6, 16, 16])
        qTh = A.bf([128, S])
        kTh = A.bf([128, S])
        vh = A.bf([128, 16, 130])
        oTh = [A.bf([128, S]) for _ in range(2)]
        pts = [(A.bf([128, 128]), ("pT", i)) for i in range(2)]
        otok = [A.bf([128, 128]) for _ in range(2)]
        rec = A.f32([128, 2])
        memset(onesr[0:8, :], 1.0, ["onesr"])
        memset(vh[:, :, 128:129], 1.0, ["vh1"])
        P.dma("sp", wf[:, :, 0:6], win[:, OFF["fox_f"]:OFF["fox_f"] + 6].rearrange("(k p) c -> p k c", p=128),
              reads=wk_in, writes=["wf"])
        P.dma("sp", negb[0:6, 0:1], small["fox_f_bias"][l:l + 1, :].rearrange("o h -> h o"), writes=["negb"])
        ts(negb[0:6, 1:2], negb[0:6, 0:1], -1.0, None, ALU.mult, None, ["negb"], ["negb2"])
        for tb in range(4):
            b = nextps()
            for kc in range(16):
                mm(ps[b][0:6, :], wf[:, kc, 0:6], hT[:, kc, tb * 512:(tb + 1) * 512], kc == 0, kc == 15, ["wf"] + hkeys(tb * 4, 4), PS(b))
            sl = slice(tb * 512, (tb + 1) * 512)
            act(lf[0:6, sl], ps[b][0:6, :], AF.Exp, [PS(b), "negb2"], [("lf", tb)], bias=negb[0:6, 1:2], scale=-1.0)
            act(lf[0:6, sl], lf[0:6, sl], AF.Ln, [("lf", tb)], [("lf", tb)], bias=1.0, scale=1.0)
        P.add("dve", lambda e: e.tensor_tensor_scan(out=cp[0:6, :], data0=onesr[0:6, :], data1=lf[0:6, :], initial=0.0,
                                                    op0=ALU.mult, op1=ALU.add),
              reads=[("lf", t) for t in range(4)] + ["onesr"], writes=["cp"])
        b = nextps()
        for kt in range(16):
            tr(ps[b][:, kt * 6:(kt + 1) * 6], cp[0:6, kt * 128:(kt + 1) * 128], identf[0:6, 0:6], ["cp", "identf"], PS(b))
        vcopy(ctok[:, :], ps[b][:, 0:96], [PS(b)], ["ctok"])
        for h in range(6):
            ts(rbd[0:6, h * 16:(h + 1) * 16], cp[0:6, 127:S:128], identf[0:6, h:h + 1], None, ALU.mult, None,
               ["cp", "identf"], ["rbd"])
        b = nextps()
        mm(ps[b][:, 0:96], onesr[0:6, 0:128], rbd[0:6, 0:96], True, True, ["onesr", "rbd"], PS(b))
        vcopy(bsh[:, :], ps[b][:, 0:96], [PS(b)], ["bsh"])
        ctv = ctok[:, :].rearrange("p (k h) -> p h k", h=6)
        for h in range(6):
            for qt in range(16):
                ts(bias[:, h, qt, :], ctv[:, h, :], bsh[:, h * 16 + qt:h * 16 + qt + 1], None, ALU.subtract, None,
                   ["ctok", "bsh"], [("bias", h)])
        for h in range(6):
            wt, wk = wslot()
            wt = wt[:, 0:16 * 384].rearrange("p (k c) -> p k c", c=384)
            for j, seg in enumerate(("fox_q", "fox_k", "fox_v")):
                c0 = OFF[seg] + h * 128
                P.dma("sp", wt[:, :, j * 128:(j + 1) * 128], win[:, c0:c0 + 128].rearrange("(k p) c -> p k c", p=128),
                      reads=wk_in, writes=[wk])
            proj_F(wt, wk, 0, 128, lambda pa, pk, tb: evac(qTh[:, tb * 512:(tb + 1) * 512], pa, [pk], ["qTh"]))
            proj_F(wt, wk, 128, 128, lambda pa, pk, tb: evac(kTh[:, tb * 512:(tb + 1) * 512], pa, [pk], ["kTh"]))
            proj_T(wt, wk, 256, 128, lambda pa, pk, t0, g: evac(vh[:, t0:t0 + g, 0:128], pa.rearrange("p (g c) -> p g c", c=128), [pk], ["vh"]))
            oT = oTh[h % 2]
            ok_ = ("oTh", h % 2)
            for qt in range(16):
                kts = []
                for kt in range(qt + 1):
                    masks = []
                    if kt == qt:
                        masks.append((ident[:], maskC[:], ["ident", "maskC"]))
                    kts.append(dict(kT=(kTh[:, kt * 128:(kt + 1) * 128], "kTh"), ns=128,
                                    V=(vh[:, kt, 0:129], ["vh", "vh1"]), masks=masks, kt=kt))
                res = attn_qtile([h], lambda hh: (qTh[:, qt * 128:(qt + 1) * 128], "qTh"), kts, 129, pts, (4, 5),
                                 2 + qt % 2, exp_bias=lambda hh, k, qt=qt: (bias[:, hh, qt, k["kt"]:k["kt"] + 1], ("bias", hh)))
                oap, opk = res[0]
                ri = qt % 2
                P.add("dve", lambda e, ri=ri, oap=oap: e.reciprocal(out=rec[:, ri:ri + 1], in_=oap[:, 128:129]),
                      reads=[opk, "vh1"], writes=[("rec", ri)])
                ts(otok[ri][:, :], oap[:, 0:128], rec[:, ri:ri + 1], None, ALU.mult, None, [opk, ("rec", ri)], [("otok", ri)])
                transposes_to(lambda j, qt=qt, oT=oT: oT[:, qt * 128:(qt + 1) * 128], ok_, otok[ri], ("otok", ri), 1, 6 + qt % 2)
            P.dma("sp", oT_dram[h * 128:(h + 1) * 128, :], oT[:, :], reads=[ok_], writes=[("dram", "oT", h)])

    def phase_nsa(l):
        P.barrier()
        A.release(0)
        win = wb["w_in"][l]
        wk_in = wkeys("w_in", l)
        qTn = A.bf([128, 4, S])
        ksT = A.bf([128, S])
        kwT = A.bf([128, S])
        vs = A.bf([128, 16, 130])
        vw = A.bf([128, 16, 130])
        kcmpT = A.bf([128, 128])
        vcx = A.bf([128, 162])
        gsig = A.f32([128, 16, 12])
        oTn = A.bf([128, 4, S])
        m_stage = A.mark()
        tabs = load_tables128()
        alloc_ropetmp()
        new_wslots(16 * 512 * 2)
        rawT = [A.bf([128, S]) for _ in range(2)]
        peT = A.bf([128, 2, 32])
        pe32 = A.f32([32, 2, 128])
        w2 = A.bf([128, 2, 2, 128])
        h1x = A.f32([128, 256])
        h1a = A.f32([128, 256])
        gT = A.bf([128, 2, 2, 128])
        memset(vs[:, :, 128:129], 1.0, ["vs1"])
        memset(vw[:, :, 128:129], 1.0, ["vw1"])
        vcopy(vcx[:, 128:161], ovl[:, :], ["ovl"], ["vcx1"])
        wt, wk = load_w_cols(win, OFF["nsa_g"], 12, 16, wk_in)
        proj_T(wt, wk, 0, 12, lambda pa, pk, t0, g: act(gsig[:, t0:t0 + g, :], pa.rearrange("p (g c) -> p g c", c=12), AF.Sigmoid, [pk], ["gsig"]), group=16)
        wt, wk = load_w_cols(win, OFF["nsa_q"], 512, 16, wk_in)
        for h in range(4):
            proj_F(wt, wk, h * 128, 128, lambda pa, pk, tb, h=h: rope128(pa, pk, qTn[:, h, tb * 512:(tb + 1) * 512], "qTn", tb, tabs))
        wt, wk = load_w_cols(win, OFF["nsa_kc"], 512, 16, wk_in)
        proj_F(wt, wk, 0, 128, lambda pa, pk, tb: evac(rawT[0][:, tb * 512:(tb + 1) * 512], pa, [pk], [("rawT", 0)]))
        proj_F(wt, wk, 128, 128, lambda pa, pk, tb: evac(rawT[1][:, tb * 512:(tb + 1) * 512], pa, [pk], [("rawT", 1)]))
        proj_F(wt, wk, 256, 128, lambda pa, pk, tb: rope128(pa, pk, ksT[:, tb * 512:(tb + 1) * 512], "ksT", tb, tabs))
        proj_T(wt, wk, 384, 128, lambda pa, pk, t0, g: evac(vs[:, t0:t0 + g, 0:128], pa.rearrange("p (g c) -> p g c", c=128), [pk], ["vs"]))
        wt, wk = load_w_cols(win, OFF["nsa_kw"], 256, 16, wk_in)
        proj_F(wt, wk, 0, 128, lambda pa, pk, tb: rope128(pa, pk, kwT[:, tb * 512:(tb + 1) * 512], "kwT", tb, tabs))
        proj_T(wt, wk, 128, 128, lambda pa, pk, t0, g: evac(vw[:, t0:t0 + g, 0:128], pa.rearrange("p (g c) -> p g c", c=128), [pk], ["vw"]))
        for which, (pen, w1n, w2n) in enumerate((("nsa_pe_k", "nsa_cmp_k1", "nsa_cmp_k2"), ("nsa_pe_v", "nsa_cmp_v1", "nsa_cmp_v2"))):
            P.dma("sp", pe32[0:32, which, :], small[pen][l], writes=[("pe32", which)])
            b = nextps()
            tr(ps[b][:, 0:32], pe32[0:32, which, :], identf[0:32, 0:32], [("pe32", which), "identf"], PS(b))
            vcopy(peT[:, which, :], ps[b][:, 0:32], [PS(b)], [("peT", which)])
            P.dma("sp", w2[:, which, :, :], wb[w2n][l].rearrange("(k p) c -> p k c", p=128), reads=wkeys(w2n, l), writes=[("w2", which)])
            buf, wk1 = wslot()
            w1 = buf[:, 0:32 * 256].rearrange("p (l c) -> p l c", c=256)
            P.dma("sp", w1, wb[w1n][l].rearrange("(l p) c -> p l c", p=128), reads=wkeys(w1n, l), writes=[wk1])
            b = nextps()
            for j in range(2):
                for li in range(32):
                    mm(ps[b][:, j * 128:j * 128 + 127], w1[:, li, j * 128:(j + 1) * 128], rawT[which][:, li:li + 16 * 126 + 1:16],
                       li == 0, False, [wk1, ("rawT", which)], PS(b))
                    mm(ps[b][:, j * 128:j * 128 + 127], w1[:, li, j * 128:(j + 1) * 128],
                       peT[:, which, li:li + 1].to_broadcast([128, 127]), False, li == 31, [wk1, ("peT", which)], PS(b))
            hx = h1x[:, :].rearrange("p (j c) -> p j c", c=128)[:, :, 0:127]
            ha = h1a[:, :].rearrange("p (j c) -> p j c", c=128)[:, :, 0:127]
            pv = ps[b][:, 0:256].rearrange("p (j c) -> p j c", c=128)[:, :, 0:127]
            acopy(hx, pv, [PS(b)], ["h1x"])
            tt(ha, hx, hx, ALU.mult, ["h1x"], ["h1a"])
            ts(ha, ha, 0.044715, 1.0, ALU.mult, ALU.add, ["h1a"], ["h1a"])
            tt(ha, ha, hx, ALU.mult, ["h1a", "h1x"], ["h1a"])
            act(ha, ha, AF.Tanh, ["h1a"], ["h1a"], scale=0.7978845608028654)
            stt(ha, ha, 1.0, hx, ALU.add, ALU.mult, ["h1a", "h1x"], ["h1a"])
            ts(gT[:, which, :, 0:127], ha, 0.5, None, ALU.mult, None, ["h1a"], [("gT", which)])
            b = nextps()
            if which == 0:
                for j in range(2):
                    mm(ps[b][:, 0:127], w2[:, 0, j, :], gT[:, 0, j, 0:127], j == 0, j == 1, [("w2", 0), ("gT", 0)], PS(b))
                C_, S_, tk = tabs
                rope128(ps[b][:, 0:127], PS(b), kcmpT[:, 0:127], "kcmpT", 0, tabs, n=127,
                        pos=(C_[:, 31:S:16], S_[0:64, 31:S:16], S_[64:128, 31:S:16]))
            else:
                for j in range(2):
                    mm(ps[b][0:127, 0:128], gT[:, 1, j, 0:127], w2[:, 1, j, :], j == 0, j == 1, [("w2", 1), ("gT", 1)], PS(b))
                evac(vcx[0:127, 0:128], ps[b][0:127, 0:128], [PS(b)], ["vcx"])
        P.barrier()
        A.release(m_stage)
        pts = [(A.bf([128, 512]), ("pT", i)) for i in range(2)]
        oacc = A.f32([128, 4, 128])
        onb = A.bf([128, 4, 128])
        rec = A.f32([128, 4])
        coef = A.f32([128, 4])
        imp = A.f32([128, 32])
        imp2 = A.f32([128, 32])
        m8 = A.f32([128, 16])
        negsel = A.bf([128, 32])
        negselT = A.bf([32, 128])
        vkeys = ["vcx", "vcx1"]
        for qt in range(16):
            qsl = slice(qt * 128, (qt + 1) * 128)

            def qof(h, qsl=qsl):
                return (qTn[:, h, qsl], "qTn")

            gq = gsig[:, qt, :].rearrange("p (h c) -> p c h", c=3)
            kts = [dict(kT=(kcmpT[:, 0:127], "kcmpT"), ns=127, V=(vcx[0:127, 0:161], ["vcx", "vcx1"]),
                        masks=[(ident[0:127, 0:127], maskCmp[0:127, qsl], ["ident", "maskCmp"])])]
            for gi, hs in enumerate(((0, 1, 2), (3,))):
                res = attn_qtile(list(hs), qof, kts, 161, pts, (4, 5), 2 + gi)
                for hi, h in enumerate(hs):
                    oap, opk = res[hi]
                    ts(rec[:, h:h + 1], oap[:, 128:129], 1e-30, None, ALU.max, None, [opk, "vcx1"], [("rec", h)])
                    P.add("dve", lambda e, h=h: e.reciprocal(out=rec[:, h:h + 1], in_=rec[:, h:h + 1]), reads=[("rec", h)], writes=[("rec", h)])
                    tt(coef[:, h:h + 1], rec[:, h:h + 1], gq[:, 0, h:h + 1], ALU.mult, [("rec", h), "gsig"], [("coef", h)])
                    ts(oacc[:, h, :], oap[:, 0:128], coef[:, h:h + 1], None, ALU.mult, None, [opk, ("coef", h)], [("oacc", h)])
                    if h == 0:
                        ts(imp[:, :], oap[:, 129:161], rec[:, h:h + 1], None, ALU.mult, None, [opk, ("rec", h)], ["imp"])
                    else:
                        stt(imp[:, :], oap[:, 129:161], rec[:, h:h + 1], imp[:, :], ALU.mult, ALU.add, [opk, ("rec", h), "imp"], ["imp"])
            tt(imp2[:, :], imp[:, :], selMul[:, qt, :], ALU.mult, ["imp", "selMul"], ["imp2"])
            tt(imp2[:, :], imp2[:, :], selAdd[:, qt, :], ALU.add, ["imp2", "selAdd"], ["imp2"])
            P.add("dve", lambda e: e.max(out=m8[:, 0:8], in_=imp2[:, :]), reads=["imp2"], writes=["m8"])
            P.add("dve", lambda e: e.match_replace(out=imp[:, :], in_to_replace=m8[:, 0:8], in_values=imp2[:, :], imm_value=-1e30),
                  reads=["imp2", "m8"], writes=["imp"])
            P.add("dve", lambda e: e.max(out=m8[:, 8:16], in_=imp[:, :]), reads=["imp"], writes=["m8b"])
            ts(negsel[:, :], imp2[:, :], m8[:, 15:16], NEG, ALU.is_lt, ALU.mult, ["imp2", "m8b"], ["negsel"])
            pb = ps[6][:].bitcast(BF16)
            tr(pb[0:32, 0:128], negsel[:, :], ident[:], ["negsel", "ident"], PS(6))
            vcopy(negselT[0:32, :], pb[0:32, 0:128], [PS(6)], ["negselT"])
            for br in range(2):
                kts = []
                if br == 0:
                    for kt in range(qt + 1):
                        masks = [(expand[:, kt, :], negselT[0:32, :], ["expand", "negselT"])]
                        if kt == qt:
                            masks.append((ident[:], maskC[:], ["ident", "maskC"]))
                        kts.append(dict(kT=(ksT[:, kt * 128:(kt + 1) * 128], "ksT"), ns=128, V=(vs[:, kt, 0:129], ["vs", "vs1"]), masks=masks))
                else:
                    for kt in range(max(0, qt - 4), qt + 1):
                        masks = []
                        if kt == qt:
                            masks.append((ident[:], maskC[:], ["ident", "maskC"]))
                        if kt == qt - 4:
                            masks.append((ident[:], maskW[:], ["ident", "maskW"]))
                        kts.append(dict(kT=(kwT[:, kt * 128:(kt + 1) * 128], "kwT"), ns=128, V=(vw[:, kt, 0:129], ["vw", "vw1"]), masks=masks))
                for gi, hs in enumerate(((0, 1, 2), (3,))):
                    res = attn_qtile(list(hs), qof, kts, 129, pts, (4, 5), (0, 1, 2, 3)[(br * 2 + gi) % 4])
                    for hi, h in enumerate(hs):
                        oap, opk = res[hi]
                        P.add("dve", lambda e, h=h, oap=oap: e.reciprocal(out=rec[:, h:h + 1], in_=oap[:, 128:129]),
                              reads=[opk, "vs1", "vw1"], writes=[("rec", h)])
                        tt(coef[:, h:h + 1], rec[:, h:h + 1], gq[:, 1 + br, h:h + 1], ALU.mult, [("rec", h), "gsig"], [("coef", h)])
                        if br == 0:
                            stt(oacc[:, h, :], oap[:, 0:128], coef[:, h:h + 1], oacc[:, h, :], ALU.mult, ALU.add,
                                [opk, ("coef", h), ("oacc", h)], [("oacc", h)])
                        else:
                            stt(onb[:, h, :], oap[:, 0:128], coef[:, h:h + 1], oacc[:, h, :], ALU.mult, ALU.add,
                                [opk, ("coef", h), ("oacc", h)], [("onb", h)])
            pb7 = ps[7][:].bitcast(BF16)
            for h in range(4):
                tr(pb7[:, h * 128:(h + 1) * 128], onb[:, h, :], ident[:], [("onb", h), "ident"], PS(7))
            evac(oTn[:, :, qsl], pb7[:, 0:512].rearrange("p (h c) -> p h c", c=128), [PS(7)], ["oTn"])
        for h in range(4):
            P.dma("sp", oT_dram[(6 + h) * 128:(7 + h) * 128, :], oTn[:, h, :], reads=["oTn"], writes=[("dram", "oT", 6 + h)])

    def phase_dsa(l):
        P.barrier()
        A.release(0)
        win = wb["w_in"][l]
        wk_in = wkeys("w_in", l)
        qTd = A.bf([128, 6, S])
        kdT = A.bf([128, S])
        vd = A.bf([128, 16, 130])
        ikT = A.bf([128, S])
        iw = A.f32([128, 16, 16])
        m0 = A.mark()
        tabs = load_tables128()
        alloc_ropetmp()
        new_wslots(16 * 512 * 2)
        ckvnT = A.bf([128, 2, S])
        ckvn = [A.bf([128, 256]) for _ in range(2)]
        gB = A.f32([128, 256])
        ssq = A.f32([128, 4])
        junk = A.f32([128, 256])
        kvup = A.bf([128, 2, 256])
        memset(vd[:, :, 128:129], 1.0, ["vd1"])
        P.dma("sp", gB[:, :], small["dsa_kv_norm"][l:l + 1, :].to_broadcast([128, 256]), writes=["gB"])
        P.dma("sp", kvup[:, :, :], wb["dsa_kv_up"][l].rearrange("(k p) c -> p k c", p=128), reads=wkeys("dsa_kv_up", l), writes=["kvup"])
        wt, wk = load_w_cols(win, OFF["dsa_q"], 512, 16, wk_in)
        for h in range(4):
            proj_F(wt, wk, h * 128, 128, lambda pa, pk, tb, h=h: rope128(pa, pk, qTd[:, h, tb * 512:(tb + 1) * 512], "qTd", tb, tabs))
        wt, wk = load_w_cols(win, OFF["dsa_q"] + 512, 512, 16, wk_in)
        for h in range(2):
            proj_F(wt, wk, h * 128, 128, lambda pa, pk, tb, h=h: rope128(pa, pk, qTd[:, 4 + h, tb * 512:(tb + 1) * 512], "qTd", tb, tabs))

        def ckv_epi(pa, pk, t0, g):
            for j in range(g):
                t_ = t0 + j
                i2 = t_ % 2
                pj = pa[:, j * 256:(j + 1) * 256]
                act(junk[:, :], pj, AF.Square, [pk], ["junk", ("ssq", i2)], accum=ssq[:, i2:i2 + 1])
                ts(ssq[:, i2:i2 + 1], ssq[:, i2:i2 + 1], 1.0 / 256.0, 1e-6, ALU.mult, ALU.add, [("ssq", i2)], [("ssq", i2)])
                act(ssq[:, i2:i2 + 1], ssq[:, i2:i2 + 1], AF.Sqrt, [("ssq", i2)], [("ssq", i2)])
                P.add("dve", lambda e, i2=i2: e.reciprocal(out=ssq[:, i2:i2 + 1], in_=ssq[:, i2:i2 + 1]), reads=[("ssq", i2)], writes=[("ssq", i2)])
                stt(ckvn[i2][:, :], pj, ssq[:, i2:i2 + 1], gB[:, :], ALU.mult, ALU.mult, [pk, ("ssq", i2), "gB"], [("ckvn", i2)])
                pb = ps[6 + i2][:].bitcast(BF16)
                for c in range(2):
                    tr(pb[:, c * 128:(c + 1) * 128], ckvn[i2][:, c * 128:(c + 1) * 128], ident[:], [("ckvn", i2), "ident"], PS(6 + i2))
                evac(ckvnT[:, :, t_ * 128:(t_ + 1) * 128], pb[:, 0:256].rearrange("p (c t) -> p c t", t=128), [PS(6 + i2)], ["ckvnT"])

        proj_T(wt, wk, 256, 256, ckv_epi, group=2)
        proj_F(kvup, "kvup", 0, 128, lambda pa, pk, tb: rope128(pa, pk, kdT[:, tb * 512:(tb + 1) * 512], "kdT", tb, tabs),
               actT=ckvnT, akey="ckvnT", nk=2)
        proj_T(kvup, "kvup", 128, 128, lambda pa, pk, t0, g: evac(vd[:, t0:t0 + g, 0:128], pa.rearrange("p (g c) -> p g c", c=128), [pk], ["vd"]),
               actT=ckvnT, akey="ckvnT", nk=2)
        P.barrier()
        A.release(m0)
        iqT = A.bf([128, 8, S])
        m0 = A.mark()
        tabsi = load_tables64()
        alloc_ropetmp()
        new_wslots(16 * 512 * 2)
        wt, wk = load_w_cols(win, OFF["idx_k"], 80, 16, wk_in)
        proj_F(wt, wk, 0, 64, lambda pa, pk, tb: rope64(pa, pk, [ikT[0:64, tb * 512:(tb + 1) * 512], ikT[64:128, tb * 512:(tb + 1) * 512]], "ikT", tb, tabsi))
        proj_T(wt, wk, 64, 16, lambda pa, pk, t0, g: act(iw[:, t0:t0 + g, :], pa.rearrange("p (g c) -> p g c", c=16), AF.Copy, [pk], ["iw"], scale=1.0 / 32.0), group=16)
        for half in range(2):
            wt, wk = load_w_cols(win, OFF["idx_q"] + half * 512, 512, 16, wk_in)
            for hh in range(8):
                h = half * 8 + hh
                proj_F(wt, wk, hh * 64, 64, lambda pa, pk, tb, h=h: rope64(
                    pa, pk, [iqT[64 * (h % 2):64 * (h % 2) + 64, h // 2, tb * 512:(tb + 1) * 512]], "iqT", tb, tabsi))
        P.barrier()
        A.release(m0)
        scA = A.f32([128, S])
        scB = A.f32([128, S])
        rel = [A.f32([128, 512]) for _ in range(2)]
        negm = [A.bf([128, S]) for _ in range(2)]
        m8 = A.f32([128, 8])
        pts = [(A.bf([128, 384]), ("pT", i)) for i in range(2)]
        odb = A.bf([128, 6, 128])
        oTd = [A.bf([128, 6, 128]) for _ in range(2)]
        rec = A.f32([128, 6])
        for qt in range(16):
            qsl = slice(qt * 128, (qt + 1) * 128)
            nk_ = (qt + 1) * 128
            nm = negm[qt % 2]
            nmk = ("negm", qt % 2)
            if qt >= 2:
                for kb in range(0, nk_, 512):
                    n = min(512, nk_ - kb)
                    for h in range(16):
                        b = nextps()
                        pb_ = 64 * (h % 2)
                        mm(ps[b][:, 0:n], iqT[pb_:pb_ + 64, h // 2, qsl], ikT[pb_:pb_ + 64, kb:kb + n], True, True, ["iqT", "ikT"], PS(b))
                        r_ = rel[h % 2]
                        act(r_[:, 0:n], ps[b][:, 0:n], AF.Relu, [PS(b)], [("rel", h % 2)])
                        if h == 0:
                            ts(scA[:, kb:kb + n], r_[:, 0:n], iw[:, qt, h:h + 1], None, ALU.mult, None, [("rel", h % 2), "iw"], [("scA", kb)])
                        else:
                            stt(scA[:, kb:kb + n], r_[:, 0:n], iw[:, qt, h:h + 1], scA[:, kb:kb + n], ALU.mult, ALU.add,
                                [("rel", h % 2), "iw", ("scA", kb)], [("scA", kb)])
                sck = [("scA", kb) for kb in range(0, nk_, 512)]
                tt(scA[:, qt * 128:nk_], scA[:, qt * 128:nk_], negC[:, :], ALU.add, sck + ["negC"], sck)
                P.add("dve", lambda e, nk_=nk_: e.max(out=m8[:, :], in_=scA[:, 0:nk_]), reads=sck, writes=["m8"])
                P.add("dve", lambda e, nk_=nk_: e.match_replace(out=scB[:, 0:nk_], in_to_replace=m8[:, :], in_values=scA[:, 0:nk_], imm_value=-1e30),
                      reads=sck + ["m8"], writes=["scB"])
                for r in range(31):
                    P.add("dve", lambda e, nk_=nk_: e.max(out=m8[:, :], in_=scB[:, 0:nk_]), reads=["scB"], writes=["m8"])
                    if r < 30:
                        P.add("dve", lambda e, nk_=nk_: e.match_replace(out=scB[:, 0:nk_], in_to_replace=m8[:, :], in_values=scB[:, 0:nk_], imm_value=-1e30),
                              reads=["scB", "m8"], writes=["scB"])
                ts(nm[:, 0:nk_], scA[:, 0:nk_], m8[:, 7:8], NEG, ALU.is_lt, ALU.mult, sck + ["m8"], [nmk])
            kts = []
            for kt in range(qt + 1):
                masks = []
                if qt >= 2:
                    masks.append((nm[:, kt * 128:(kt + 1) * 128], ident[:], [nmk, "ident"]))
                if kt == qt:
                    masks.append((ident[:], maskC[:], ["ident", "maskC"]))
                kts.append(dict(kT=(kdT[:, kt * 128:(kt + 1) * 128], "kdT"), ns=128, V=(vd[:, kt, 0:129], ["vd", "vd1"]), masks=masks))
            for gi, hs in enumerate(((0, 1, 2), (3, 4, 5))):
                res = attn_qtile(list(hs), lambda h, qsl=qsl: (qTd[:, h, qsl], "qTd"), kts, 129, pts, (4, 5), 2 + gi)
                for hi, h in enumerate(hs):
                    oap, opk = res[hi]
                    P.add("dve", lambda e, h=h, oap=oap: e.reciprocal(out=rec[:, h:h + 1], in_=oap[:, 128:129]), reads=[opk, "vd1"], writes=[("rec", h)])
                    ts(odb[:, h, :], oap[:, 0:128], rec[:, h:h + 1], None, ALU.mult, None, [opk, ("rec", h)], [("odb", h)])
            ot = oTd[qt % 2]
            otk = ("oTd", qt % 2)
            for half in range(2):
                bnk = 6 + half
                pb = ps[bnk][:].bitcast(BF16)
                for j in range(3):
                    h = half * 3 + j
                    tr(pb[:, j * 128:(j + 1) * 128], odb[:, h, :], ident[:], [("odb", h), "ident"], PS(bnk))
                evac(ot[:, half * 3:half * 3 + 3, :], pb[:, 0:384].rearrange("p (h c) -> p h c", c=128), [PS(bnk)], [otk])
            P.dma("sp", oT_dram[10 * 128:16 * 128, qsl].rearrange("(h d) t -> d h t", d=128), ot[:, :, :], reads=[otk],
                  writes=[("dram", "oT", 10 + qt * 0)])

    def phase_merge(l):
        P.barrier()
        A.release(0)
        win = wb["w_in"][l]
        wk_in = wkeys("w_in", l)
        oT = A.bf([128, 16, S])
        new_wslots(16 * 512 * 2)
        sg = [A.f32([128, 512]) for _ in range(3)]
        macc = [A.f32([128, 512]) for _ in range(2)]
        mix = [A.bf([128, S]) for _ in range(2)]
        for h in range(16):
            P.dma("sp", oT[:, h, :], oT_dram[h * 128:(h + 1) * 128, :], reads=[("dram", "oT", min(h, 10))], writes=["oT"])
        brs = (("w_br_fox", 0, 6), ("w_br_nsa", 6, 4), ("w_br_dsa", 10, 6))
        for c in range(16):
            buf, wk = wslot()
            wt = buf[:, 0:16 * 512].rearrange("p (k c) -> p k c", c=512)
            for bi, (wn, h0, nh) in enumerate(brs):
                P.dma("sp", wt[:, h0:h0 + nh, 0:128], wb[wn][l][:, c * 128:(c + 1) * 128].rearrange("(k p) c -> p k c", p=128),
                      reads=wkeys(wn, l), writes=[wk])
                g0 = OFF["gate"] + bi * 2048 + c * 128
                P.dma("sp", wt[:, :, 128 * (bi + 1):128 * (bi + 2)], win[:, g0:g0 + 128].rearrange("(k p) c -> p k c", p=128),
                      reads=wk_in, writes=[wk])
            mx = mix[c % 2]
            mxk = ("mix", c % 2)
            for tb in range(4):
                tsl = slice(tb * 512, (tb + 1) * 512)
                mk = ("macc", tb % 2)
                ma = macc[tb % 2]
                for bi, (wn, h0, nh) in enumerate(brs):
                    bg = nextps(0, 8)
                    for kc in range(16):
                        mm(ps[bg][:, :], wt[:, kc, 128 * (bi + 1):128 * (bi + 2)], hT[:, kc, tsl], kc == 0, kc == 15, [wk] + hkeys(tb * 4, 4), PS(bg))
                    act(sg[bi][:, :], ps[bg][:, :], AF.Sigmoid, [PS(bg)], [("sg", bi)])
                    bo = nextps(0, 8)
                    for j in range(nh):
                        mm(ps[bo][:, :], wt[:, h0 + j, 0:128], oT[:, h0 + j, tsl], j == 0, j == nh - 1, [wk, "oT"], PS(bo))
                    if bi == 0:
                        tt(ma[:, :], ps[bo][:, :], sg[bi][:, :], ALU.mult, [PS(bo), ("sg", bi)], [mk])
                    else:
                        tt(sg[bi][:, :], ps[bo][:, :], sg[bi][:, :], ALU.mult, [PS(bo), ("sg", bi)], [("sg", bi)])
                        if bi == 1:
                            tt(ma[:, :], ma[:, :], sg[bi][:, :], ALU.add, [mk, ("sg", bi)], [mk])
                        else:
                            tt(mx[:, tsl], ma[:, :], sg[bi][:, :], ALU.add, [mk, ("sg", bi)], [mxk])
            P.dma("sp", mixT_dram[c * 128:(c + 1) * 128, :], mx[:, :], reads=[mxk], writes=[("dram", "mixT", c)])

    def gemm_resid(actT, akey, nk, wsrc2d, wkl, ncg, cgw, tiles, hsrc):
        hin = [A.f32([128, cgw]) for _ in range(2)]
        rout = [A.f32([128, cgw]) for _ in range(2)]
        k_ = 0
        nxt = load_w_cols(wsrc2d, 0, cgw, nk, wkl)
        for cg in range(ncg):
            wt, wk = nxt
            if cg + 1 < ncg:
                nxt = load_w_cols(wsrc2d, (cg + 1) * cgw, cgw, nk, wkl)
            for t_ in tiles:
                i2 = k_ % 2
                k_ += 1
                P.dma("sp", hin[i2][:, :], hsrc[t_ * 128:(t_ + 1) * 128, cg * cgw:(cg + 1) * cgw], reads=[("dram", "h", t_)], writes=[("hin", i2)])
                b = nextps(0, 8)
                for kc in range(nk):
                    mm(ps[b][:, 0:cgw], actT(kc, t_), wt[:, kc, :], kc == 0, kc == nk - 1, [wk, akey], PS(b))
                stt(rout[i2][:, :], hin[i2][:, :], ALPHA, ps[b][:, 0:cgw], ALU.mult, ALU.add, [("hin", i2), PS(b)], [("rout", i2)])
                P.dma("sp", r_dram[t_ * 128:(t_ + 1) * 128, cg * cgw:(cg + 1) * cgw], rout[i2][:, :], reads=[("rout", i2)], writes=[("dram", "r", t_)])

    def layernorm_pass(l, gname, bname, tiles, hdst, write_y=False):
        gB = A.f32([128, D])
        bB = A.f32([128, D])
        rt = [A.f32([128, D]) for _ in range(2)]
        hb = [A.bf([128, D]) for _ in range(2)]
        st = A.f32([128, 2, 4, 6])
        mv = A.f32([128, 2, 2])
        P.dma("sp", gB[:, :], small[gname][l:l + 1, :].to_broadcast([128, D]), writes=["lnG"])
        P.dma("sp", bB[:, :], small[bname][l:l + 1, :].to_broadcast([128, D]), writes=["lnB"])
        for k_, t_ in enumerate(tiles):
            i2 = k_ % 2
            r_ = rt[i2]
            rk = ("rt", i2)
            P.dma("sp", r_[:, :], r_dram[t_ * 128:(t_ + 1) * 128, :], reads=[("dram", "r", t_)], writes=[rk])
            for c in range(4):
                P.add("dve", lambda e, c=c, r_=r_, i2=i2: e.bn_stats(out=st[:, i2, c, :], in_=r_[:, c * 512:(c + 1) * 512]), reads=[rk], writes=[("st", i2)])
            P.add("dve", lambda e, i2=i2: e.bn_aggr(out=mv[:, i2, :], in_=st[:, i2, :, :].rearrange("p a b -> p (a b)")), reads=[("st", i2)], writes=[("mv", i2)])
            ts(mv[:, i2, 1:2], mv[:, i2, 1:2], 1e-5, None, ALU.add, None, [("mv", i2)], [("mv", i2)])
            act(mv[:, i2, 1:2], mv[:, i2, 1:2], AF.Sqrt, [("mv", i2)], [("mv", i2)])
            P.add("dve", lambda e, i2=i2: e.reciprocal(out=mv[:, i2, 1:2], in_=mv[:, i2, 1:2]), reads=[("mv", i2)], writes=[("mv", i2)])
            ts(r_[:, :], r_[:, :], mv[:, i2, 0:1], mv[:, i2, 1:2], ALU.subtract, ALU.mult, [rk, ("mv", i2)], [rk])
            tt(r_[:, :], r_[:, :], gB[:, :], ALU.mult, [rk, "lnG"], [rk])
            tt(r_[:, :], r_[:, :], bB[:, :], ALU.add, [rk, "lnB"], [rk])
            P.dma("sp", hdst[t_ * 128:(t_ + 1) * 128, :], r_[:, :], reads=[rk], writes=[("dram", "h", t_)])
            act(hb[i2][:, :], r_[:, :], AF.Copy, [rk], [("hb", i2)])
            for q4 in range(4):
                bnk = 4 + (k_ * 4 + q4) % 4
                pb = ps[bnk][:].bitcast(BF16)
                for j in range(4):
                    kc = q4 * 4 + j
                    tr(pb[:, j * 128:(j + 1) * 128], hb[i2][:, kc * 128:(kc + 1) * 128], ident[:], [("hb", i2), "ident"], PS(bnk))
                evac(hT[:, q4 * 4:q4 * 4 + 4, t_ * 128:(t_ + 1) * 128], pb[:, 0:512].rearrange("p (k t) -> p k t", t=128), [PS(bnk)], [("hT", t_)])

    def phase_outproj(l):
        P.barrier()
        A.release(0)
        mixT = A.bf([128, 16, S])
        new_wslots(16 * 512 * 2)
        for c in range(16):
            P.dma("sp", mixT[:, c, :], mixT_dram[c * 128:(c + 1) * 128, :], reads=[("dram", "mixT", c)], writes=["mixT"])
        gemm_resid(lambda kc, t_: mixT[:, kc, t_ * 128:(t_ + 1) * 128], "mixT", 16, wb["w_out"][l], wkeys("w_out", l), 4, 512, range(NT), h_dram)
        P.barrier()
        A.release(0)
        layernorm_pass(l, "ln1_g", "ln1_b", range(NT), h_dram)

    def phase_ffn(l):
        w1 = wb["w_ffn_in"][l]
        w2 = wb["w_ffn_out"][l]
        k1 = wkeys("w_ffn_in", l)
        k2 = wkeys("w_ffn_out", l)
        for tb in range(4):
            P.barrier()
            A.release(0)
            uT = A.bf([128, 44, 512])
            new_wslots(44 * 256 * 2)
            sa = [A.f32([128, 512]) for _ in range(2)]
            tsl = slice(tb * 512, (tb + 1) * 512)

            def load_ab(c):
                buf, wk = wslot()
                wt = buf[:, 0:16 * 256].rearrange("p (k c) -> p k c", c=256)
                P.dma("sp", wt[:, :, 0:128], w1[:, c * 128:(c + 1) * 128].rearrange("(k p) c -> p k c", p=128), reads=k1, writes=[wk])
                P.dma("sp", wt[:, :, 128:256], w1[:, DFF + c * 128:DFF + (c + 1) * 128].rearrange("(k p) c -> p k c", p=128), reads=k1, writes=[wk])
                return wt, wk

            nxt = load_ab(0)
            for c in range(44):
                wt, wk = nxt
                if c + 1 < 44:
                    nxt = load_ab(c + 1)
                ba = nextps(0, 8)
                for kc in range(16):
                    mm(ps[ba][:, :], wt[:, kc, 0:128], hT[:, kc, tsl], kc == 0, kc == 15, [wk] + hkeys(tb * 4, 4), PS(ba))
                bb = nextps(0, 8)
                for kc in range(16):
                    mm(ps[bb][:, :], wt[:, kc, 128:256], hT[:, kc, tsl], kc == 0, kc == 15, [wk] + hkeys(tb * 4, 4), PS(bb))
                pump_cast(1, [wk])
                s_ = sa[c % 2]
                act(s_[:, :], ps[ba][:, :], AF.Silu, [PS(ba)], [("sa", c % 2)])
                tt(uT[:, c, :], s_[:, :], ps[bb][:, :], ALU.mult, [("sa", c % 2), PS(bb)], [("uT", c)])
            P.add("dve", lambda e: e.memset(sa[0][:, 0:1], 0.0), reads=[("uT", c) for c in range(44)] + [("sa", 0)], writes=["uTall", ("sa", 0)])
            gemm_resid(lambda kc, t_: uT[:, kc, (t_ % 4) * 128:(t_ % 4 + 1) * 128], "uTall", 44, w2, k2, 8, 256,
                       range(tb * 4, tb * 4 + 4), h_dram)
        P.barrier()
        A.release(0)
        layernorm_pass(l, "ln2_g", "ln2_b", range(NT), h_dram)

    def to_hT_tile(src, skey, t_, hbuf, hkey, k_):
        act(hbuf[:, :], src, AF.Copy, [skey], [hkey])
        for q4 in range(4):
            bnk = 4 + (k_ * 4 + q4) % 4
            pb = ps[bnk][:].bitcast(BF16)
            for j in range(4):
                kc = q4 * 4 + j
                tr(pb[:, j * 128:(j + 1) * 128], hbuf[:, kc * 128:(kc + 1) * 128], ident[:], [hkey, "ident"], PS(bnk))
            evac(hT[:, q4 * 4:q4 * 4 + 4, t_ * 128:(t_ + 1) * 128], pb[:, 0:512].rearrange("p (k t) -> p k t", t=128),
                 [PS(bnk)], [("hT", t_)])

    def phase_ple(l, last):
        P.barrier()
        A.release(0)
        new_wslots(16 * 512 * 2)
        pT = A.bf([128, 2, S])
        pin = [A.f32([128, 256]) for _ in range(2)]
        pbf = [A.bf([128, 256]) for _ in range(2)]
        wpl = A.bf([128, 2, D])
        hin = [A.f32([128, 512]) for _ in range(2)]
        sg = [A.f32([128, 512]) for _ in range(2)]
        P.dma("sp", wpl[:, :, :], wb["w_ple_in"][l].rearrange("(k p) c -> p k c", p=128), reads=wkeys("w_ple_in", l), writes=["wpl"])
        for t_ in range(NT):
            i2 = t_ % 2
            P.dma("sp", pin[i2][:, :], p_d[l, t_ * 128:(t_ + 1) * 128, :], writes=[("pin", i2)])
            vcopy(pbf[i2][:, :], pin[i2][:, :], [("pin", i2)], [("pbf", i2)])
            pb = ps[6 + i2][:].bitcast(BF16)
            for c in range(2):
                tr(pb[:, c * 128:(c + 1) * 128], pbf[i2][:, c * 128:(c + 1) * 128], ident[:], [("pbf", i2), "ident"], PS(6 + i2))
            evac(pT[:, :, t_ * 128:(t_ + 1) * 128], pb[:, 0:256].rearrange("p (c t) -> p c t", t=128), [PS(6 + i2)], ["pT"])
        wg = wb["w_ple_gate"][l]
        kg = wkeys("w_ple_gate", l)
        k_ = 0
        nxt = load_w_cols(wg, 0, 512, 16, kg)
        for cg in range(4):
            wt, wk = nxt
            if cg + 1 < 4:
                nxt = load_w_cols(wg, (cg + 1) * 512, 512, 16, kg)
            for t_ in range(NT):
                i2 = k_ % 2
                k_ += 1
                P.dma("sp", hin[i2][:, :], h_dram[t_ * 128:(t_ + 1) * 128, cg * 512:(cg + 1) * 512], reads=[("dram", "h", t_)], writes=[("hin", i2)])
                bg = nextps(0, 4)
                for kc in range(16):
                    mm(ps[bg][:, :], hT[:, kc, t_ * 128:(t_ + 1) * 128], wt[:, kc, :], kc == 0, kc == 15, [wk, ("hT", t_)], PS(bg))
                be = nextps(0, 4)
                for kc in range(2):
                    mm(ps[be][:, :], pT[:, kc, t_ * 128:(t_ + 1) * 128], wpl[:, kc, cg * 512:(cg + 1) * 512], kc == 0, kc == 1, ["pT", "wpl"], PS(be))
                act(sg[i2][:, :], ps[bg][:, :], AF.Sigmoid, [PS(bg)], [("sg", i2)])
                tt(sg[i2][:, :], sg[i2][:, :], ps[be][:, :], ALU.mult, [("sg", i2), PS(be)], [("sg", i2)])
                tt(hin[i2][:, :], hin[i2][:, :], sg[i2][:, :], ALU.add, [("hin", i2), ("sg", i2)], [("hin", i2)])
                if last:
                    finals.append(P.dma("sp", y_d[t_ * 128:(t_ + 1) * 128, cg * 512:(cg + 1) * 512], hin[i2][:, :],
                                        reads=[("hin", i2)], writes=[("dram", "y", t_, cg)]))
                else:
                    P.dma("sp", r_dram[t_ * 128:(t_ + 1) * 128, cg * 512:(cg + 1) * 512], hin[i2][:, :], reads=[("hin", i2)],
                          writes=[("dram", "r", t_)])
        if not last:
            P.barrier()
            A.release(0)
            rt = [A.f32([128, D]) for _ in range(2)]
            hb = [A.bf([128, D]) for _ in range(2)]
            for t_ in range(NT):
                i2 = t_ % 2
                P.dma("sp", rt[i2][:, :], r_dram[t_ * 128:(t_ + 1) * 128, :], reads=[("dram", "r", t_)], writes=[("rt", i2)])
                P.dma("sp", h_dram[t_ * 128:(t_ + 1) * 128, :], rt[i2][:, :], reads=[("rt", i2)], writes=[("dram", "h", t_)])
                to_hT_tile(rt[i2][:, :], ("rt", i2), t_, hb[i2], ("hb", i2), t_)

    finals = []

    cast_layer(0)
    pump_cast(10 ** 9)
    A.release(0)
    xin = [A.f32([128, D]) for _ in range(2)]
    xb = [A.bf([128, D]) for _ in range(2)]
    for t_ in range(NT):
        i2 = t_ % 2
        P.dma("sp", xin[i2][:, :], x_d[t_ * 128:(t_ + 1) * 128, :], writes=[("xin", i2)])
        P.dma("sp", h_dram[t_ * 128:(t_ + 1) * 128, :], xin[i2][:, :], reads=[("xin", i2)], writes=[("dram", "h", t_)])
        to_hT_tile(xin[i2][:, :], ("xin", i2), t_, xb[i2], ("xb", i2), t_)

    def dbg_dump(name, src_ap, keys):
        if name in dbg_d:
            P.barrier()
            finals.append(P.dma("sp", dbg_d[name], src_ap, reads=keys, writes=[("dram", "dbg", name)]))

    for l in range(depth):
        if l + 1 < depth:
            cast_layer(l + 1)
        phase_fox(l)
        phase_nsa(l)
        phase_dsa(l)
        if l == 0:
            dbg_dump("oT", oT_dram, [("dram", "oT", h) for h in range(11)])
        phase_merge(l)
        if l == 0:
            dbg_dump("mixT", mixT_dram, [("dram", "mixT", c) for c in range(16)])
        phase_outproj(l)
        if l == 0:
            dbg_dump("h1", h_dram, [("dram", "h", t) for t in range(NT)])
        phase_ffn(l)
        if l == 0:
            dbg_dump("h2", h_dram, [("dram", "h", t) for t in range(NT)])
        pump_cast(10 ** 9)
        phase_ple(l, l == depth - 1)
    P.emit(nc, ctx, final_wait_ops=finals)
    return nc, ctx


_CACHE = {}


def kernel(**inputs):
    n_cores = 8
    if "nc" not in _CACHE:
        _CACHE["nc"] = build(DEPTH)
    nc, _ctx = _CACHE["nc"]
    consts = make_consts()
    shared = {}
    for n in WSHAPES:
        shared[n] = np.ascontiguousarray(np.asarray(inputs[n], dtype=np.float32))
    for n in SMALL:
        shared[n] = np.ascontiguousarray(np.asarray(inputs[n], dtype=np.float32))
    for n, v in consts.items():
        shared["c_" + n] = np.ascontiguousarray(v)
    x = np.asarray(inputs["x"], dtype=np.float32)
    p = np.asarray(inputs["p"], dtype=np.float32)
    in_maps = []
    for c in range(n_cores):
        b = c % 4
        m = dict(shared)
        m["x"] = np.ascontiguousarray(x[b])
        m["p"] = np.ascontiguousarray(p[:, b])
        in_maps.append(m)
    res = run_bass_kernel_spmd(nc, in_maps, core_ids=list(range(n_cores)))
    out = np.stack([np.asarray(res.results[b]["y"], dtype=np.float32) for b in range(4)], axis=0)
    return out
```

```python
import numpy as np
import ml_dtypes
from contextlib import ExitStack
import concourse.bass as bass
import concourse.mybir as mybir
from concourse.bass_utils import run_bass_kernel_spmd

F32 = mybir.dt.float32
BF16 = mybir.dt.bfloat16
AF = mybir.ActivationFunctionType
ALU = mybir.AluOpType

D = 2048
S = 2048
NT = 16
DEPTH = 4
DFF = 5632
INC = 11874
PLE = 256
ALPHA = float((2 * DEPTH) ** 0.25)
SC = float(128 ** -0.5)
NEG = -30000.0
OFF = dict(fox_q=0, fox_k=768, fox_v=1536, fox_f=2304, nsa_q=2310, nsa_kc=2822, nsa_vc=2950, nsa_ks=3078,
           nsa_vs=3206, nsa_kw=3334, nsa_vw=3462, nsa_g=3590, dsa_q=3602, dsa_ckv=4370, idx_q=4626,
           idx_k=5650, idx_w=5714, gate=5730)
WSHAPES = dict(w_in=(2048, INC), nsa_cmp_k1=(4096, 256), nsa_cmp_k2=(256, 128), nsa_cmp_v1=(4096, 256),
               nsa_cmp_v2=(256, 128), dsa_kv_up=(256, 256), w_br_fox=(768, 2048), w_br_nsa=(512, 2048),
               w_br_dsa=(768, 2048), w_out=(2048, 2048), w_ffn_in=(2048, 2 * DFF), w_ffn_out=(DFF, 2048),
               w_ple_in=(256, 2048), w_ple_gate=(2048, 2048))
WORDER = ["w_in", "nsa_cmp_k1", "nsa_cmp_k2", "nsa_cmp_v1", "nsa_cmp_v2", "dsa_kv_up", "w_br_fox", "w_br_nsa",
          "w_br_dsa", "w_out", "w_ffn_in", "w_ffn_out", "w_ple_in", "w_ple_gate"]
SMALL = dict(fox_f_bias=(6,), nsa_pe_k=(32, 128), nsa_pe_v=(32, 128), dsa_kv_norm=(256,), ln1_g=(2048,),
             ln1_b=(2048,), ln2_g=(2048,), ln2_b=(2048,))
ENGS = ("pe", "act", "dve", "pool", "sp")


def _lineno():
    import sys
    f = sys._getframe(2)
    out = []
    for _ in range(4):
        if f is None:
            break
        out.append(f.f_lineno)
        f = f.f_back
    return out


class Op:
    __slots__ = ("eng", "fn", "deps", "signal", "tick", "is_dma", "dsem", "dcount", "dprev", "hard", "line")

    def __init__(self, eng, fn, is_dma=False):
        self.eng = eng
        self.fn = fn
        self.deps = set()
        self.signal = False
        self.tick = 0
        self.is_dma = is_dma
        self.dsem = None
        self.dcount = 0
        self.dprev = 0
        self.hard = False


class Res:
    __slots__ = ("last_w", "readers")

    def __init__(self):
        self.last_w = None
        self.readers = []


class Prog:
    N_DMA_SEMS = 14

    def __init__(self):
        self.ops = []
        self.res = {}
        self.last = {e: None for e in ENGS}
        self.dmas_since_bar = []

    def _r(self, key):
        r = self.res.get(key)
        if r is None:
            r = self.res[key] = Res()
        return r

    def add(self, eng, fn, reads=(), writes=(), is_dma=False, nobar=False):
        op = Op(eng, fn, is_dma)
        op.line = _lineno()
        for k in reads:
            r = self._r(k)
            if r.last_w is not None:
                op.deps.add(r.last_w)
        for k in writes:
            r = self._r(k)
            if r.last_w is not None:
                op.deps.add(r.last_w)
            op.deps.update(r.readers)
        for k in reads:
            self._r(k).readers.append(op)
        for k in writes:
            r = self._r(k)
            r.last_w = op
            r.readers = []
        op.deps.discard(op)
        self.ops.append(op)
        self.last[eng] = op
        if is_dma and not nobar:
            self.dmas_since_bar.append(op)
        return op

    def dma(self, q, out, in_, reads=(), writes=(), nobar=False):
        return self.add(q, lambda e: e.dma_start(out=out, in_=in_), reads, writes, is_dma=True, nobar=nobar)

    def barrier(self):
        lasts = [op for op in self.last.values() if op is not None and not op.is_dma]
        dmas = list(self.dmas_since_bar)
        self.dmas_since_bar = []
        for e in ("pe", "act", "dve", "sp"):
            op = Op(e, lambda eng: eng.nop())
            op.hard = True
            op.deps.update(lasts)
            op.deps.update(dmas)
            self.ops.append(op)
            self.last[e] = op
        self.res = {k: v for k, v in self.res.items() if isinstance(k, tuple) and k and k[0] in ("ps", "dram", "wb")}

    def emit(self, nc, ctx, final_wait_ops=()):
        streams = {e: [] for e in ENGS}
        for op in self.ops:
            streams[op.eng].append(op)
        for op in self.ops:
            for d in op.deps:
                if d.is_dma:
                    continue
                if d.eng == op.eng and d.eng == "pe" and not op.hard:
                    continue
                d.signal = True
        for e in ENGS:
            t = 0
            for op in streams[e]:
                if not op.is_dma and op.signal:
                    t += 1
                    op.tick = t
        esem = {e: ctx.enter_context(nc.semaphore("s_" + e)) for e in ENGS}
        dsems = {}
        for q in ENGS:
            if any(op.is_dma for op in streams[q]):
                dsems[q] = [ctx.enter_context(nc.semaphore("d_%s_%d" % (q, i))) for i in range(self.N_DMA_SEMS)]
        for q, sl in dsems.items():
            cnt = [0] * len(sl)
            i = 0
            for op in streams[q]:
                if op.is_dma:
                    j = i % len(sl)
                    op.dsem = (q, j)
                    op.dprev = cnt[j]
                    cnt[j] += 16
                    op.dcount = cnt[j]
                    i += 1
        block = ctx.enter_context(nc.Block())

        def run_stream(ename, eng):
            known = {}
            for op in streams[ename]:
                waits = {}
                for d in op.deps:
                    if d.is_dma:
                        key = ("d",) + d.dsem
                        val = d.dcount
                    else:
                        if d.eng == ename and ename == "pe" and not op.hard:
                            continue
                        key = ("e", d.eng)
                        val = d.tick
                    if waits.get(key, 0) < val:
                        waits[key] = val
                if op.is_dma and op.dprev > 0:
                    key = ("d",) + op.dsem
                    if waits.get(key, 0) < op.dprev:
                        waits[key] = op.dprev
                for key, val in waits.items():
                    if known.get(key, 0) >= val:
                        continue
                    known[key] = val
                    sem = esem[key[1]] if key[0] == "e" else dsems[key[1]][key[2]]
                    eng.wait_ge(sem, val)
                try:
                    ins = op.fn(eng)
                except Exception:
                    print("EMIT FAILED for op recorded at lines", getattr(op, "line", None), flush=True)
                    raise
                if op.is_dma:
                    ins.then_inc(dsems[op.dsem[0]][op.dsem[1]], 16)
                elif op.signal:
                    ins.then_inc(esem[ename], 1)
            if ename == "sp":
                for op in final_wait_ops:
                    eng.wait_ge(dsems[op.dsem[0]][op.dsem[1]], op.dcount)

        @block.tensor
        def _(e):
            run_stream("pe", e)

        @block.scalar
        def _(e):
            run_stream("act", e)

        @block.vector
        def _(e):
            run_stream("dve", e)

        @block.gpsimd
        def _(e):
            run_stream("pool", e)

        @block.sync
        def _(e):
            run_stream("sp", e)


class Arena:
    def __init__(self, t, nwords):
        self.t = t
        self.nwords = nwords
        self.off = 0
        self.n = 0

    def mark(self):
        return self.off

    def release(self, m):
        self.off = m

    def _alloc(self, nbytes):
        nb = (nbytes + 63) // 64 * 64
        o = self.off
        self.off += nb
        assert self.off <= self.nwords * 4, ("arena overflow", self.off, self.nwords * 4)
        return o

    def f32(self, shape):
        n = int(np.prod(shape[1:]))
        o = self._alloc(n * 4)
        v = self.t[:, o // 4: o // 4 + n]
        if len(shape) == 3:
            v = v.rearrange("p (a b) -> p a b", b=shape[2])
        elif len(shape) == 4:
            v = v.rearrange("p (a b c) -> p a b c", b=shape[2], c=shape[3])
        return v

    def bf(self, shape):
        n = int(np.prod(shape[1:]))
        n2 = (n + 1) // 2
        o = self._alloc(n2 * 4)
        v = self.t[:, o // 4: o // 4 + n2].bitcast(BF16)[:, 0:n]
        if len(shape) == 3:
            v = v.rearrange("p (a b) -> p a b", b=shape[2])
        elif len(shape) == 4:
            v = v.rearrange("p (a b c) -> p a b c", b=shape[2], c=shape[3])
        return v


def _rope_np(n, dim):
    inv = (1.0 / (np.float32(10000.0) ** (np.arange(0, dim, 2, dtype=np.float32) / np.float32(dim)))).astype(np.float32)
    ang = np.arange(n, dtype=np.float32)[:, None] * inv[None, :]
    return np.cos(ang).astype(np.float32), np.sin(ang).astype(np.float32)


def make_consts():
    c = {}
    cos, sin = _rope_np(S, 128)
    c["ropeC"] = np.ascontiguousarray(np.concatenate([cos, cos], 1).T)
    c["ropeS"] = np.ascontiguousarray(np.concatenate([-sin, sin], 1).T)
    ci, si = _rope_np(S, 64)
    c["ropeCi"] = np.ascontiguousarray(np.concatenate([ci, ci], 1).T)
    c["ropeSi"] = np.ascontiguousarray(np.concatenate([-si, si], 1).T)
    sp = np.arange(128)[:, None]
    tp = np.arange(128)[None, :]
    c["maskC"] = np.where(sp <= tp, 0.0, NEG).astype(ml_dtypes.bfloat16)
    c["maskW"] = np.where(sp > tp, 0.0, NEG).astype(ml_dtypes.bfloat16)
    cc = np.arange(127)[:, None]
    tt = np.arange(S)[None, :]
    cm = np.zeros((128, S), np.float32)
    cm[:127] = np.where(16 * cc + 31 <= tt, 0.0, NEG)
    c["maskCmp"] = cm.astype(ml_dtypes.bfloat16)
    c_start = np.arange(127) * 16
    c_end = c_start + 31
    s_start = np.arange(32) * 64
    ov = np.maximum(np.minimum(c_end[:, None], s_start[None, :] + 63) - np.maximum(c_start[:, None], s_start[None, :]) + 1, 0) / 32.0
    oe = np.zeros((128, 33), np.float32)
    oe[:127, 0] = 1.0
    oe[:127, 1:] = ov
    c["ovl"] = oe.astype(ml_dtypes.bfloat16)
    pos = np.arange(S)
    blk = np.arange(32)[None, :]
    cur = (pos // 64)[:, None]
    forced = (blk == 0) | (blk == cur) | (blk == cur - 1)
    vis = s_start[None, :] <= pos[:, None]
    mul = (vis & ~forced).astype(np.float32)
    add = np.where(forced, 1e4, np.where(vis, 0.0, -1e4)).astype(np.float32)
    c["selMul"] = np.ascontiguousarray(mul.reshape(16, 128, 32).transpose(1, 0, 2))
    c["selAdd"] = np.ascontiguousarray(add.reshape(16, 128, 32).transpose(1, 0, 2))
    ex = np.zeros((32, 16, 128), np.float32)
    for kt in range(16):
        for s_ in range(128):
            ex[2 * kt + s_ // 64, kt, s_] = 1.0
    c["expand"] = ex.astype(ml_dtypes.bfloat16)
    c["negC"] = np.where(tp.T >= sp.T, 0.0, -1e30).astype(np.float32)
    c["negC"] = np.where(np.arange(128)[None, :] <= np.arange(128)[:, None], 0.0, -1e30).astype(np.float32)
    return c


CONST_SHAPES = dict(ropeC=([128, S], F32), ropeS=([128, S], F32), ropeCi=([64, S], F32), ropeSi=([64, S], F32),
                    maskC=([128, 128], BF16), maskW=([128, 128], BF16), maskCmp=([128, S], BF16),
                    ovl=([128, 33], BF16), selMul=([128, 16, 32], F32), selAdd=([128, 16, 32], F32),
                    expand=([32, 16, 128], BF16), negC=([128, 128], F32))


marks_out = []


def build(depth=DEPTH, dbg=(), marks=False):
    nc = bass.Bass("TRN2", target_bir_lowering=False)
    ctx = ExitStack()
    P = Prog()
    dt_in = {}
    x_d = nc.dram_tensor("x", [S, D], F32, kind="ExternalInput").ap()
    p_d = nc.dram_tensor("p", [DEPTH, S, PLE], F32, kind="ExternalInput").ap()
    wsrc = {n: nc.dram_tensor(n, [DEPTH] + list(s), F32, kind="ExternalInput").ap() for n, s in WSHAPES.items()}
    small = {n: nc.dram_tensor(n, [DEPTH] + list(s), F32, kind="ExternalInput").ap() for n, s in SMALL.items()}
    cst = {n: nc.dram_tensor("c_" + n, s, d, kind="ExternalInput").ap() for n, (s, d) in CONST_SHAPES.items()}
    y_d = nc.dram_tensor("y", [S, D], F32, kind="ExternalOutput").ap()
    dbg_d = {}
    for name, shape, dty in dbg:
        dbg_d[name] = nc.dram_tensor("dbg_" + name, list(shape), dty, kind="ExternalOutput").ap()
    wb = {n: nc.dram_tensor("wb_" + n, [depth] + list(s), BF16, kind="Internal").ap() for n, s in WSHAPES.items()}
    h_dram = nc.dram_tensor("h_dram", [S, D], F32, kind="Internal").ap()
    r_dram = nc.dram_tensor("r_dram", [S, D], F32, kind="Internal").ap()
    oT_dram = nc.dram_tensor("oT_dram", [D, S], BF16, kind="Internal").ap()
    mixT_dram = nc.dram_tensor("mixT_dram", [D, S], BF16, kind="Internal").ap()

    ARENA_WORDS = 32 * 1024
    hT = ctx.enter_context(nc.sbuf_tensor("hT", [128, 16, S], BF16))
    arena_t = ctx.enter_context(nc.sbuf_tensor("arena", [128, ARENA_WORDS], F32))
    A = Arena(arena_t, ARENA_WORDS)
    ident = ctx.enter_context(nc.sbuf_tensor("ident", [128, 128], BF16))
    identf = ctx.enter_context(nc.sbuf_tensor("identf", [128, 128], F32))
    maskC = ctx.enter_context(nc.sbuf_tensor("maskC", [128, 128], BF16))
    maskW = ctx.enter_context(nc.sbuf_tensor("maskW", [128, 128], BF16))
    maskCmp = ctx.enter_context(nc.sbuf_tensor("maskCmp", [128, S], BF16))
    ovl = ctx.enter_context(nc.sbuf_tensor("ovl", [128, 33], BF16))
    selMul = ctx.enter_context(nc.sbuf_tensor("selMul", [128, 16, 32], F32))
    selAdd = ctx.enter_context(nc.sbuf_tensor("selAdd", [128, 16, 32], F32))
    expand = ctx.enter_context(nc.sbuf_tensor("expand", [32, 16, 128], BF16))
    negC = ctx.enter_context(nc.sbuf_tensor("negC", [128, 128], F32))
    onesb = ctx.enter_context(nc.sbuf_tensor("onesb", [128, 16], BF16))
    ps = [ctx.enter_context(nc.psum_tensor("ps%d" % i, [128, 512], F32)) for i in range(8)]

    def PS(i):
        return ("ps", i)

    cnt = {"ev": 0, "ps": 0, "w": 0}

    def mm(out, lhsT, rhs, start, stop, reads, pskey):
        P.add("pe", lambda e: e.matmul(out, lhsT, rhs, start=start, stop=stop), reads=reads, writes=[pskey])

    def tr(out, in_, idn, reads, pskey):
        P.add("pe", lambda e: e.transpose(out=out, in_=in_, identity=idn), reads=reads, writes=[pskey])

    def act(out, in_, func, reads, writes, bias=0.0, scale=1.0, accum=None):
        if accum is None:
            P.add("act", lambda e: e.activation(out=out, in_=in_, func=func, bias=bias, scale=scale), reads=reads, writes=writes)
        else:
            P.add("act", lambda e: e.activation(out=out, in_=in_, func=func, bias=bias, scale=scale, accum_out=accum), reads=reads, writes=writes)

    def tt(out, in0, in1, op, reads, writes):
        P.add("dve", lambda e: e.tensor_tensor(out=out, in0=in0, in1=in1, op=op), reads=reads, writes=writes)

    def ts(out, in0, s1, s2, op0, op1, reads, writes):
        if s2 is None:
            P.add("dve", lambda e: e.tensor_scalar(out=out, in0=in0, scalar1=s1, scalar2=None, op0=op0), reads=reads, writes=writes)
        else:
            P.add("dve", lambda e: e.tensor_scalar(out=out, in0=in0, scalar1=s1, scalar2=s2, op0=op0, op1=op1), reads=reads, writes=writes)

    def stt(out, in0, scalar, in1, op0, op1, reads, writes):
        P.add("dve", lambda e: e.scalar_tensor_tensor(out=out, in0=in0, scalar=scalar, in1=in1, op0=op0, op1=op1), reads=reads, writes=writes)

    def vcopy(out, in_, reads, writes):
        P.add("dve", lambda e: e.tensor_copy(out=out, in_=in_), reads=reads, writes=writes)

    def acopy(out, in_, reads, writes):
        act(out, in_, AF.Copy, reads, writes)

    def evac(out, in_, reads, writes):
        cnt["ev"] += 1
        if cnt["ev"] % 2:
            acopy(out, in_, reads, writes)
        else:
            vcopy(out, in_, reads, writes)

    def memset(ap, val, writes, eng="dve"):
        P.add(eng, lambda e: e.memset(ap, val), writes=writes)

    def nextps(lo=0, n=4):
        cnt["ps"] += 1
        return lo + cnt["ps"] % n

    def hkeys(t0, n):
        return [("hT", t) for t in range(t0, t0 + n)]

    def wkeys(name, l):
        return [("wb", name, l, c) for c in range(nchunks[name])]

    for t_, n_ in ((maskC, "maskC"), (maskW, "maskW"), (maskCmp, "maskCmp"), (ovl, "ovl"), (selMul, "selMul"),
                   (selAdd, "selAdd"), (expand, "expand"), (negC, "negC")):
        P.dma("sp", t_[:], cst[n_], writes=[n_])
    memset(ident[:], 1.0, ["ident"], eng="pool")
    P.add("pool", lambda e: e.affine_select(out=ident[:], in_=ident[:], pattern=[[-1, 128]], compare_op=ALU.is_equal,
                                            fill=0.0, base=0, channel_multiplier=1), reads=["ident"], writes=["ident"])
    memset(identf[:], 1.0, ["identf"], eng="pool")
    P.add("pool", lambda e: e.affine_select(out=identf[:], in_=identf[:], pattern=[[-1, 128]], compare_op=ALU.is_equal,
                                            fill=0.0, base=0, channel_multiplier=1), reads=["identf"], writes=["identf"])
    memset(onesb[:], 1.0, ["onesb"], eng="pool")

    CW = 4096
    nchunks = {}
    for n, (R_, C_) in WSHAPES.items():
        m_ = R_ * C_ // 128
        nchunks[n] = (m_ + CW - 1) // CW
    cast_q = []

    def cast_layer(l):
        for n in WORDER:
            R_, C_ = WSHAPES[n]
            m_ = R_ * C_ // 128
            src = wsrc[n][l].rearrange("(p a) c -> p (a c)", p=128)
            dst = wb[n][l].rearrange("(p a) c -> p (a c)", p=128)
            for c in range(nchunks[n]):
                c0 = c * CW
                c1 = min(m_, c0 + CW)
                cast_q.append((dst[:, c0:c1], src[:, c0:c1], ("wb", n, l, c)))

    def pump_cast(k, dep_keys=()):
        for _ in range(min(k, len(cast_q))):
            d_, s_, key = cast_q.pop(0)
            P.dma("pool", d_, s_, reads=list(dep_keys), writes=[key], nobar=True)

    wslots = {}

    def new_wslots(nbytes, n=2):
        wslots["v"] = [A.bf([128, nbytes // 2]) for _ in range(n)]
        wslots["i"] = 0

    def wslot():
        i = wslots["i"] % len(wslots["v"])
        wslots["i"] += 1
        return wslots["v"][i], ("wslot", i)

    def load_w_cols(src2d, c0, ncols, nk, keys, dst=None, dkey=None, coff=0):
        if dst is None:
            buf, dkey = wslot()
            dst = buf[:, 0:nk * ncols].rearrange("p (k c) -> p k c", c=ncols)
            P.dma("sp", dst, src2d[:, c0:c0 + ncols].rearrange("(k p) c -> p k c", p=128), reads=keys, writes=[dkey])
        else:
            P.dma("sp", dst[:, :, coff:coff + ncols], src2d[:, c0:c0 + ncols].rearrange("(k p) c -> p k c", p=128),
                  reads=keys, writes=[dkey])
        return dst, dkey

    def rope128(psap, pk, dst, dkey, tb, tabs, n=512, pos=None):
        C_, S_, tk = tabs
        if pos is None:
            cs = C_[:, tb * 512: tb * 512 + n]
            ss_lo = S_[0:64, tb * 512: tb * 512 + n]
            ss_hi = S_[64:128, tb * 512: tb * 512 + n]
        else:
            cs, ss_lo, ss_hi = pos
        ta = ropetmp["a"][cnt["ev"] % 2]
        tbm = ropetmp["b"][cnt["ev"] % 2]
        ka = ("ropeA", cnt["ev"] % 2)
        kb = ("ropeB", cnt["ev"] % 2)
        cnt["ev"] += 1
        tt(ta[:, 0:n], psap, cs, ALU.mult, [pk, tk], [ka])
        tt(tbm[0:64, 0:n], psap[64:128, :], ss_lo, ALU.mult, [pk, tk], [kb])
        tt(tbm[64:128, 0:n], psap[0:64, :], ss_hi, ALU.mult, [pk, tk], [kb])
        tt(dst, ta[:, 0:n], tbm[:, 0:n], ALU.add, [ka, kb], [dkey])

    def rope64(psap, pk, dsts, dkey, tb, tabs, n=512):
        C_, S_, tk = tabs
        ta = ropetmp["a"][cnt["ev"] % 2]
        tbm = ropetmp["b"][cnt["ev"] % 2]
        ka = ("ropeA", cnt["ev"] % 2)
        kb = ("ropeB", cnt["ev"] % 2)
        cnt["ev"] += 1
        sl = slice(tb * 512, tb * 512 + n)
        tt(ta[0:64, 0:n], psap, C_[0:64, sl], ALU.mult, [pk, tk], [ka])
        tt(tbm[0:32, 0:n], psap[32:64, :], S_[0:32, sl], ALU.mult, [pk, tk], [kb])
        tt(tbm[32:64, 0:n], psap[0:32, :], S_[32:64, sl], ALU.mult, [pk, tk], [kb])
        for dst in dsts:
            tt(dst, ta[0:64, 0:n], tbm[0:64, 0:n], ALU.add, [ka, kb], [dkey])

    ropetmp = {}

    def alloc_ropetmp():
        ropetmp["a"] = [A.f32([128, 512]) for _ in range(2)]
        ropetmp["b"] = [A.f32([128, 512]) for _ in range(2)]

    def proj_F(wt, wk, c0, M, epi, actT=None, akey="hT", nk=16, base_ps=0):
        src = hT if actT is None else actT
        for tb in range(4):
            b = nextps()
            for kc in range(nk):
                mm(ps[b][0:M, :], wt[:, kc, c0:c0 + M], src[:, kc, tb * 512:(tb + 1) * 512], kc == 0, kc == nk - 1,
                   [wk] + (hkeys(tb * 4, 4) if akey == "hT" else [akey]), PS(b))
            epi(ps[b][0:M, :], PS(b), tb)

    def proj_T(wt, wk, c0, N, epi, actT=None, akey="hT", nk=16, group=None):
        src = hT if actT is None else actT
        if group is None:
            group = max(1, 512 // N)
        for t0 in range(0, NT, group):
            b = nextps()
            g = min(group, NT - t0)
            for j in range(g):
                tt_ = t0 + j
                for kc in range(nk):
                    mm(ps[b][:, j * N:(j + 1) * N], src[:, kc, tt_ * 128:(tt_ + 1) * 128], wt[:, kc, c0:c0 + N],
                       kc == 0, kc == nk - 1, [wk] + (hkeys(tt_, 1) if akey == "hT" else [akey]), PS(b))
            epi(ps[b][:, 0:g * N], PS(b), t0, g)

    def attn_qtile(heads, qT_of, ktiles, ncol, pT, skey_banks, obanks, exp_bias=None):
        G = len(heads)
        ob = obanks
        nkt = len(ktiles)
        for ki, kt in enumerate(ktiles):
            sb = skey_banks[ki % 2]
            ns = kt["ns"]
            kT, kkey = kt["kT"]
            for hi, h in enumerate(heads):
                q, qk = qT_of(h)
                nm = len(kt["masks"])
                mm(ps[sb][0:ns, hi * 128:(hi + 1) * 128], kT, q, hi == 0, nm == 0 and hi == G - 1, [kkey, qk], PS(sb))
                for mi, (ml, mr, mk) in enumerate(kt["masks"]):
                    mm(ps[sb][0:ns, hi * 128:(hi + 1) * 128], ml, mr, False, mi == nm - 1 and hi == G - 1, mk, PS(sb))
            pt, ptk = pT[ki % 2]
            if exp_bias is None:
                act(pt[0:ns, 0:G * 128], ps[sb][0:ns, 0:G * 128], AF.Exp, [PS(sb)], [ptk], scale=SC)
            else:
                for hi, h in enumerate(heads):
                    bap, bk = exp_bias(h, kt)
                    act(pt[0:ns, hi * 128:(hi + 1) * 128], ps[sb][0:ns, hi * 128:(hi + 1) * 128], AF.Exp,
                        [PS(sb), bk], [ptk], scale=SC, bias=bap)
            V, vk = kt["V"]
            for hi, h in enumerate(heads):
                mm(ps[ob][:, hi * ncol:(hi + 1) * ncol], pt[0:ns, hi * 128:(hi + 1) * 128], V, ki == 0 and hi == 0,
                   ki == nkt - 1 and hi == G - 1, [ptk] + list(vk), PS(ob))
        return [(ps[ob][:, hi * ncol:(hi + 1) * ncol], PS(ob)) for hi in range(G)]

    def transposes_to(dstT, dkey, src_tok, skey, nblk, tbank):
        pb = ps[tbank][:].bitcast(BF16)
        for j in range(nblk):
            tr(pb[:, j * 128:(j + 1) * 128], src_tok[:, j * 128:(j + 1) * 128], ident[:], [skey, "ident"], PS(tbank))
        for j in range(nblk):
            evac(dstT(j), pb[:, j * 128:(j + 1) * 128], [PS(tbank)], [dkey])

    def load_tables128():
        C_ = A.f32([128, S])
        S_ = A.f32([128, S])
        P.dma("sp", C_, cst["ropeC"], writes=["tabC"])
        P.dma("sp", S_, cst["ropeS"], writes=["tabC"])
        return (C_, S_, "tabC")

    def load_tables64():
        C_ = A.f32([128, S])
        S_ = A.f32([128, S])
        P.dma("sp", C_[0:64, :], cst["ropeCi"], writes=["tabI"])
        P.dma("sp", S_[0:64, :], cst["ropeSi"], writes=["tabI"])
        return (C_, S_, "tabI")

    def phase_fox(l):
        A.release(0)
        win = wb["w_in"][l]
        wk_in = wkeys("w_in", l)
        new_wslots(16 * 384 * 2)
        wf = A.bf([128, 16, 8])
        lf = A.f32([128, S])
        cp = A.f32([128, S])
        onesr = A.f32([128, S])
        negb = A.f32([128, 2])
        ctok = A.f32([128, 96])
        bsh = A.f32([128, 96])
        rbd = A.f32([128, 96])
        bias = A.f32([128, 6, 16, 16])
        qTh = A.bf([128, S])
        kTh = A.bf([128, S])
        vh = A.bf([128, 16, 130])
        oTh = [A.bf([128, S]) for _ in range(2)]
        pts = [(A.bf([128, 128]), ("pT", i)) for i in range(2)]
        otok = [A.bf([128, 128]) for _ in range(2)]
        rec = A.f32([128, 2])
        memset(onesr[0:8, :], 1.0, ["onesr"])
        memset(vh[:, :, 128:129], 1.0, ["vh1"])
        P.dma("sp", wf[:, :, 0:6], win[:, OFF["fox_f"]:OFF["fox_f"] + 6].rearrange("(k p) c -> p k c", p=128),
              reads=wk_in, writes=["wf"])
        P.dma("sp", negb[0:6, 0:1], small["fox_f_bias"][l:l + 1, :].rearrange("o h -> h o"), writes=["negb"])
        ts(negb[0:6, 1:2], negb[0:6, 0:1], -1.0, None, ALU.mult, None, ["negb"], ["negb2"])
        for tb in range(4):
            b = nextps()
            for kc in range(16):
                mm(ps[b][0:6, :], wf[:, kc, 0:6], hT[:, kc, tb * 512:(tb + 1) * 512], kc == 0, kc == 15, ["wf"] + hkeys(tb * 4, 4), PS(b))
            sl = slice(tb * 512, (tb + 1) * 512)
            act(lf[0:6, sl], ps[b][0:6, :], AF.Exp, [PS(b), "negb2"], [("lf", tb)], bias=negb[0:6, 1:2], scale=-1.0)
            act(lf[0:6, sl], lf[0:6, sl], AF.Ln, [("lf", tb)], [("lf", tb)], bias=1.0, scale=1.0)
        P.add("dve", lambda e: e.tensor_tensor_scan(out=cp[0:6, :], data0=onesr[0:6, :], data1=lf[0:6, :], initial=0.0,
                                                    op0=ALU.mult, op1=ALU.add),
              reads=[("lf", t) for t in range(4)] + ["onesr"], writes=["cp"])
        b = nextps()
        for kt in range(16):
            tr(ps[b][:, kt * 6:(kt + 1) * 6], cp[0:6, kt * 128:(kt + 1) * 128], identf[0:6, 0:6], ["cp", "identf"], PS(b))
        vcopy(ctok[:, :], ps[b][:, 0:96], [PS(b)], ["ctok"])
        for h in range(6):
            ts(rbd[0:6, h * 16:(h + 1) * 16], cp[0:6, 127:S:128], identf[0:6, h:h + 1], None, ALU.mult, None,
               ["cp", "identf"], ["rbd"])
        b = nextps()
        mm(ps[b][:, 0:96], onesr[0:6, 0:128], rbd[0:6, 0:96], True, True, ["onesr", "rbd"], PS(b))
        vcopy(bsh[:, :], ps[b][:, 0:96], [PS(b)], ["bsh"])
        ctv = ctok[:, :].rearrange("p (k h) -> p h k", h=6)
        for h in range(6):
            for qt in range(16):
                ts(bias[:, h, qt, :], ctv[:, h, :], bsh[:, h * 16 + qt:h * 16 + qt + 1], None, ALU.subtract, None,
                   ["ctok", "bsh"], [("bias", h)])
        for h in range(6):
            wt, wk = wslot()
            wt = wt[:, 0:16 * 384].rearrange("p (k c) -> p k c", c=384)
            for j, seg in enumerate(("fox_q", "fox_k", "fox_v")):
                c0 = OFF[seg] + h * 128
                P.dma("sp", wt[:, :, j * 128:(j + 1) * 128], win[:, c0:c0 + 128].rearrange("(k p) c -> p k c", p=128),
                      reads=wk_in, writes=[wk])
            proj_F(wt, wk, 0, 128, lambda pa, pk, tb: evac(qTh[:, tb * 512:(tb + 1) * 512], pa, [pk], ["qTh"]))
            proj_F(wt, wk, 128, 128, lambda pa, pk, tb: evac(kTh[:, tb * 512:(tb + 1) * 512], pa, [pk], ["kTh"]))
            proj_T(wt, wk, 256, 128, lambda pa, pk, t0, g: evac(vh[:, t0:t0 + g, 0:128], pa.rearrange("p (g c) -> p g c", c=128), [pk], ["vh"]))
            oT = oTh[h % 2]
            ok_ = ("oTh", h % 2)
            for qt in range(16):
                kts = []
                for kt in range(qt + 1):
                    masks = []
                    if kt == qt:
                        masks.append((ident[:], maskC[:], ["ident", "maskC"]))
                    kts.append(dict(kT=(kTh[:, kt * 128:(kt + 1) * 128], "kTh"), ns=128,
                                    V=(vh[:, kt, 0:129], ["vh", "vh1"]), masks=masks, kt=kt))
                res = attn_qtile([h], lambda hh: (qTh[:, qt * 128:(qt + 1) * 128], "qTh"), kts, 129, pts, (4, 5),
                                 2 + qt % 2, exp_bias=lambda hh, k, qt=qt: (bias[:, hh, qt, k["kt"]:k["kt"] + 1], ("bias", hh)))
                oap, opk = res[0]
                ri = qt % 2
                P.add("dve", lambda e, ri=ri, oap=oap: e.reciprocal(out=rec[:, ri:ri + 1], in_=oap[:, 128:129]),
                      reads=[opk, "vh1"], writes=[("rec", ri)])
                ts(otok[ri][:, :], oap[:, 0:128], rec[:, ri:ri + 1], None, ALU.mult, None, [opk, ("rec", ri)], [("otok", ri)])
                transposes_to(lambda j, qt=qt, oT=oT: oT[:, qt * 128:(qt + 1) * 128], ok_, otok[ri], ("otok", ri), 1, 6 + qt % 2)
            P.dma("sp", oT_dram[h * 128:(h + 1) * 128, :], oT[:, :], reads=[ok_], writes=[("dram", "oT", h)])

    def phase_nsa(l):
        P.barrier()
        A.release(0)
        win = wb["w_in"][l]
        wk_in = wkeys("w_in", l)
        qTn = A.bf([128, 4, S])
        ksT = A.bf([128, S])
        kwT = A.bf([128, S])
        vs = A.bf([128, 16, 130])
        vw = A.bf([128, 16, 130])
        kcmpT = A.bf([128, 128])
        vcx = A.bf([128, 162])
        gsig = A.f32([128, 16, 12])
        oTn = A.bf([128, 4, S])
        m_stage = A.mark()
        tabs = load_tables128()
        alloc_ropetmp()
        new_wslots(16 * 512 * 2)
        rawT = [A.bf([128, S]) for _ in range(2)]
        peT = A.bf([128, 2, 32])
        pe32 = A.f32([32, 2, 128])
        w2 = A.bf([128, 2, 2, 128])
        h1x = A.f32([128, 256])
        h1a = A.f32([128, 256])
        gT = A.bf([128, 2, 2, 128])
        memset(vs[:, :, 128:129], 1.0, ["vs1"])
        memset(vw[:, :, 128:129], 1.0, ["vw1"])
        vcopy(vcx[:, 128:161], ovl[:, :], ["ovl"], ["vcx1"])
        wt, wk = load_w_cols(win, OFF["nsa_g"], 12, 16, wk_in)
        proj_T(wt, wk, 0, 12, lambda pa, pk, t0, g: act(gsig[:, t0:t0 + g, :], pa.rearrange("p (g c) -> p g c", c=12), AF.Sigmoid, [pk], ["gsig"]), group=16)
        wt, wk = load_w_cols(win, OFF["nsa_q"], 512, 16, wk_in)
        for h in range(4):
            proj_F(wt, wk, h * 128, 128, lambda pa, pk, tb, h=h: rope128(pa, pk, qTn[:, h, tb * 512:(tb + 1) * 512], "qTn", tb, tabs))
        wt, wk = load_w_cols(win, OFF["nsa_kc"], 512, 16, wk_in)
        proj_F(wt, wk, 0, 128, lambda pa, pk, tb: evac(rawT[0][:, tb * 512:(tb + 1) * 512], pa, [pk], [("rawT", 0)]))
        proj_F(wt, wk, 128, 128, lambda pa, pk, tb: evac(rawT[1][:, tb * 512:(tb + 1) * 512], pa, [pk], [("rawT", 1)]))
        proj_F(wt, wk, 256, 128, lambda pa, pk, tb: rope128(pa, pk, ksT[:, tb * 512:(tb + 1) * 512], "ksT", tb, tabs))
        proj_T(wt, wk, 384, 128, lambda pa, pk, t0, g: evac(vs[:, t0:t0 + g, 0:128], pa.rearrange("p (g c) -> p g c", c=128), [pk], ["vs"]))
        wt, wk = load_w_cols(win, OFF["nsa_kw"], 256, 16, wk_in)
        proj_F(wt, wk, 0, 128, lambda pa, pk, tb: rope128(pa, pk, kwT[:, tb * 512:(tb + 1) * 512], "kwT", tb, tabs))
        proj_T(wt, wk, 128, 128, lambda pa, pk, t0, g: evac(vw[:, t0:t0 + g, 0:128], pa.rearrange("p (g c) -> p g c", c=128), [pk], ["vw"]))
        for which, (pen, w1n, w2n) in enumerate((("nsa_pe_k", "nsa_cmp_k1", "nsa_cmp_k2"), ("nsa_pe_v", "nsa_cmp_v1", "nsa_cmp_v2"))):
            P.dma("sp", pe32[0:32, which, :], small[pen][l], writes=[("pe32", which)])
            b = nextps()
            tr(ps[b][:, 0:32], pe32[0:32, which, :], identf[0:32, 0:32], [("pe32", which), "identf"], PS(b))
            vcopy(peT[:, which, :], ps[b][:, 0:32], [PS(b)], [("peT", which)])
            P.dma("sp", w2[:, which, :, :], wb[w2n][l].rearrange("(k p) c -> p k c", p=128), reads=wkeys(w2n, l), writes=[("w2", which)])
            buf, wk1 = wslot()
            w1 = buf[:, 0:32 * 256].rearrange("p (l c) -> p l c", c=256)
            P.dma("sp", w1, wb[w1n][l].rearrange("(l p) c -> p l c", p=128), reads=wkeys(w1n, l), writes=[wk1])
            b = nextps()
            for j in range(2):
                for li in range(32):
                    mm(ps[b][:, j * 128:j * 128 + 127], w1[:, li, j * 128:(j + 1) * 128], rawT[which][:, li:li + 16 * 126 + 1:16],
                       li == 0, False, [wk1, ("rawT", which)], PS(b))
                    mm(ps[b][:, j * 128:j * 128 + 127], w1[:, li, j * 128:(j + 1) * 128],
                       peT[:, which, li:li + 1].to_broadcast([128, 127]), False, li == 31, [wk1, ("peT", which)], PS(b))
            hx = h1x[:, :].rearrange("p (j c) -> p j c", c=128)[:, :, 0:127]
            ha = h1a[:, :].rearrange("p (j c) -> p j c", c=128)[:, :, 0:127]
            pv = ps[b][:, 0:256].rearrange("p (j c) -> p j c", c=128)[:, :, 0:127]
            acopy(hx, pv, [PS(b)], ["h1x"])
            tt(ha, hx, hx, ALU.mult, ["h1x"], ["h1a"])
            ts(ha, ha, 0.044715, 1.0, ALU.mult, ALU.add, ["h1a"], ["h1a"])
            tt(ha, ha, hx, ALU.mult, ["h1a", "h1x"], ["h1a"])
            act(ha, ha, AF.Tanh, ["h1a"], ["h1a"], scale=0.7978845608028654)
            stt(ha, ha, 1.0, hx, ALU.add, ALU.mult, ["h1a", "h1x"], ["h1a"])
            ts(gT[:, which, :, 0:127], ha, 0.5, None, ALU.mult, None, ["h1a"], [("gT", which)])
            b = nextps()
            if which == 0:
                for j in range(2):
                    mm(ps[b][:, 0:127], w2[:, 0, j, :], gT[:, 0, j, 0:127], j == 0, j == 1, [("w2", 0), ("gT", 0)], PS(b))
                C_, S_, tk = tabs
                rope128(ps[b][:, 0:127], PS(b), kcmpT[:, 0:127], "kcmpT", 0, tabs, n=127,
                        pos=(C_[:, 31:S:16], S_[0:64, 31:S:16], S_[64:128, 31:S:16]))
            else:
                for j in range(2):
                    mm(ps[b][0:127, 0:128], gT[:, 1, j, 0:127], w2[:, 1, j, :], j == 0, j == 1, [("w2", 1), ("gT", 1)], PS(b))
                evac(vcx[0:127, 0:128], ps[b][0:127, 0:128], [PS(b)], ["vcx"])
        P.barrier()
        A.release(m_stage)
        pts = [(A.bf([128, 512]), ("pT", i)) for i in range(2)]
        oacc = A.f32([128, 4, 128])
        onb = A.bf([128, 4, 128])
        rec = A.f32([128, 4])
        coef = A.f32([128, 4])
        imp = A.f32([128, 32])
        imp2 = A.f32([128, 32])
        m8 = A.f32([128, 16])
        negsel = A.bf([128, 32])
        negselT = A.bf([32, 128])
        vkeys = ["vcx", "vcx1"]
        for qt in range(16):
            qsl = slice(qt * 128, (qt + 1) * 128)

            def qof(h, qsl=qsl):
                return (qTn[:, h, qsl], "qTn")

            gq = gsig[:, qt, :].rearrange("p (h c) -> p c h", c=3)
            kts = [dict(kT=(kcmpT[:, 0:127], "kcmpT"), ns=127, V=(vcx[0:127, 0:161], ["vcx", "vcx1"]),
                        masks=[(ident[0:127, 0:127], maskCmp[0:127, qsl], ["ident", "maskCmp"])])]
            for gi, hs in enumerate(((0, 1, 2), (3,))):
                res = attn_qtile(list(hs), qof, kts, 161, pts, (4, 5), 2 + gi)
                for hi, h in enumerate(hs):
                    oap, opk = res[hi]
                    ts(rec[:, h:h + 1], oap[:, 128:129], 1e-30, None, ALU.max, None, [opk, "vcx1"], [("rec", h)])
                    P.add("dve", lambda e, h=h: e.reciprocal(out=rec[:, h:h + 1], in_=rec[:, h:h + 1]), reads=[("rec", h)], writes=[("rec", h)])
                    tt(coef[:, h:h + 1], rec[:, h:h + 1], gq[:, 0, h:h + 1], ALU.mult, [("rec", h), "gsig"], [("coef", h)])
                    ts(oacc[:, h, :], oap[:, 0:128], coef[:, h:h + 1], None, ALU.mult, None, [opk, ("coef", h)], [("oacc", h)])
                    if h == 0:
                        ts(imp[:, :], oap[:, 129:161], rec[:, h:h + 1], None, ALU.mult, None, [opk, ("rec", h)], ["imp"])
                    else:
                        stt(imp[:, :], oap[:, 129:161], rec[:, h:h + 1], imp[:, :], ALU.mult, ALU.add, [opk, ("rec", h), "imp"], ["imp"])
            tt(imp2[:, :], imp[:, :], selMul[:, qt, :], ALU.mult, ["imp", "selMul"], ["imp2"])
            tt(imp2[:, :], imp2[:, :], selAdd[:, qt, :], ALU.add, ["imp2", "selAdd"], ["imp2"])
            P.add("dve", lambda e: e.max(out=m8[:, 0:8], in_=imp2[:, :]), reads=["imp2"], writes=["m8"])
            P.add("dve", lambda e: e.match_replace(out=imp[:, :], in_to_replace=m8[:, 0:8], in_values=imp2[:, :], imm_value=-1e30),
                  reads=["imp2", "m8"], writes=["imp"])
            P.add("dve", lambda e: e.max(out=m8[:, 8:16], in_=imp[:, :]), reads=["imp"], writes=["m8b"])
            ts(negsel[:, :], imp2[:, :], m8[:, 15:16], NEG, ALU.is_lt, ALU.mult, ["imp2", "m8b"], ["negsel"])
            pb = ps[6][:].bitcast(BF16)
            tr(pb[0:32, 0:128], negsel[:, :], ident[:], ["negsel", "ident"], PS(6))
            vcopy(negselT[0:32, :], pb[0:32, 0:128], [PS(6)], ["negselT"])
            for br in range(2):
                kts = []
                if br == 0:
                    for kt in range(qt + 1):
                        masks = [(expand[:, kt, :], negselT[0:32, :], ["expand", "negselT"])]
                        if kt == qt:
                            masks.append((ident[:], maskC[:], ["ident", "maskC"]))
                        kts.append(dict(kT=(ksT[:, kt * 128:(kt + 1) * 128], "ksT"), ns=128, V=(vs[:, kt, 0:129], ["vs", "vs1"]), masks=masks))
                else:
                    for kt in range(max(0, qt - 4), qt + 1):
                        masks = []
                        if kt == qt:
                            masks.append((ident[:], maskC[:], ["ident", "maskC"]))
                        if kt == qt - 4:
                            masks.append((ident[:], maskW[:], ["ident", "maskW"]))
                        kts.append(dict(kT=(kwT[:, kt * 128:(kt + 1) * 128], "kwT"), ns=128, V=(vw[:, kt, 0:129], ["vw", "vw1"]), masks=masks))
                for gi, hs in enumerate(((0, 1, 2), (3,))):
                    res = attn_qtile(list(hs), qof, kts, 129, pts, (4, 5), (0, 1, 2, 3)[(br * 2 + gi) % 4])
                    for hi, h in enumerate(hs):
                        oap, opk = res[hi]
                        P.add("dve", lambda e, h=h, oap=oap: e.reciprocal(out=rec[:, h:h + 1], in_=oap[:, 128:129]),
                              reads=[opk, "vs1", "vw1"], writes=[("rec", h)])
                        tt(coef[:, h:h + 1], rec[:, h:h + 1], gq[:, 1 + br, h:h + 1], ALU.mult, [("rec", h), "gsig"], [("coef", h)])
                        if br == 0:
                            stt(oacc[:, h, :], oap[:, 0:128], coef[:, h:h + 1], oacc[:, h, :], ALU.mult, ALU.add,
                                [opk, ("coef", h), ("oacc", h)], [("oacc", h)])
                        else:
                            stt(onb[:, h, :], oap[:, 0:128], coef[:, h:h + 1], oacc[:, h, :], ALU.mult, ALU.add,
                                [opk, ("coef", h), ("oacc", h)], [("onb", h)])
            pb7 = ps[7][:].bitcast(BF16)
            for h in range(4):
                tr(pb7[:, h * 128:(h + 1) * 128], onb[:, h, :], ident[:], [("onb", h), "ident"], PS(7))
            evac(oTn[:, :, qsl], pb7[:, 0:512].rearrange("p (h c) -> p h c", c=128), [PS(7)], ["oTn"])
        for h in range(4):
            P.dma("sp", oT_dram[(6 + h) * 128:(7 + h) * 128, :], oTn[:, h, :], reads=["oTn"], writes=[("dram", "oT", 6 + h)])

    def phase_dsa(l):
        P.barrier()
        A.release(0)
        win = wb["w_in"][l]
        wk_in = wkeys("w_in", l)
        qTd = A.bf([128, 6, S])
        kdT = A.bf([128, S])
        vd = A.bf([128, 16, 130])
        ikT = A.bf([128, S])
        iw = A.f32([128, 16, 16])
        m0 = A.mark()
        tabs = load_tables128()
        alloc_ropetmp()
        new_wslots(16 * 512 * 2)
        ckvnT = A.bf([128, 2, S])
        ckvn = [A.bf([128, 256]) for _ in range(2)]
        gB = A.f32([128, 256])
        ssq = A.f32([128, 4])
        junk = A.f32([128, 256])
        kvup = A.bf([128, 2, 256])
        memset(vd[:, :, 128:129], 1.0, ["vd1"])
        P.dma("sp", gB[:, :], small["dsa_kv_norm"][l:l + 1, :].to_broadcast([128, 256]), writes=["gB"])
        P.dma("sp", kvup[:, :, :], wb["dsa_kv_up"][l].rearrange("(k p) c -> p k c", p=128), reads=wkeys("dsa_kv_up", l), writes=["kvup"])
        wt, wk = load_w_cols(win, OFF["dsa_q"], 512, 16, wk_in)
        for h in range(4):
            proj_F(wt, wk, h * 128, 128, lambda pa, pk, tb, h=h: rope128(pa, pk, qTd[:, h, tb * 512:(tb + 1) * 512], "qTd", tb, tabs))
        wt, wk = load_w_cols(win, OFF["dsa_q"] + 512, 512, 16, wk_in)
        for h in range(2):
            proj_F(wt, wk, h * 128, 128, lambda pa, pk, tb, h=h: rope128(pa, pk, qTd[:, 4 + h, tb * 512:(tb + 1) * 512], "qTd", tb, tabs))

        def ckv_epi(pa, pk, t0, g):
            for j in range(g):
                t_ = t0 + j
                i2 = t_ % 2
                pj = pa[:, j * 256:(j + 1) * 256]
                act(junk[:, :], pj, AF.Square, [pk], ["junk", ("ssq", i2)], accum=ssq[:, i2:i2 + 1])
                ts(ssq[:, i2:i2 + 1], ssq[:, i2:i2 + 1], 1.0 / 256.0, 1e-6, ALU.mult, ALU.add, [("ssq", i2)], [("ssq", i2)])
                act(ssq[:, i2:i2 + 1], ssq[:, i2:i2 + 1], AF.Sqrt, [("ssq", i2)], [("ssq", i2)])
                P.add("dve", lambda e, i2=i2: e.reciprocal(out=ssq[:, i2:i2 + 1], in_=ssq[:, i2:i2 + 1]), reads=[("ssq", i2)], writes=[("ssq", i2)])
                stt(ckvn[i2][:, :], pj, ssq[:, i2:i2 + 1], gB[:, :], ALU.mult, ALU.mult, [pk, ("ssq", i2), "gB"], [("ckvn", i2)])
                pb = ps[6 + i2][:].bitcast(BF16)
                for c in range(2):
                    tr(pb[:, c * 128:(c + 1) * 128], ckvn[i2][:, c * 128:(c + 1) * 128], ident[:], [("ckvn", i2), "ident"], PS(6 + i2))
                evac(ckvnT[:, :, t_ * 128:(t_ + 1) * 128], pb[:, 0:256].rearrange("p (c t) -> p c t", t=128), [PS(6 + i2)], ["ckvnT"])

        proj_T(wt, wk, 256, 256, ckv_epi, group=2)
        proj_F(kvup, "kvup", 0, 128, lambda pa, pk, tb: rope128(pa, pk, kdT[:, tb * 512:(tb + 1) * 512], "kdT", tb, tabs),
               actT=ckvnT, akey="ckvnT", nk=2)
        proj_T(kvup, "kvup", 128, 128, lambda pa, pk, t0, g: evac(vd[:, t0:t0 + g, 0:128], pa.rearrange("p (g c) -> p g c", c=128), [pk], ["vd"]),
               actT=ckvnT, akey="ckvnT", nk=2)
        P.barrier()
        A.release(m0)
        iqT = A.bf([128, 8, S])
        m0 = A.mark()
        tabsi = load_tables64()
        alloc_ropetmp()
        new_wslots(16 * 512 * 2)
        wt, wk = load_w_cols(win, OFF["idx_k"], 80, 16, wk_in)
        proj_F(wt, wk, 0, 64, lambda pa, pk, tb: rope64(pa, pk, [ikT[0:64, tb * 512:(tb + 1) * 512], ikT[64:128, tb * 512:(tb + 1) * 512]], "ikT", tb, tabsi))
        proj_T(wt, wk, 64, 16, lambda pa, pk, t0, g: act(iw[:, t0:t0 + g, :], pa.rearrange("p (g c) -> p g c", c=16), AF.Copy, [pk], ["iw"], scale=1.0 / 32.0), group=16)
        for half in range(2):
            wt, wk = load_w_cols(win, OFF["idx_q"] + half * 512, 512, 16, wk_in)
            for hh in range(8):
                h = half * 8 + hh
                proj_F(wt, wk, hh * 64, 64, lambda pa, pk, tb, h=h: rope64(
                    pa, pk, [iqT[64 * (h % 2):64 * (h % 2) + 64, h // 2, tb * 512:(tb + 1) * 512]], "iqT", tb, tabsi))
        P.barrier()
        A.release(m0)
        scA = A.f32([128, S])
        scB = A.f32([128, S])
        rel = [A.f32([128, 512]) for _ in range(2)]
        negm = [A.bf([128, S]) for _ in range(2)]
        m8 = A.f32([128, 8])
        bis = A.f32([128, 8])
        pts = [(A.bf([128, 384]), ("pT", i)) for i in range(2)]
        odb = A.bf([128, 6, 128])
        oTd = [A.bf([128, 6, 128]) for _ in range(2)]
        rec = A.f32([128, 6])
        def dsa_prep(qt):
            qsl = slice(qt * 128, (qt + 1) * 128)
            nk_ = (qt + 1) * 128
            nm = negm[qt % 2]
            nmk = ("negm", qt % 2)
            if qt >= 2:
                for kb in range(0, nk_, 512):
                    n = min(512, nk_ - kb)
                    for h in range(16):
                        b = nextps(0, 2)
                        pb_ = 64 * (h % 2)
                        mm(ps[b][:, 0:n], iqT[pb_:pb_ + 64, h // 2, qsl], ikT[pb_:pb_ + 64, kb:kb + n], True, True, ["iqT", "ikT"], PS(b))
                        r_ = rel[h % 2]
                        act(r_[:, 0:n], ps[b][:, 0:n], AF.Relu, [PS(b)], [("rel", h % 2)])
                        if h == 0:
                            ts(scA[:, kb:kb + n], r_[:, 0:n], iw[:, qt, h:h + 1], None, ALU.mult, None, [("rel", h % 2), "iw"], [("scA", kb)])
                        else:
                            stt(scA[:, kb:kb + n], r_[:, 0:n], iw[:, qt, h:h + 1], scA[:, kb:kb + n], ALU.mult, ALU.add,
                                [("rel", h % 2), "iw", ("scA", kb)], [("scA", kb)])
                sck = [("scA", kb) for kb in range(0, nk_, 512)]
                P.add("dve", lambda e, nk_=nk_: e.tensor_reduce(out=bis[:, 0:1], in_=scA[:, 0:nk_], axis=mybir.AxisListType.X, op=ALU.max),
                      reads=sck, writes=["bis_hi"])
                P.add("dve", lambda e, nk_=nk_: e.tensor_reduce(out=bis[:, 1:2], in_=scA[:, 0:nk_], axis=mybir.AxisListType.X, op=ALU.min),
                      reads=sck, writes=["bis_lo"])
                tt(bis[:, 2:3], bis[:, 0:1], bis[:, 1:2], ALU.subtract, ["bis_hi", "bis_lo"], ["bis_w"])
                tt(scA[:, qt * 128:nk_], scA[:, qt * 128:nk_], negC[:, :], ALU.add, sck + ["negC", "bis_hi", "bis_lo"], sck)
                for it in range(24):
                    ck = float(2.0 ** -(it + 1))
                    ts(bis[:, 3:4], bis[:, 2:3], ck, bis[:, 1:2], ALU.mult, ALU.add, ["bis_w", "bis_lo"], ["bis_mid"])
                    P.add("dve", lambda e, nk_=nk_: e.tensor_scalar(out=scB[:, 0:nk_], in0=scA[:, 0:nk_], scalar1=bis[:, 3:4], scalar2=None,
                                                                    op0=ALU.is_ge, op1=ALU.add, accum_out=bis[:, 4:5]),
                          reads=sck + ["bis_mid"], writes=["scB", "bis_cnt"])
                    ts(bis[:, 5:6], bis[:, 4:5], 256.0, bis[:, 2:3], ALU.is_ge, ALU.mult, ["bis_cnt", "bis_w"], ["bis_tmp"])
                    stt(bis[:, 1:2], bis[:, 5:6], ck, bis[:, 1:2], ALU.mult, ALU.add, ["bis_tmp", "bis_lo"], ["bis_lo"])
                ts(nm[:, 0:nk_], scA[:, 0:nk_], bis[:, 1:2], NEG, ALU.is_lt, ALU.mult, sck + ["bis_lo"], [nmk])

        def dsa_attn(qt):
            qsl = slice(qt * 128, (qt + 1) * 128)
            nk_ = (qt + 1) * 128
            nm = negm[qt % 2]
            nmk = ("negm", qt % 2)
            kts = []
            for kt in range(qt + 1):
                masks = []
                if qt >= 2:
                    masks.append((nm[:, kt * 128:(kt + 1) * 128], ident[:], [nmk, "ident"]))
                if kt == qt:
                    masks.append((ident[:], maskC[:], ["ident", "maskC"]))
                kts.append(dict(kT=(kdT[:, kt * 128:(kt + 1) * 128], "kdT"), ns=128, V=(vd[:, kt, 0:129], ["vd", "vd1"]), masks=masks))
            for gi, hs in enumerate(((0, 1, 2), (3, 4, 5))):
                res = attn_qtile(list(hs), lambda h, qsl=qsl: (qTd[:, h, qsl], "qTd"), kts, 129, pts, (4, 5), 2 + gi)
                for hi, h in enumerate(hs):
                    oap, opk = res[hi]
                    P.add("dve", lambda e, h=h, oap=oap: e.reciprocal(out=rec[:, h:h + 1], in_=oap[:, 128:129]), reads=[opk, "vd1"], writes=[("rec", h)])
                    ts(odb[:, h, :], oap[:, 0:128], rec[:, h:h + 1], None, ALU.mult, None, [opk, ("rec", h)], [("odb", h)])
            ot = oTd[qt % 2]
            otk = ("oTd", qt % 2)
            for half in range(2):
                bnk = 6 + half
                pb = ps[bnk][:].bitcast(BF16)
                for j in range(3):
                    h = half * 3 + j
                    tr(pb[:, j * 128:(j + 1) * 128], odb[:, h, :], ident[:], [("odb", h), "ident"], PS(bnk))
                evac(ot[:, half * 3:half * 3 + 3, :], pb[:, 0:384].rearrange("p (h c) -> p h c", c=128), [PS(bnk)], [otk])
            P.dma("sp", oT_dram[10 * 128:16 * 128, qsl].rearrange("(h d) t -> d h t", d=128), ot[:, :, :], reads=[otk],
                  writes=[("dram", "oT", 10 + qt * 0)])

        dsa_prep(0)
        for qt in range(16):
            if qt + 1 < 16:
                dsa_prep(qt + 1)
            dsa_attn(qt)

    def phase_merge(l):
        P.barrier()
        A.release(0)
        win = wb["w_in"][l]
        wk_in = wkeys("w_in", l)
        oT = A.bf([128, 16, S])
        new_wslots(16 * 512 * 2, n=3)
        sg = [A.f32([128, 512]) for _ in range(3)]
        macc = [A.f32([128, 512]) for _ in range(1)] * 2
        mix = [A.bf([128, S]) for _ in range(1)] * 2
        for h in range(16):
            P.dma("sp", oT[:, h, :], oT_dram[h * 128:(h + 1) * 128, :], reads=[("dram", "oT", min(h, 10))], writes=["oT"])
        brs = (("w_br_fox", 0, 6), ("w_br_nsa", 6, 4), ("w_br_dsa", 10, 6))
        def load_merge(c):
            buf, wk = wslot()
            wt = buf[:, 0:16 * 512].rearrange("p (k c) -> p k c", c=512)
            for bi, (wn, h0, nh) in enumerate(brs):
                P.dma("sp", wt[:, h0:h0 + nh, 0:128], wb[wn][l][:, c * 128:(c + 1) * 128].rearrange("(k p) c -> p k c", p=128),
                      reads=wkeys(wn, l), writes=[wk])
                g0 = OFF["gate"] + bi * 2048 + c * 128
                P.dma("sp", wt[:, :, 128 * (bi + 1):128 * (bi + 2)], win[:, g0:g0 + 128].rearrange("(k p) c -> p k c", p=128),
                      reads=wk_in, writes=[wk])
            return wt, wk

        pend = [load_merge(0), load_merge(1)]
        for c in range(16):
            wt, wk = pend.pop(0)
            if c + 2 < 16:
                pend.append(load_merge(c + 2))
            mx = mix[0]
            mxk = ("mix", 0)
            for tb in range(4):
                tsl = slice(tb * 512, (tb + 1) * 512)
                mk = ("macc", 0)
                ma = macc[0]
                for bi, (wn, h0, nh) in enumerate(brs):
                    bg = nextps(0, 8)
                    for kc in range(16):
                        mm(ps[bg][:, :], wt[:, kc, 128 * (bi + 1):128 * (bi + 2)], hT[:, kc, tsl], kc == 0, kc == 15, [wk] + hkeys(tb * 4, 4), PS(bg))
                    act(sg[bi][:, :], ps[bg][:, :], AF.Sigmoid, [PS(bg)], [("sg", bi)])
                    bo = nextps(0, 8)
                    for j in range(nh):
                        mm(ps[bo][:, :], wt[:, h0 + j, 0:128], oT[:, h0 + j, tsl], j == 0, j == nh - 1, [wk, "oT"], PS(bo))
                    if bi == 0:
                        tt(ma[:, :], ps[bo][:, :], sg[bi][:, :], ALU.mult, [PS(bo), ("sg", bi)], [mk])
                    else:
                        tt(sg[bi][:, :], ps[bo][:, :], sg[bi][:, :], ALU.mult, [PS(bo), ("sg", bi)], [("sg", bi)])
                        if bi == 1:
                            tt(ma[:, :], ma[:, :], sg[bi][:, :], ALU.add, [mk, ("sg", bi)], [mk])
                        else:
                            tt(mx[:, tsl], ma[:, :], sg[bi][:, :], ALU.add, [mk, ("sg", bi)], [mxk])
            P.dma("sp", mixT_dram[c * 128:(c + 1) * 128, :], mx[:, :], reads=[mxk], writes=[("dram", "mixT", c)])

    def gemm_resid(actT, akey, nk, wsrc2d, wkl, ncg, cgw, tiles, hsrc):
        hin = [A.f32([128, cgw]) for _ in range(2)]
        rout = [A.f32([128, cgw]) for _ in range(2)]
        k_ = 0
        depth_ = len(wslots["v"]) - 1
        pend = [load_w_cols(wsrc2d, g * cgw, cgw, nk, wkl) for g in range(min(depth_, ncg))]
        for cg in range(ncg):
            wt, wk = pend.pop(0)
            if cg + depth_ < ncg:
                pend.append(load_w_cols(wsrc2d, (cg + depth_) * cgw, cgw, nk, wkl))
            for t_ in tiles:
                i2 = k_ % 2
                k_ += 1
                P.dma("sp", hin[i2][:, :], hsrc[t_ * 128:(t_ + 1) * 128, cg * cgw:(cg + 1) * cgw], reads=[("dram", "h", t_)], writes=[("hin", i2)])
                b = nextps(0, 8)
                for kc in range(nk):
                    mm(ps[b][:, 0:cgw], actT(kc, t_), wt[:, kc, :], kc == 0, kc == nk - 1, [wk, akey], PS(b))
                stt(rout[i2][:, :], hin[i2][:, :], ALPHA, ps[b][:, 0:cgw], ALU.mult, ALU.add, [("hin", i2), PS(b)], [("rout", i2)])
                P.dma("sp", r_dram[t_ * 128:(t_ + 1) * 128, cg * cgw:(cg + 1) * cgw], rout[i2][:, :], reads=[("rout", i2)], writes=[("dram", "r", t_)])

    def layernorm_pass(l, gname, bname, tiles, hdst, write_y=False):
        gB = A.f32([128, D])
        bB = A.f32([128, D])
        rt = [A.f32([128, D]) for _ in range(2)]
        hb = [A.bf([128, D]) for _ in range(2)]
        st = A.f32([128, 2, 4, 6])
        mv = A.f32([128, 2, 2])
        P.dma("sp", gB[:, :], small[gname][l:l + 1, :].to_broadcast([128, D]), writes=["lnG"])
        P.dma("sp", bB[:, :], small[bname][l:l + 1, :].to_broadcast([128, D]), writes=["lnB"])
        for k_, t_ in enumerate(tiles):
            i2 = k_ % 2
            r_ = rt[i2]
            rk = ("rt", i2)
            P.dma("sp", r_[:, :], r_dram[t_ * 128:(t_ + 1) * 128, :], reads=[("dram", "r", t_)], writes=[rk])
            for c in range(4):
                P.add("dve", lambda e, c=c, r_=r_, i2=i2: e.bn_stats(out=st[:, i2, c, :], in_=r_[:, c * 512:(c + 1) * 512]), reads=[rk], writes=[("st", i2)])
            P.add("dve", lambda e, i2=i2: e.bn_aggr(out=mv[:, i2, :], in_=st[:, i2, :, :].rearrange("p a b -> p (a b)")), reads=[("st", i2)], writes=[("mv", i2)])
            ts(mv[:, i2, 1:2], mv[:, i2, 1:2], 1e-5, None, ALU.add, None, [("mv", i2)], [("mv", i2)])
            act(mv[:, i2, 1:2], mv[:, i2, 1:2], AF.Sqrt, [("mv", i2)], [("mv", i2)])
            P.add("dve", lambda e, i2=i2: e.reciprocal(out=mv[:, i2, 1:2], in_=mv[:, i2, 1:2]), reads=[("mv", i2)], writes=[("mv", i2)])
            ts(r_[:, :], r_[:, :], mv[:, i2, 0:1], mv[:, i2, 1:2], ALU.subtract, ALU.mult, [rk, ("mv", i2)], [rk])
            tt(r_[:, :], r_[:, :], gB[:, :], ALU.mult, [rk, "lnG"], [rk])
            tt(r_[:, :], r_[:, :], bB[:, :], ALU.add, [rk, "lnB"], [rk])
            P.dma("sp", hdst[t_ * 128:(t_ + 1) * 128, :], r_[:, :], reads=[rk], writes=[("dram", "h", t_)])
            act(hb[i2][:, :], r_[:, :], AF.Copy, [rk], [("hb", i2)])
            for q4 in range(4):
                bnk = 4 + (k_ * 4 + q4) % 4
                pb = ps[bnk][:].bitcast(BF16)
                for j in range(4):
                    kc = q4 * 4 + j
                    tr(pb[:, j * 128:(j + 1) * 128], hb[i2][:, kc * 128:(kc + 1) * 128], ident[:], [("hb", i2), "ident"], PS(bnk))
                evac(hT[:, q4 * 4:q4 * 4 + 4, t_ * 128:(t_ + 1) * 128], pb[:, 0:512].rearrange("p (k t) -> p k t", t=128), [PS(bnk)], [("hT", t_)])

    def phase_outproj(l):
        P.barrier()
        A.release(0)
        mixT = A.bf([128, 16, S])
        new_wslots(16 * 512 * 2)
        for c in range(16):
            P.dma("sp", mixT[:, c, :], mixT_dram[c * 128:(c + 1) * 128, :], reads=[("dram", "mixT", c)], writes=["mixT"])
        gemm_resid(lambda kc, t_: mixT[:, kc, t_ * 128:(t_ + 1) * 128], "mixT", 16, wb["w_out"][l], wkeys("w_out", l), 4, 512, range(NT), h_dram)
        P.barrier()
        A.release(0)
        layernorm_pass(l, "ln1_g", "ln1_b", range(NT), h_dram)

    def phase_ffn(l):
        w1 = wb["w_ffn_in"][l]
        w2 = wb["w_ffn_out"][l]
        k1 = wkeys("w_ffn_in", l)
        k2 = wkeys("w_ffn_out", l)
        for tb in range(4):
            P.barrier()
            A.release(0)
            uT = A.bf([128, 44, 512])
            new_wslots(44 * 256 * 2, n=3)
            sa = [A.f32([128, 512]) for _ in range(2)]
            tsl = slice(tb * 512, (tb + 1) * 512)

            def load_ab(c):
                buf, wk = wslot()
                wt = buf[:, 0:16 * 256].rearrange("p (k c) -> p k c", c=256)
                P.dma("sp", wt[:, :, 0:128], w1[:, c * 128:(c + 1) * 128].rearrange("(k p) c -> p k c", p=128), reads=k1, writes=[wk])
                P.dma("sp", wt[:, :, 128:256], w1[:, DFF + c * 128:DFF + (c + 1) * 128].rearrange("(k p) c -> p k c", p=128), reads=k1, writes=[wk])
                return wt, wk

            pend = [load_ab(0), load_ab(1)]
            for c in range(44):
                wt, wk = pend.pop(0)
                if c + 2 < 44:
                    pend.append(load_ab(c + 2))
                ba = nextps(0, 8)
                for kc in range(16):
                    mm(ps[ba][:, :], wt[:, kc, 0:128], hT[:, kc, tsl], kc == 0, kc == 15, [wk] + hkeys(tb * 4, 4), PS(ba))
                bb = nextps(0, 8)
                for kc in range(16):
                    mm(ps[bb][:, :], wt[:, kc, 128:256], hT[:, kc, tsl], kc == 0, kc == 15, [wk] + hkeys(tb * 4, 4), PS(bb))
                pump_cast(1, [wk])
                s_ = sa[c % 2]
                act(s_[:, :], ps[ba][:, :], AF.Silu, [PS(ba)], [("sa", c % 2)])
                tt(uT[:, c, :], s_[:, :], ps[bb][:, :], ALU.mult, [("sa", c % 2), PS(bb)], [("uT", c)])
            P.add("dve", lambda e: e.memset(sa[0][:, 0:1], 0.0), reads=[("uT", c) for c in range(44)] + [("sa", 0)], writes=["uTall", ("sa", 0)])
            gemm_resid(lambda kc, t_: uT[:, kc, (t_ % 4) * 128:(t_ % 4 + 1) * 128], "uTall", 44, w2, k2, 8, 256,
                       range(tb * 4, tb * 4 + 4), h_dram)
        P.barrier()
        A.release(0)
        layernorm_pass(l, "ln2_g", "ln2_b", range(NT), h_dram)

    def to_hT_tile(src, skey, t_, hbuf, hkey, k_):
        act(hbuf[:, :], src, AF.Copy, [skey], [hkey])
        for q4 in range(4):
            bnk = 4 + (k_ * 4 + q4) % 4
            pb = ps[bnk][:].bitcast(BF16)
            for j in range(4):
                kc = q4 * 4 + j
                tr(pb[:, j * 128:(j + 1) * 128], hbuf[:, kc * 128:(kc + 1) * 128], ident[:], [hkey, "ident"], PS(bnk))
            evac(hT[:, q4 * 4:q4 * 4 + 4, t_ * 128:(t_ + 1) * 128], pb[:, 0:512].rearrange("p (k t) -> p k t", t=128),
                 [PS(bnk)], [("hT", t_)])

    def phase_ple(l, last):
        P.barrier()
        A.release(0)
        new_wslots(16 * 512 * 2)
        pT = A.bf([128, 2, S])
        pin = [A.f32([128, 256]) for _ in range(2)]
        pbf = [A.bf([128, 256]) for _ in range(2)]
        wpl = A.bf([128, 2, D])
        hin = [A.f32([128, 512]) for _ in range(2)]
        sg = [A.f32([128, 512]) for _ in range(2)]
        P.dma("sp", wpl[:, :, :], wb["w_ple_in"][l].rearrange("(k p) c -> p k c", p=128), reads=wkeys("w_ple_in", l), writes=["wpl"])
        for t_ in range(NT):
            i2 = t_ % 2
            P.dma("sp", pin[i2][:, :], p_d[l, t_ * 128:(t_ + 1) * 128, :], writes=[("pin", i2)])
            vcopy(pbf[i2][:, :], pin[i2][:, :], [("pin", i2)], [("pbf", i2)])
            pb = ps[6 + i2][:].bitcast(BF16)
            for c in range(2):
                tr(pb[:, c * 128:(c + 1) * 128], pbf[i2][:, c * 128:(c + 1) * 128], ident[:], [("pbf", i2), "ident"], PS(6 + i2))
            evac(pT[:, :, t_ * 128:(t_ + 1) * 128], pb[:, 0:256].rearrange("p (c t) -> p c t", t=128), [PS(6 + i2)], ["pT"])
        wg = wb["w_ple_gate"][l]
        kg = wkeys("w_ple_gate", l)
        k_ = 0
        nxt = load_w_cols(wg, 0, 512, 16, kg)
        for cg in range(4):
            wt, wk = nxt
            if cg + 1 < 4:
                nxt = load_w_cols(wg, (cg + 1) * 512, 512, 16, kg)
            for t_ in range(NT):
                i2 = k_ % 2
                k_ += 1
                P.dma("sp", hin[i2][:, :], h_dram[t_ * 128:(t_ + 1) * 128, cg * 512:(cg + 1) * 512], reads=[("dram", "h", t_)], writes=[("hin", i2)])
                bg = nextps(0, 4)
                for kc in range(16):
                    mm(ps[bg][:, :], hT[:, kc, t_ * 128:(t_ + 1) * 128], wt[:, kc, :], kc == 0, kc == 15, [wk, ("hT", t_)], PS(bg))
                be = nextps(0, 4)
                for kc in range(2):
                    mm(ps[be][:, :], pT[:, kc, t_ * 128:(t_ + 1) * 128], wpl[:, kc, cg * 512:(cg + 1) * 512], kc == 0, kc == 1, ["pT", "wpl"], PS(be))
                act(sg[i2][:, :], ps[bg][:, :], AF.Sigmoid, [PS(bg)], [("sg", i2)])
                tt(sg[i2][:, :], sg[i2][:, :], ps[be][:, :], ALU.mult, [("sg", i2), PS(be)], [("sg", i2)])
                tt(hin[i2][:, :], hin[i2][:, :], sg[i2][:, :], ALU.add, [("hin", i2), ("sg", i2)], [("hin", i2)])
                if last:
                    finals.append(P.dma("sp", y_d[t_ * 128:(t_ + 1) * 128, cg * 512:(cg + 1) * 512], hin[i2][:, :],
                                        reads=[("hin", i2)], writes=[("dram", "y", t_, cg)]))
                else:
                    P.dma("sp", r_dram[t_ * 128:(t_ + 1) * 128, cg * 512:(cg + 1) * 512], hin[i2][:, :], reads=[("hin", i2)],
                          writes=[("dram", "r", t_)])
        if not last:
            P.barrier()
            A.release(0)
            rt = [A.f32([128, D]) for _ in range(2)]
            hb = [A.bf([128, D]) for _ in range(2)]
            for t_ in range(NT):
                i2 = t_ % 2
                P.dma("sp", rt[i2][:, :], r_dram[t_ * 128:(t_ + 1) * 128, :], reads=[("dram", "r", t_)], writes=[("rt", i2)])
                P.dma("sp", h_dram[t_ * 128:(t_ + 1) * 128, :], rt[i2][:, :], reads=[("rt", i2)], writes=[("dram", "h", t_)])
                to_hT_tile(rt[i2][:, :], ("rt", i2), t_, hb[i2], ("hb", i2), t_)

    finals = []
    mark_id = [0]

    def mark(name):
        if not marks:
            return
        mark_id[0] += 1
        n = 1 + mark_id[0]
        marks_out.append((name, n))
        mm(ps[7][0:1, 0:n], ident[0:1, 0:1], ident[0:1, 0:n], True, True, ["ident"], PS(7))

    cast_layer(0)
    pump_cast(10 ** 9)
    A.release(0)
    xin = [A.f32([128, D]) for _ in range(2)]
    xb = [A.bf([128, D]) for _ in range(2)]
    for t_ in range(NT):
        i2 = t_ % 2
        P.dma("sp", xin[i2][:, :], x_d[t_ * 128:(t_ + 1) * 128, :], writes=[("xin", i2)])
        P.dma("sp", h_dram[t_ * 128:(t_ + 1) * 128, :], xin[i2][:, :], reads=[("xin", i2)], writes=[("dram", "h", t_)])
        to_hT_tile(xin[i2][:, :], ("xin", i2), t_, xb[i2], ("xb", i2), t_)

    def dbg_dump(name, src_ap, keys):
        if name in dbg_d:
            P.barrier()
            finals.append(P.dma("sp", dbg_d[name], src_ap, reads=keys, writes=[("dram", "dbg", name)]))

    for l in range(depth):
        if l + 1 < depth:
            cast_layer(l + 1)
        mark("fox%d" % l)
        phase_fox(l)
        mark("nsa%d" % l)
        phase_nsa(l)
        mark("dsa%d" % l)
        phase_dsa(l)
        mark("merge%d" % l)
        if l == 0:
            dbg_dump("oT", oT_dram, [("dram", "oT", h) for h in range(11)])
        phase_merge(l)
        if l == 0:
            dbg_dump("mixT", mixT_dram, [("dram", "mixT", c) for c in range(16)])
        mark("outproj%d" % l)
        phase_outproj(l)
        mark("ffn%d" % l)
        if l == 0:
            dbg_dump("h1", h_dram, [("dram", "h", t) for t in range(NT)])
        phase_ffn(l)
        if l == 0:
            dbg_dump("h2", h_dram, [("dram", "h", t) for t in range(NT)])
        pump_cast(10 ** 9)
        mark("ple%d" % l)
        phase_ple(l, l == depth - 1)
        mark("end%d" % l)
    P.emit(nc, ctx, final_wait_ops=finals)
    return nc, ctx


_CACHE = {}


def kernel(**inputs):
    n_cores = 8
    if "nc" not in _CACHE:
        _CACHE["nc"] = build(DEPTH)
    nc, _ctx = _CACHE["nc"]
    consts = make_consts()
    shared = {}
    for n in WSHAPES:
        shared[n] = np.ascontiguousarray(np.asarray(inputs[n], dtype=np.float32))
    for n in SMALL:
        shared[n] = np.ascontiguousarray(np.asarray(inputs[n], dtype=np.float32))
    for n, v in consts.items():
        shared["c_" + n] = np.ascontiguousarray(v)
    x = np.asarray(inputs["x"], dtype=np.float32)
    p = np.asarray(inputs["p"], dtype=np.float32)
    zeros = {k: np.zeros_like(v) for k, v in shared.items()}
    zx = np.zeros_like(np.ascontiguousarray(x[0]))
    zp = np.zeros_like(np.ascontiguousarray(p[:, 0]))
    in_maps = []
    for c in range(n_cores):
        if c % 2 == 0:
            b = c // 2
            m = dict(shared)
            m["x"] = np.ascontiguousarray(x[b])
            m["p"] = np.ascontiguousarray(p[:, b])
        else:
            m = dict(zeros)
            m["x"] = zx
            m["p"] = zp
        in_maps.append(m)
    res = run_bass_kernel_spmd(nc, in_maps, core_ids=list(range(n_cores)))
    out = np.stack([np.asarray(res.results[2 * b]["y"], dtype=np.float32) for b in range(4)], axis=0)
    return out
```

```python
import numpy as np
import ml_dtypes
from contextlib import ExitStack
import concourse.bass as bass
import concourse.mybir as mybir
from concourse.bass_utils import run_bass_kernel_spmd

F32 = mybir.dt.float32
BF16 = mybir.dt.bfloat16
AF = mybir.ActivationFunctionType
ALU = mybir.AluOpType

D = 2048
S = 2048
NT = 16
DEPTH = 4
DFF = 5632
INC = 11874
PLE = 256
ALPHA = float((2 * DEPTH) ** 0.25)
SC = float(128 ** -0.5)
NEG = -30000.0
OFF = dict(fox_q=0, fox_k=768, fox_v=1536, fox_f=2304, nsa_q=2310, nsa_kc=2822, nsa_vc=2950, nsa_ks=3078,
           nsa_vs=3206, nsa_kw=3334, nsa_vw=3462, nsa_g=3590, dsa_q=3602, dsa_ckv=4370, idx_q=4626,
           idx_k=5650, idx_w=5714, gate=5730)
WSHAPES = dict(w_in=(2048, INC), nsa_cmp_k1=(4096, 256), nsa_cmp_k2=(256, 128), nsa_cmp_v1=(4096, 256),
               nsa_cmp_v2=(256, 128), dsa_kv_up=(256, 256), w_br_fox=(768, 2048), w_br_nsa=(512, 2048),
               w_br_dsa=(768, 2048), w_out=(2048, 2048), w_ffn_in=(2048, 2 * DFF), w_ffn_out=(DFF, 2048),
               w_ple_in=(256, 2048), w_ple_gate=(2048, 2048))
WORDER = ["w_in", "nsa_cmp_k1", "nsa_cmp_k2", "nsa_cmp_v1", "nsa_cmp_v2", "dsa_kv_up", "w_br_fox", "w_br_nsa",
          "w_br_dsa", "w_out", "w_ffn_in", "w_ffn_out", "w_ple_in", "w_ple_gate"]
SMALL = dict(fox_f_bias=(6,), nsa_pe_k=(32, 128), nsa_pe_v=(32, 128), dsa_kv_norm=(256,), ln1_g=(2048,),
             ln1_b=(2048,), ln2_g=(2048,), ln2_b=(2048,))
ENGS = ("pe", "act", "dve", "pool", "sp")


def _lineno():
    import sys
    f = sys._getframe(2)
    out = []
    for _ in range(4):
        if f is None:
            break
        out.append(f.f_lineno)
        f = f.f_back
    return out


class Op:
    __slots__ = ("eng", "fn", "deps", "signal", "tick", "is_dma", "dsem", "dcount", "dprev", "hard", "line")

    def __init__(self, eng, fn, is_dma=False):
        self.eng = eng
        self.fn = fn
        self.deps = set()
        self.signal = False
        self.tick = 0
        self.is_dma = is_dma
        self.dsem = None
        self.dcount = 0
        self.dprev = 0
        self.hard = False


class Res:
    __slots__ = ("last_w", "readers")

    def __init__(self):
        self.last_w = None
        self.readers = []


class Prog:
    N_DMA_SEMS = 14

    def __init__(self):
        self.ops = []
        self.res = {}
        self.last = {e: None for e in ENGS}
        self.dmas_since_bar = []

    def _r(self, key):
        r = self.res.get(key)
        if r is None:
            r = self.res[key] = Res()
        return r

    def add(self, eng, fn, reads=(), writes=(), is_dma=False, nobar=False):
        op = Op(eng, fn, is_dma)
        op.line = _lineno()
        for k in reads:
            r = self._r(k)
            if r.last_w is not None:
                op.deps.add(r.last_w)
        for k in writes:
            r = self._r(k)
            if r.last_w is not None:
                op.deps.add(r.last_w)
            op.deps.update(r.readers)
        for k in reads:
            self._r(k).readers.append(op)
        for k in writes:
            r = self._r(k)
            r.last_w = op
            r.readers = []
        op.deps.discard(op)
        self.ops.append(op)
        self.last[eng] = op
        if is_dma and not nobar:
            self.dmas_since_bar.append(op)
        return op

    def dma(self, q, out, in_, reads=(), writes=(), nobar=False):
        return self.add(q, lambda e: e.dma_start(out=out, in_=in_), reads, writes, is_dma=True, nobar=nobar)

    def barrier(self):
        lasts = [op for op in self.last.values() if op is not None and not op.is_dma]
        dmas = list(self.dmas_since_bar)
        self.dmas_since_bar = []
        for e in ("pe", "act", "dve", "sp"):
            op = Op(e, lambda eng: eng.nop())
            op.hard = True
            op.deps.update(lasts)
            op.deps.update(dmas)
            self.ops.append(op)
            self.last[e] = op
        self.res = {k: v for k, v in self.res.items() if isinstance(k, tuple) and k and k[0] in ("ps", "dram", "wb")}

    def emit(self, nc, ctx, final_wait_ops=()):
        streams = {e: [] for e in ENGS}
        for op in self.ops:
            streams[op.eng].append(op)
        for op in self.ops:
            for d in op.deps:
                if d.is_dma:
                    continue
                if d.eng == op.eng and d.eng == "pe" and not op.hard:
                    continue
                d.signal = True
        for e in ENGS:
            t = 0
            for op in streams[e]:
                if not op.is_dma and op.signal:
                    t += 1
                    op.tick = t
        esem = {e: ctx.enter_context(nc.semaphore("s_" + e)) for e in ENGS}
        dsems = {}
        for q in ENGS:
            if any(op.is_dma for op in streams[q]):
                dsems[q] = [ctx.enter_context(nc.semaphore("d_%s_%d" % (q, i))) for i in range(self.N_DMA_SEMS)]
        for q, sl in dsems.items():
            cnt = [0] * len(sl)
            i = 0
            for op in streams[q]:
                if op.is_dma:
                    j = i % len(sl)
                    op.dsem = (q, j)
                    op.dprev = cnt[j]
                    cnt[j] += 16
                    op.dcount = cnt[j]
                    i += 1
        block = ctx.enter_context(nc.Block())

        def run_stream(ename, eng):
            known = {}
            for op in streams[ename]:
                waits = {}
                for d in op.deps:
                    if d.is_dma:
                        key = ("d",) + d.dsem
                        val = d.dcount
                    else:
                        if d.eng == ename and ename == "pe" and not op.hard:
                            continue
                        key = ("e", d.eng)
                        val = d.tick
                    if waits.get(key, 0) < val:
                        waits[key] = val
                if op.is_dma and op.dprev > 0:
                    key = ("d",) + op.dsem
                    if waits.get(key, 0) < op.dprev:
                        waits[key] = op.dprev
                for key, val in waits.items():
                    if known.get(key, 0) >= val:
                        continue
                    known[key] = val
                    sem = esem[key[1]] if key[0] == "e" else dsems[key[1]][key[2]]
                    eng.wait_ge(sem, val)
                try:
                    ins = op.fn(eng)
                except Exception:
                    print("EMIT FAILED for op recorded at lines", getattr(op, "line", None), flush=True)
                    raise
                if op.is_dma:
                    ins.then_inc(dsems[op.dsem[0]][op.dsem[1]], 16)
                elif op.signal:
                    ins.then_inc(esem[ename], 1)
            if ename == "sp":
                for op in final_wait_ops:
                    eng.wait_ge(dsems[op.dsem[0]][op.dsem[1]], op.dcount)

        @block.tensor
        def _(e):
            run_stream("pe", e)

        @block.scalar
        def _(e):
            run_stream("act", e)

        @block.vector
        def _(e):
            run_stream("dve", e)

        @block.gpsimd
        def _(e):
            run_stream("pool", e)

        @block.sync
        def _(e):
            run_stream("sp", e)


class Arena:
    def __init__(self, t, nwords):
        self.t = t
        self.nwords = nwords
        self.off = 0
        self.n = 0

    def mark(self):
        return self.off

    def release(self, m):
        self.off = m

    def _alloc(self, nbytes):
        nb = (nbytes + 63) // 64 * 64
        o = self.off
        self.off += nb
        assert self.off <= self.nwords * 4, ("arena overflow", self.off, self.nwords * 4)
        return o

    def f32(self, shape):
        n = int(np.prod(shape[1:]))
        o = self._alloc(n * 4)
        v = self.t[:, o // 4: o // 4 + n]
        if len(shape) == 3:
            v = v.rearrange("p (a b) -> p a b", b=shape[2])
        elif len(shape) == 4:
            v = v.rearrange("p (a b c) -> p a b c", b=shape[2], c=shape[3])
        return v

    def bf(self, shape):
        n = int(np.prod(shape[1:]))
        n2 = (n + 1) // 2
        o = self._alloc(n2 * 4)
        v = self.t[:, o // 4: o // 4 + n2].bitcast(BF16)[:, 0:n]
        if len(shape) == 3:
            v = v.rearrange("p (a b) -> p a b", b=shape[2])
        elif len(shape) == 4:
            v = v.rearrange("p (a b c) -> p a b c", b=shape[2], c=shape[3])
        return v


def _rope_np(n, dim):
    inv = (1.0 / (np.float32(10000.0) ** (np.arange(0, dim, 2, dtype=np.float32) / np.float32(dim)))).astype(np.float32)
    ang = np.arange(n, dtype=np.float32)[:, None] * inv[None, :]
    return np.cos(ang).astype(np.float32), np.sin(ang).astype(np.float32)


def make_consts():
    c = {}
    cos, sin = _rope_np(S, 128)
    c["ropeC"] = np.ascontiguousarray(np.concatenate([cos, cos], 1).T)
    c["ropeS"] = np.ascontiguousarray(np.concatenate([-sin, sin], 1).T)
    ci, si = _rope_np(S, 64)
    c["ropeCi"] = np.ascontiguousarray(np.concatenate([ci, ci], 1).T)
    c["ropeSi"] = np.ascontiguousarray(np.concatenate([-si, si], 1).T)
    sp = np.arange(128)[:, None]
    tp = np.arange(128)[None, :]
    c["maskC"] = np.where(sp <= tp, 0.0, NEG).astype(ml_dtypes.bfloat16)
    c["maskW"] = np.where(sp > tp, 0.0, NEG).astype(ml_dtypes.bfloat16)
    cc = np.arange(127)[:, None]
    tt = np.arange(S)[None, :]
    cm = np.zeros((128, S), np.float32)
    cm[:127] = np.where(16 * cc + 31 <= tt, 0.0, NEG)
    c["maskCmp"] = cm.astype(ml_dtypes.bfloat16)
    c_start = np.arange(127) * 16
    c_end = c_start + 31
    s_start = np.arange(32) * 64
    ov = np.maximum(np.minimum(c_end[:, None], s_start[None, :] + 63) - np.maximum(c_start[:, None], s_start[None, :]) + 1, 0) / 32.0
    oe = np.zeros((128, 33), np.float32)
    oe[:127, 0] = 1.0
    oe[:127, 1:] = ov
    c["ovl"] = oe.astype(ml_dtypes.bfloat16)
    pos = np.arange(S)
    blk = np.arange(32)[None, :]
    cur = (pos // 64)[:, None]
    forced = (blk == 0) | (blk == cur) | (blk == cur - 1)
    vis = s_start[None, :] <= pos[:, None]
    mul = (vis & ~forced).astype(np.float32)
    add = np.where(forced, 1e4, np.where(vis, 0.0, -1e4)).astype(np.float32)
    c["selMul"] = np.ascontiguousarray(mul.reshape(16, 128, 32).transpose(1, 0, 2))
    c["selAdd"] = np.ascontiguousarray(add.reshape(16, 128, 32).transpose(1, 0, 2))
    ex = np.zeros((32, 16, 128), np.float32)
    for kt in range(16):
        for s_ in range(128):
            ex[2 * kt + s_ // 64, kt, s_] = 1.0
    c["expand"] = ex.astype(ml_dtypes.bfloat16)
    c["negC"] = np.where(tp.T >= sp.T, 0.0, -1e30).astype(np.float32)
    c["negC"] = np.where(np.arange(128)[None, :] <= np.arange(128)[:, None], 0.0, -1e30).astype(np.float32)
    return c


CONST_SHAPES = dict(ropeC=([128, S], F32), ropeS=([128, S], F32), ropeCi=([64, S], F32), ropeSi=([64, S], F32),
                    maskC=([128, 128], BF16), maskW=([128, 128], BF16), maskCmp=([128, S], BF16),
                    ovl=([128, 33], BF16), selMul=([128, 16, 32], F32), selAdd=([128, 16, 32], F32),
                    expand=([32, 16, 128], BF16), negC=([128, 128], F32))


marks_out = []


def build(depth=DEPTH, dbg=(), marks=False):
    nc = bass.Bass("TRN2", target_bir_lowering=False)
    ctx = ExitStack()
    P = Prog()
    dt_in = {}
    x_d = nc.dram_tensor("x", [S, D], F32, kind="ExternalInput").ap()
    p_d = nc.dram_tensor("p", [DEPTH, S, PLE], F32, kind="ExternalInput").ap()
    wsrc = {n: nc.dram_tensor(n, [DEPTH] + list(s), F32, kind="ExternalInput").ap() for n, s in WSHAPES.items()}
    small = {n: nc.dram_tensor(n, [DEPTH] + list(s), F32, kind="ExternalInput").ap() for n, s in SMALL.items()}
    cst = {n: nc.dram_tensor("c_" + n, s, d, kind="ExternalInput").ap() for n, (s, d) in CONST_SHAPES.items()}
    y_d = nc.dram_tensor("y", [S, D], F32, kind="ExternalOutput").ap()
    dbg_d = {}
    for name, shape, dty in dbg:
        dbg_d[name] = nc.dram_tensor("dbg_" + name, list(shape), dty, kind="ExternalOutput").ap()
    wb = {n: nc.dram_tensor("wb_" + n, [depth] + list(s), BF16, kind="Internal").ap() for n, s in WSHAPES.items()}
    h_dram = nc.dram_tensor("h_dram", [S, D], F32, kind="Internal").ap()
    r_dram = nc.dram_tensor("r_dram", [S, D], F32, kind="Internal").ap()
    oT_dram = nc.dram_tensor("oT_dram", [D, S], BF16, kind="Internal").ap()
    mixT_dram = nc.dram_tensor("mixT_dram", [D, S], BF16, kind="Internal").ap()

    ARENA_WORDS = 32 * 1024
    hT = ctx.enter_context(nc.sbuf_tensor("hT", [128, 16, S], BF16))
    arena_t = ctx.enter_context(nc.sbuf_tensor("arena", [128, ARENA_WORDS], F32))
    A = Arena(arena_t, ARENA_WORDS)
    ident = ctx.enter_context(nc.sbuf_tensor("ident", [128, 128], BF16))
    identf = ctx.enter_context(nc.sbuf_tensor("identf", [128, 128], F32))
    maskC = ctx.enter_context(nc.sbuf_tensor("maskC", [128, 128], BF16))
    maskW = ctx.enter_context(nc.sbuf_tensor("maskW", [128, 128], BF16))
    maskCmp = ctx.enter_context(nc.sbuf_tensor("maskCmp", [128, S], BF16))
    ovl = ctx.enter_context(nc.sbuf_tensor("ovl", [128, 33], BF16))
    selMul = ctx.enter_context(nc.sbuf_tensor("selMul", [128, 16, 32], F32))
    selAdd = ctx.enter_context(nc.sbuf_tensor("selAdd", [128, 16, 32], F32))
    expand = ctx.enter_context(nc.sbuf_tensor("expand", [32, 16, 128], BF16))
    negC = ctx.enter_context(nc.sbuf_tensor("negC", [128, 128], F32))
    onesb = ctx.enter_context(nc.sbuf_tensor("onesb", [128, 16], BF16))
    ps = [ctx.enter_context(nc.psum_tensor("ps%d" % i, [128, 512], F32)) for i in range(8)]

    def PS(i):
        return ("ps", i)

    cnt = {"ev": 0, "ps": 0, "w": 0}

    def mm(out, lhsT, rhs, start, stop, reads, pskey):
        P.add("pe", lambda e: e.matmul(out, lhsT, rhs, start=start, stop=stop), reads=reads, writes=[pskey])

    def tr(out, in_, idn, reads, pskey):
        P.add("pe", lambda e: e.transpose(out=out, in_=in_, identity=idn), reads=reads, writes=[pskey])

    def act(out, in_, func, reads, writes, bias=0.0, scale=1.0, accum=None):
        if accum is None:
            P.add("act", lambda e: e.activation(out=out, in_=in_, func=func, bias=bias, scale=scale), reads=reads, writes=writes)
        else:
            P.add("act", lambda e: e.activation(out=out, in_=in_, func=func, bias=bias, scale=scale, accum_out=accum), reads=reads, writes=writes)

    def tt(out, in0, in1, op, reads, writes):
        P.add("dve", lambda e: e.tensor_tensor(out=out, in0=in0, in1=in1, op=op), reads=reads, writes=writes)

    def ts(out, in0, s1, s2, op0, op1, reads, writes):
        if s2 is None:
            P.add("dve", lambda e: e.tensor_scalar(out=out, in0=in0, scalar1=s1, scalar2=None, op0=op0), reads=reads, writes=writes)
        else:
            P.add("dve", lambda e: e.tensor_scalar(out=out, in0=in0, scalar1=s1, scalar2=s2, op0=op0, op1=op1), reads=reads, writes=writes)

    def stt(out, in0, scalar, in1, op0, op1, reads, writes):
        P.add("dve", lambda e: e.scalar_tensor_tensor(out=out, in0=in0, scalar=scalar, in1=in1, op0=op0, op1=op1), reads=reads, writes=writes)

    def vcopy(out, in_, reads, writes):
        P.add("dve", lambda e: e.tensor_copy(out=out, in_=in_), reads=reads, writes=writes)

    def acopy(out, in_, reads, writes):
        act(out, in_, AF.Copy, reads, writes)

    def evac(out, in_, reads, writes):
        cnt["ev"] += 1
        if cnt["ev"] % 2:
            acopy(out, in_, reads, writes)
        else:
            vcopy(out, in_, reads, writes)

    def memset(ap, val, writes, eng="dve"):
        P.add(eng, lambda e: e.memset(ap, val), writes=writes)

    def nextps(lo=0, n=4):
        cnt["ps"] += 1
        return lo + cnt["ps"] % n

    def hkeys(t0, n):
        return [("hT", t) for t in range(t0, t0 + n)]

    def wkeys(name, l):
        return [("wb", name, l, c) for c in range(nchunks[name])]

    for t_, n_ in ((maskC, "maskC"), (maskW, "maskW"), (maskCmp, "maskCmp"), (ovl, "ovl"), (selMul, "selMul"),
                   (selAdd, "selAdd"), (expand, "expand"), (negC, "negC")):
        P.dma("sp", t_[:], cst[n_], writes=[n_])
    memset(ident[:], 1.0, ["ident"], eng="pool")
    P.add("pool", lambda e: e.affine_select(out=ident[:], in_=ident[:], pattern=[[-1, 128]], compare_op=ALU.is_equal,
                                            fill=0.0, base=0, channel_multiplier=1), reads=["ident"], writes=["ident"])
    memset(identf[:], 1.0, ["identf"], eng="pool")
    P.add("pool", lambda e: e.affine_select(out=identf[:], in_=identf[:], pattern=[[-1, 128]], compare_op=ALU.is_equal,
                                            fill=0.0, base=0, channel_multiplier=1), reads=["identf"], writes=["identf"])
    memset(onesb[:], 1.0, ["onesb"], eng="pool")

    CW = 4096
    nchunks = {}
    for n, (R_, C_) in WSHAPES.items():
        m_ = R_ * C_ // 128
        nchunks[n] = (m_ + CW - 1) // CW
    cast_q = []

    def cast_layer(l):
        for n in WORDER:
            R_, C_ = WSHAPES[n]
            m_ = R_ * C_ // 128
            src = wsrc[n][l].rearrange("(p a) c -> p (a c)", p=128)
            dst = wb[n][l].rearrange("(p a) c -> p (a c)", p=128)
            for c in range(nchunks[n]):
                c0 = c * CW
                c1 = min(m_, c0 + CW)
                cast_q.append((dst[:, c0:c1], src[:, c0:c1], ("wb", n, l, c)))

    def pump_cast(k, dep_keys=()):
        for _ in range(min(k, len(cast_q))):
            d_, s_, key = cast_q.pop(0)
            P.dma("pool", d_, s_, reads=list(dep_keys), writes=[key], nobar=True)

    wslots = {}

    def new_wslots(nbytes, n=2):
        wslots["v"] = [A.bf([128, nbytes // 2]) for _ in range(n)]
        wslots["i"] = 0

    def wslot():
        i = wslots["i"] % len(wslots["v"])
        wslots["i"] += 1
        return wslots["v"][i], ("wslot", i)

    def load_w_cols(src2d, c0, ncols, nk, keys, dst=None, dkey=None, coff=0):
        if dst is None:
            buf, dkey = wslot()
            dst = buf[:, 0:nk * ncols].rearrange("p (k c) -> p k c", c=ncols)
            P.dma("sp", dst, src2d[:, c0:c0 + ncols].rearrange("(k p) c -> p k c", p=128), reads=keys, writes=[dkey])
        else:
            P.dma("sp", dst[:, :, coff:coff + ncols], src2d[:, c0:c0 + ncols].rearrange("(k p) c -> p k c", p=128),
                  reads=keys, writes=[dkey])
        return dst, dkey

    def rope128(psap, pk, dst, dkey, tb, tabs, n=512, pos=None):
        C_, S_, tk = tabs
        if pos is None:
            cs = C_[:, tb * 512: tb * 512 + n]
            ss_lo = S_[0:64, tb * 512: tb * 512 + n]
            ss_hi = S_[64:128, tb * 512: tb * 512 + n]
        else:
            cs, ss_lo, ss_hi = pos
        ta = ropetmp["a"][cnt["ev"] % 2]
        tbm = ropetmp["b"][cnt["ev"] % 2]
        ka = ("ropeA", cnt["ev"] % 2)
        kb = ("ropeB", cnt["ev"] % 2)
        cnt["ev"] += 1
        tt(ta[:, 0:n], psap, cs, ALU.mult, [pk, tk], [ka])
        tt(tbm[0:64, 0:n], psap[64:128, :], ss_lo, ALU.mult, [pk, tk], [kb])
        tt(tbm[64:128, 0:n], psap[0:64, :], ss_hi, ALU.mult, [pk, tk], [kb])
        tt(dst, ta[:, 0:n], tbm[:, 0:n], ALU.add, [ka, kb], [dkey])

    def rope64(psap, pk, dsts, dkey, tb, tabs, n=512):
        C_, S_, tk = tabs
        ta = ropetmp["a"][cnt["ev"] % 2]
        tbm = ropetmp["b"][cnt["ev"] % 2]
        ka = ("ropeA", cnt["ev"] % 2)
        kb = ("ropeB", cnt["ev"] % 2)
        cnt["ev"] += 1
        sl = slice(tb * 512, tb * 512 + n)
        tt(ta[0:64, 0:n], psap, C_[0:64, sl], ALU.mult, [pk, tk], [ka])
        tt(tbm[0:32, 0:n], psap[32:64, :], S_[0:32, sl], ALU.mult, [pk, tk], [kb])
        tt(tbm[32:64, 0:n], psap[0:32, :], S_[32:64, sl], ALU.mult, [pk, tk], [kb])
        for dst in dsts:
            tt(dst, ta[0:64, 0:n], tbm[0:64, 0:n], ALU.add, [ka, kb], [dkey])

    ropetmp = {}

    def alloc_ropetmp():
        ropetmp["a"] = [A.f32([128, 512]) for _ in range(2)]
        ropetmp["b"] = [A.f32([128, 512]) for _ in range(2)]

    def proj_F(wt, wk, c0, M, epi, actT=None, akey="hT", nk=16, base_ps=0):
        src = hT if actT is None else actT
        for tb in range(4):
            b = nextps()
            for kc in range(nk):
                mm(ps[b][0:M, :], wt[:, kc, c0:c0 + M], src[:, kc, tb * 512:(tb + 1) * 512], kc == 0, kc == nk - 1,
                   [wk] + (hkeys(tb * 4, 4) if akey == "hT" else [akey]), PS(b))
            epi(ps[b][0:M, :], PS(b), tb)

    def proj_T(wt, wk, c0, N, epi, actT=None, akey="hT", nk=16, group=None):
        src = hT if actT is None else actT
        if group is None:
            group = max(1, 512 // N)
        for t0 in range(0, NT, group):
            b = nextps()
            g = min(group, NT - t0)
            for j in range(g):
                tt_ = t0 + j
                for kc in range(nk):
                    mm(ps[b][:, j * N:(j + 1) * N], src[:, kc, tt_ * 128:(tt_ + 1) * 128], wt[:, kc, c0:c0 + N],
                       kc == 0, kc == nk - 1, [wk] + (hkeys(tt_, 1) if akey == "hT" else [akey]), PS(b))
            epi(ps[b][:, 0:g * N], PS(b), t0, g)

    def attn_qtile(heads, qT_of, ktiles, ncol, pT, skey_banks, obanks, exp_bias=None):
        G = len(heads)
        ob = obanks
        nkt = len(ktiles)

        def scores(ki):
            kt = ktiles[ki]
            sb = skey_banks[ki % 2]
            ns = kt["ns"]
            kT, kkey = kt["kT"]
            for hi, h in enumerate(heads):
                q, qk = qT_of(h)
                nm = len(kt["masks"])
                mm(ps[sb][0:ns, hi * 128:(hi + 1) * 128], kT, q, hi == 0, nm == 0 and hi == G - 1, [kkey, qk], PS(sb))
                for mi, (ml, mr, mk) in enumerate(kt["masks"]):
                    mm(ps[sb][0:ns, hi * 128:(hi + 1) * 128], ml, mr, False, mi == nm - 1 and hi == G - 1, mk, PS(sb))
            pt, ptk = pT[ki % 2]
            if exp_bias is None:
                act(pt[0:ns, 0:G * 128], ps[sb][0:ns, 0:G * 128], AF.Exp, [PS(sb)], [ptk], scale=SC)
            else:
                for hi, h in enumerate(heads):
                    bap, bk = exp_bias(h, kt)
                    act(pt[0:ns, hi * 128:(hi + 1) * 128], ps[sb][0:ns, hi * 128:(hi + 1) * 128], AF.Exp,
                        [PS(sb), bk], [ptk], scale=SC, bias=bap)

        def pv(ki):
            kt = ktiles[ki]
            ns = kt["ns"]
            pt, ptk = pT[ki % 2]
            V, vk = kt["V"]
            for hi, h in enumerate(heads):
                mm(ps[ob][:, hi * ncol:(hi + 1) * ncol], pt[0:ns, hi * 128:(hi + 1) * 128], V, ki == 0 and hi == 0,
                   ki == nkt - 1 and hi == G - 1, [ptk] + list(vk), PS(ob))

        for ki in range(nkt + 1):
            if ki < nkt:
                scores(ki)
            if ki >= 1:
                pv(ki - 1)
        return [(ps[ob][:, hi * ncol:(hi + 1) * ncol], PS(ob)) for hi in range(G)]

    def transposes_to(dstT, dkey, src_tok, skey, nblk, tbank):
        pb = ps[tbank][:].bitcast(BF16)
        for j in range(nblk):
            tr(pb[:, j * 128:(j + 1) * 128], src_tok[:, j * 128:(j + 1) * 128], ident[:], [skey, "ident"], PS(tbank))
        for j in range(nblk):
            evac(dstT(j), pb[:, j * 128:(j + 1) * 128], [PS(tbank)], [dkey])

    def load_tables128():
        C_ = A.f32([128, S])
        S_ = A.f32([128, S])
        P.dma("sp", C_, cst["ropeC"], writes=["tabC"])
        P.dma("sp", S_, cst["ropeS"], writes=["tabC"])
        return (C_, S_, "tabC")

    def load_tables64():
        C_ = A.f32([128, S])
        S_ = A.f32([128, S])
        P.dma("sp", C_[0:64, :], cst["ropeCi"], writes=["tabI"])
        P.dma("sp", S_[0:64, :], cst["ropeSi"], writes=["tabI"])
        return (C_, S_, "tabI")

    def phase_fox(l):
        A.release(0)
        win = wb["w_in"][l]
        wk_in = wkeys("w_in", l)
        new_wslots(16 * 384 * 2)
        wf = A.bf([128, 16, 8])
        lf = A.f32([128, S])
        cp = A.f32([128, S])
        onesr = A.f32([128, S])
        negb = A.f32([128, 2])
        ctok = A.f32([128, 96])
        bsh = A.f32([128, 96])
        rbd = A.f32([128, 96])
        bias = A.f32([128, 6, 16, 16])
        qTh = A.bf([128, S])
        kTh = A.bf([128, S])
        vh = A.bf([128, 16, 130])
        oTh = [A.bf([128, S]) for _ in range(2)]
        pts = [(A.bf([128, 128]), ("pT", i)) for i in range(2)]
        otok = [A.bf([128, 128]) for _ in range(2)]
        rec = A.f32([128, 2])
        memset(onesr[0:8, :], 1.0, ["onesr"])
        memset(vh[:, :, 128:129], 1.0, ["vh1"])
        P.dma("sp", wf[:, :, 0:6], win[:, OFF["fox_f"]:OFF["fox_f"] + 6].rearrange("(k p) c -> p k c", p=128),
              reads=wk_in, writes=["wf"])
        P.dma("sp", negb[0:6, 0:1], small["fox_f_bias"][l:l + 1, :].rearrange("o h -> h o"), writes=["negb"])
        ts(negb[0:6, 1:2], negb[0:6, 0:1], -1.0, None, ALU.mult, None, ["negb"], ["negb2"])
        for tb in range(4):
            b = nextps()
            for kc in range(16):
                mm(ps[b][0:6, :], wf[:, kc, 0:6], hT[:, kc, tb * 512:(tb + 1) * 512], kc == 0, kc == 15, ["wf"] + hkeys(tb * 4, 4), PS(b))
            sl = slice(tb * 512, (tb + 1) * 512)
            act(lf[0:6, sl], ps[b][0:6, :], AF.Exp, [PS(b), "negb2"], [("lf", tb)], bias=negb[0:6, 1:2], scale=-1.0)
            act(lf[0:6, sl], lf[0:6, sl], AF.Ln, [("lf", tb)], [("lf", tb)], bias=1.0, scale=1.0)
        P.add("dve", lambda e: e.tensor_tensor_scan(out=cp[0:6, :], data0=onesr[0:6, :], data1=lf[0:6, :], initial=0.0,
                                                    op0=ALU.mult, op1=ALU.add),
              reads=[("lf", t) for t in range(4)] + ["onesr"], writes=["cp"])
        b = nextps()
        for kt in range(16):
            tr(ps[b][:, kt * 6:(kt + 1) * 6], cp[0:6, kt * 128:(kt + 1) * 128], identf[0:6, 0:6], ["cp", "identf"], PS(b))
        vcopy(ctok[:, :], ps[b][:, 0:96], [PS(b)], ["ctok"])
        for h in range(6):
            ts(rbd[0:6, h * 16:(h + 1) * 16], cp[0:6, 127:S:128], identf[0:6, h:h + 1], None, ALU.mult, None,
               ["cp", "identf"], ["rbd"])
        b = nextps()
        mm(ps[b][:, 0:96], onesr[0:6, 0:128], rbd[0:6, 0:96], True, True, ["onesr", "rbd"], PS(b))
        vcopy(bsh[:, :], ps[b][:, 0:96], [PS(b)], ["bsh"])
        ctv = ctok[:, :].rearrange("p (k h) -> p h k", h=6)
        for h in range(6):
            for qt in range(16):
                ts(bias[:, h, qt, :], ctv[:, h, :], bsh[:, h * 16 + qt:h * 16 + qt + 1], None, ALU.subtract, None,
                   ["ctok", "bsh"], [("bias", h)])
        for h in range(6):
            wt, wk = wslot()
            wt = wt[:, 0:16 * 384].rearrange("p (k c) -> p k c", c=384)
            for j, seg in enumerate(("fox_q", "fox_k", "fox_v")):
                c0 = OFF[seg] + h * 128
                P.dma("sp", wt[:, :, j * 128:(j + 1) * 128], win[:, c0:c0 + 128].rearrange("(k p) c -> p k c", p=128),
                      reads=wk_in, writes=[wk])
            proj_F(wt, wk, 0, 128, lambda pa, pk, tb: evac(qTh[:, tb * 512:(tb + 1) * 512], pa, [pk], ["qTh"]))
            proj_F(wt, wk, 128, 128, lambda pa, pk, tb: evac(kTh[:, tb * 512:(tb + 1) * 512], pa, [pk], ["kTh"]))
            proj_T(wt, wk, 256, 128, lambda pa, pk, t0, g: evac(vh[:, t0:t0 + g, 0:128], pa.rearrange("p (g c) -> p g c", c=128), [pk], ["vh"]))
            oT = oTh[h % 2]
            ok_ = ("oTh", h % 2)
            for qt in range(16):
                kts = []
                for kt in range(qt + 1):
                    masks = []
                    if kt == qt:
                        masks.append((ident[:], maskC[:], ["ident", "maskC"]))
                    kts.append(dict(kT=(kTh[:, kt * 128:(kt + 1) * 128], "kTh"), ns=128,
                                    V=(vh[:, kt, 0:129], ["vh", "vh1"]), masks=masks, kt=kt))
                res = attn_qtile([h], lambda hh: (qTh[:, qt * 128:(qt + 1) * 128], "qTh"), kts, 129, pts, (4, 5),
                                 2 + qt % 2, exp_bias=lambda hh, k, qt=qt: (bias[:, hh, qt, k["kt"]:k["kt"] + 1], ("bias", hh)))
                oap, opk = res[0]
                ri = qt % 2
                P.add("dve", lambda e, ri=ri, oap=oap: e.reciprocal(out=rec[:, ri:ri + 1], in_=oap[:, 128:129]),
                      reads=[opk, "vh1"], writes=[("rec", ri)])
                ts(otok[ri][:, :], oap[:, 0:128], rec[:, ri:ri + 1], None, ALU.mult, None, [opk, ("rec", ri)], [("otok", ri)])
                transposes_to(lambda j, qt=qt, oT=oT: oT[:, qt * 128:(qt + 1) * 128], ok_, otok[ri], ("otok", ri), 1, 6 + qt % 2)
                pump_cast(1, [("otok", ri)])
            P.dma("sp", oT_dram[h * 128:(h + 1) * 128, :], oT[:, :], reads=[ok_], writes=[("dram", "oT", h)])

    def phase_nsa(l):
        P.barrier()
        A.release(0)
        win = wb["w_in"][l]
        wk_in = wkeys("w_in", l)
        qTn = A.bf([128, 4, S])
        ksT = A.bf([128, S])
        kwT = A.bf([128, S])
        vs = A.bf([128, 16, 130])
        vw = A.bf([128, 16, 130])
        kcmpT = A.bf([128, 128])
        vcx = A.bf([128, 162])
        gsig = A.f32([128, 16, 12])
        oTn = A.bf([128, 4, S])
        m_stage = A.mark()
        tabs = load_tables128()
        alloc_ropetmp()
        new_wslots(16 * 512 * 2)
        rawT = [A.bf([128, S]) for _ in range(2)]
        peT = A.bf([128, 2, 32])
        pe32 = A.f32([32, 2, 128])
        w2 = A.bf([128, 2, 2, 128])
        h1x = A.f32([128, 256])
        h1a = A.f32([128, 256])
        gT = A.bf([128, 2, 2, 128])
        memset(vs[:, :, 128:129], 1.0, ["vs1"])
        memset(vw[:, :, 128:129], 1.0, ["vw1"])
        vcopy(vcx[:, 128:161], ovl[:, :], ["ovl"], ["vcx1"])
        wt, wk = load_w_cols(win, OFF["nsa_g"], 12, 16, wk_in)
        proj_T(wt, wk, 0, 12, lambda pa, pk, t0, g: act(gsig[:, t0:t0 + g, :], pa.rearrange("p (g c) -> p g c", c=12), AF.Sigmoid, [pk], ["gsig"]), group=16)
        wt, wk = load_w_cols(win, OFF["nsa_q"], 512, 16, wk_in)
        for h in range(4):
            proj_F(wt, wk, h * 128, 128, lambda pa, pk, tb, h=h: rope128(pa, pk, qTn[:, h, tb * 512:(tb + 1) * 512], "qTn", tb, tabs))
        wt, wk = load_w_cols(win, OFF["nsa_kc"], 512, 16, wk_in)
        proj_F(wt, wk, 0, 128, lambda pa, pk, tb: evac(rawT[0][:, tb * 512:(tb + 1) * 512], pa, [pk], [("rawT", 0)]))
        proj_F(wt, wk, 128, 128, lambda pa, pk, tb: evac(rawT[1][:, tb * 512:(tb + 1) * 512], pa, [pk], [("rawT", 1)]))
        proj_F(wt, wk, 256, 128, lambda pa, pk, tb: rope128(pa, pk, ksT[:, tb * 512:(tb + 1) * 512], "ksT", tb, tabs))
        proj_T(wt, wk, 384, 128, lambda pa, pk, t0, g: evac(vs[:, t0:t0 + g, 0:128], pa.rearrange("p (g c) -> p g c", c=128), [pk], ["vs"]))
        wt, wk = load_w_cols(win, OFF["nsa_kw"], 256, 16, wk_in)
        proj_F(wt, wk, 0, 128, lambda pa, pk, tb: rope128(pa, pk, kwT[:, tb * 512:(tb + 1) * 512], "kwT", tb, tabs))
        proj_T(wt, wk, 128, 128, lambda pa, pk, t0, g: evac(vw[:, t0:t0 + g, 0:128], pa.rearrange("p (g c) -> p g c", c=128), [pk], ["vw"]))
        for which, (pen, w1n, w2n) in enumerate((("nsa_pe_k", "nsa_cmp_k1", "nsa_cmp_k2"), ("nsa_pe_v", "nsa_cmp_v1", "nsa_cmp_v2"))):
            P.dma("sp", pe32[0:32, which, :], small[pen][l], writes=[("pe32", which)])
            b = nextps()
            tr(ps[b][:, 0:32], pe32[0:32, which, :], identf[0:32, 0:32], [("pe32", which), "identf"], PS(b))
            vcopy(peT[:, which, :], ps[b][:, 0:32], [PS(b)], [("peT", which)])
            P.dma("sp", w2[:, which, :, :], wb[w2n][l].rearrange("(k p) c -> p k c", p=128), reads=wkeys(w2n, l), writes=[("w2", which)])
            buf, wk1 = wslot()
            w1 = buf[:, 0:32 * 256].rearrange("p (l c) -> p l c", c=256)
            P.dma("sp", w1, wb[w1n][l].rearrange("(l p) c -> p l c", p=128), reads=wkeys(w1n, l), writes=[wk1])
            b = nextps()
            for j in range(2):
                for li in range(32):
                    mm(ps[b][:, j * 128:j * 128 + 127], w1[:, li, j * 128:(j + 1) * 128], rawT[which][:, li:li + 16 * 126 + 1:16],
                       li == 0, False, [wk1, ("rawT", which)], PS(b))
                    mm(ps[b][:, j * 128:j * 128 + 127], w1[:, li, j * 128:(j + 1) * 128],
                       peT[:, which, li:li + 1].to_broadcast([128, 127]), False, li == 31, [wk1, ("peT", which)], PS(b))
            hx = h1x[:, :].rearrange("p (j c) -> p j c", c=128)[:, :, 0:127]
            ha = h1a[:, :].rearrange("p (j c) -> p j c", c=128)[:, :, 0:127]
            pv = ps[b][:, 0:256].rearrange("p (j c) -> p j c", c=128)[:, :, 0:127]
            acopy(hx, pv, [PS(b)], ["h1x"])
            tt(ha, hx, hx, ALU.mult, ["h1x"], ["h1a"])
            ts(ha, ha, 0.044715, 1.0, ALU.mult, ALU.add, ["h1a"], ["h1a"])
            tt(ha, ha, hx, ALU.mult, ["h1a", "h1x"], ["h1a"])
            act(ha, ha, AF.Tanh, ["h1a"], ["h1a"], scale=0.7978845608028654)
            stt(ha, ha, 1.0, hx, ALU.add, ALU.mult, ["h1a", "h1x"], ["h1a"])
            ts(gT[:, which, :, 0:127], ha, 0.5, None, ALU.mult, None, ["h1a"], [("gT", which)])
            b = nextps()
            if which == 0:
                for j in range(2):
                    mm(ps[b][:, 0:127], w2[:, 0, j, :], gT[:, 0, j, 0:127], j == 0, j == 1, [("w2", 0), ("gT", 0)], PS(b))
                C_, S_, tk = tabs
                rope128(ps[b][:, 0:127], PS(b), kcmpT[:, 0:127], "kcmpT", 0, tabs, n=127,
                        pos=(C_[:, 31:S:16], S_[0:64, 31:S:16], S_[64:128, 31:S:16]))
            else:
                for j in range(2):
                    mm(ps[b][0:127, 0:128], gT[:, 1, j, 0:127], w2[:, 1, j, :], j == 0, j == 1, [("w2", 1), ("gT", 1)], PS(b))
                evac(vcx[0:127, 0:128], ps[b][0:127, 0:128], [PS(b)], ["vcx"])
        P.barrier()
        A.release(m_stage)
        pts = [(A.bf([128, 512]), ("pT", i)) for i in range(2)]
        oacc = A.f32([128, 4, 128])
        onb = A.bf([128, 4, 128])
        rec = A.f32([128, 4])
        coef = A.f32([128, 4])
        imp = A.f32([128, 32])
        imp2 = A.f32([128, 32])
        m8 = A.f32([128, 16])
        negsel = A.bf([128, 32])
        negselT = A.bf([32, 128])
        vkeys = ["vcx", "vcx1"]
        for qt in range(16):
            qsl = slice(qt * 128, (qt + 1) * 128)

            def qof(h, qsl=qsl):
                return (qTn[:, h, qsl], "qTn")

            gq = gsig[:, qt, :].rearrange("p (h c) -> p c h", c=3)
            kts = [dict(kT=(kcmpT[:, 0:127], "kcmpT"), ns=127, V=(vcx[0:127, 0:161], ["vcx", "vcx1"]),
                        masks=[(ident[0:127, 0:127], maskCmp[0:127, qsl], ["ident", "maskCmp"])])]
            for gi, hs in enumerate(((0, 1, 2), (3,))):
                res = attn_qtile(list(hs), qof, kts, 161, pts, (4, 5), 2 + gi)
                for hi, h in enumerate(hs):
                    oap, opk = res[hi]
                    ts(rec[:, h:h + 1], oap[:, 128:129], 1e-30, None, ALU.max, None, [opk, "vcx1"], [("rec", h)])
                    P.add("dve", lambda e, h=h: e.reciprocal(out=rec[:, h:h + 1], in_=rec[:, h:h + 1]), reads=[("rec", h)], writes=[("rec", h)])
                    tt(coef[:, h:h + 1], rec[:, h:h + 1], gq[:, 0, h:h + 1], ALU.mult, [("rec", h), "gsig"], [("coef", h)])
                    ts(oacc[:, h, :], oap[:, 0:128], coef[:, h:h + 1], None, ALU.mult, None, [opk, ("coef", h)], [("oacc", h)])
                    if h == 0:
                        ts(imp[:, :], oap[:, 129:161], rec[:, h:h + 1], None, ALU.mult, None, [opk, ("rec", h)], ["imp"])
                    else:
                        stt(imp[:, :], oap[:, 129:161], rec[:, h:h + 1], imp[:, :], ALU.mult, ALU.add, [opk, ("rec", h), "imp"], ["imp"])
            tt(imp2[:, :], imp[:, :], selMul[:, qt, :], ALU.mult, ["imp", "selMul"], ["imp2"])
            tt(imp2[:, :], imp2[:, :], selAdd[:, qt, :], ALU.add, ["imp2", "selAdd"], ["imp2"])
            P.add("dve", lambda e: e.max(out=m8[:, 0:8], in_=imp2[:, :]), reads=["imp2"], writes=["m8"])
            P.add("dve", lambda e: e.match_replace(out=imp[:, :], in_to_replace=m8[:, 0:8], in_values=imp2[:, :], imm_value=-1e30),
                  reads=["imp2", "m8"], writes=["imp"])
            P.add("dve", lambda e: e.max(out=m8[:, 8:16], in_=imp[:, :]), reads=["imp"], writes=["m8b"])
            ts(negsel[:, :], imp2[:, :], m8[:, 15:16], NEG, ALU.is_lt, ALU.mult, ["imp2", "m8b"], ["negsel"])
            pb = ps[6][:].bitcast(BF16)
            tr(pb[0:32, 0:128], negsel[:, :], ident[:], ["negsel", "ident"], PS(6))
            vcopy(negselT[0:32, :], pb[0:32, 0:128], [PS(6)], ["negselT"])
            for br in range(2):
                kts = []
                if br == 0:
                    for kt in range(qt + 1):
                        masks = [(expand[:, kt, :], negselT[0:32, :], ["expand", "negselT"])]
                        if kt == qt:
                            masks.append((ident[:], maskC[:], ["ident", "maskC"]))
                        kts.append(dict(kT=(ksT[:, kt * 128:(kt + 1) * 128], "ksT"), ns=128, V=(vs[:, kt, 0:129], ["vs", "vs1"]), masks=masks))
                else:
                    for kt in range(max(0, qt - 4), qt + 1):
                        masks = []
                        if kt == qt:
                            masks.append((ident[:], maskC[:], ["ident", "maskC"]))
                        if kt == qt - 4:
                            masks.append((ident[:], maskW[:], ["ident", "maskW"]))
                        kts.append(dict(kT=(kwT[:, kt * 128:(kt + 1) * 128], "kwT"), ns=128, V=(vw[:, kt, 0:129], ["vw", "vw1"]), masks=masks))
                for gi, hs in enumerate(((0, 1, 2), (3,))):
                    res = attn_qtile(list(hs), qof, kts, 129, pts, (4, 5), (0, 1, 2, 3)[(br * 2 + gi) % 4])
                    for hi, h in enumerate(hs):
                        oap, opk = res[hi]
                        P.add("dve", lambda e, h=h, oap=oap: e.reciprocal(out=rec[:, h:h + 1], in_=oap[:, 128:129]),
                              reads=[opk, "vs1", "vw1"], writes=[("rec", h)])
                        tt(coef[:, h:h + 1], rec[:, h:h + 1], gq[:, 1 + br, h:h + 1], ALU.mult, [("rec", h), "gsig"], [("coef", h)])
                        if br == 0:
                            stt(oacc[:, h, :], oap[:, 0:128], coef[:, h:h + 1], oacc[:, h, :], ALU.mult, ALU.add,
                                [opk, ("coef", h), ("oacc", h)], [("oacc", h)])
                        else:
                            stt(onb[:, h, :], oap[:, 0:128], coef[:, h:h + 1], oacc[:, h, :], ALU.mult, ALU.add,
                                [opk, ("coef", h), ("oacc", h)], [("onb", h)])
            pb7 = ps[7][:].bitcast(BF16)
            for h in range(4):
                tr(pb7[:, h * 128:(h + 1) * 128], onb[:, h, :], ident[:], [("onb", h), "ident"], PS(7))
            evac(oTn[:, :, qsl], pb7[:, 0:512].rearrange("p (h c) -> p h c", c=128), [PS(7)], ["oTn"])
            pump_cast(1, ["negselT"])
        for h in range(4):
            P.dma("sp", oT_dram[(6 + h) * 128:(7 + h) * 128, :], oTn[:, h, :], reads=["oTn"], writes=[("dram", "oT", 6 + h)])

    def phase_dsa(l):
        P.barrier()
        A.release(0)
        win = wb["w_in"][l]
        wk_in = wkeys("w_in", l)
        qTd = A.bf([128, 6, S])
        kdT = A.bf([128, S])
        vd = A.bf([128, 16, 130])
        ikT = A.bf([128, S])
        iw = A.f32([128, 16, 16])
        m0 = A.mark()
        tabs = load_tables128()
        alloc_ropetmp()
        new_wslots(16 * 512 * 2)
        ckvnT = A.bf([128, 2, S])
        ckvn = [A.bf([128, 256]) for _ in range(2)]
        gB = A.f32([128, 256])
        ssq = A.f32([128, 4])
        junk = A.f32([128, 256])
        kvup = A.bf([128, 2, 256])
        memset(vd[:, :, 128:129], 1.0, ["vd1"])
        P.dma("sp", gB[:, :], small["dsa_kv_norm"][l:l + 1, :].to_broadcast([128, 256]), writes=["gB"])
        P.dma("sp", kvup[:, :, :], wb["dsa_kv_up"][l].rearrange("(k p) c -> p k c", p=128), reads=wkeys("dsa_kv_up", l), writes=["kvup"])
        wt, wk = load_w_cols(win, OFF["dsa_q"], 512, 16, wk_in)
        for h in range(4):
            proj_F(wt, wk, h * 128, 128, lambda pa, pk, tb, h=h: rope128(pa, pk, qTd[:, h, tb * 512:(tb + 1) * 512], "qTd", tb, tabs))
        wt, wk = load_w_cols(win, OFF["dsa_q"] + 512, 512, 16, wk_in)
        for h in range(2):
            proj_F(wt, wk, h * 128, 128, lambda pa, pk, tb, h=h: rope128(pa, pk, qTd[:, 4 + h, tb * 512:(tb + 1) * 512], "qTd", tb, tabs))

        def ckv_epi(pa, pk, t0, g):
            for j in range(g):
                t_ = t0 + j
                i2 = t_ % 2
                pj = pa[:, j * 256:(j + 1) * 256]
                act(junk[:, :], pj, AF.Square, [pk], ["junk", ("ssq", i2)], accum=ssq[:, i2:i2 + 1])
                ts(ssq[:, i2:i2 + 1], ssq[:, i2:i2 + 1], 1.0 / 256.0, 1e-6, ALU.mult, ALU.add, [("ssq", i2)], [("ssq", i2)])
                act(ssq[:, i2:i2 + 1], ssq[:, i2:i2 + 1], AF.Sqrt, [("ssq", i2)], [("ssq", i2)])
                P.add("dve", lambda e, i2=i2: e.reciprocal(out=ssq[:, i2:i2 + 1], in_=ssq[:, i2:i2 + 1]), reads=[("ssq", i2)], writes=[("ssq", i2)])
                stt(ckvn[i2][:, :], pj, ssq[:, i2:i2 + 1], gB[:, :], ALU.mult, ALU.mult, [pk, ("ssq", i2), "gB"], [("ckvn", i2)])
                pb = ps[6 + i2][:].bitcast(BF16)
                for c in range(2):
                    tr(pb[:, c * 128:(c + 1) * 128], ckvn[i2][:, c * 128:(c + 1) * 128], ident[:], [("ckvn", i2), "ident"], PS(6 + i2))
                evac(ckvnT[:, :, t_ * 128:(t_ + 1) * 128], pb[:, 0:256].rearrange("p (c t) -> p c t", t=128), [PS(6 + i2)], ["ckvnT"])

        proj_T(wt, wk, 256, 256, ckv_epi, group=2)
        proj_F(kvup, "kvup", 0, 128, lambda pa, pk, tb: rope128(pa, pk, kdT[:, tb * 512:(tb + 1) * 512], "kdT", tb, tabs),
               actT=ckvnT, akey="ckvnT", nk=2)
        proj_T(kvup, "kvup", 128, 128, lambda pa, pk, t0, g: evac(vd[:, t0:t0 + g, 0:128], pa.rearrange("p (g c) -> p g c", c=128), [pk], ["vd"]),
               actT=ckvnT, akey="ckvnT", nk=2)
        P.barrier()
        A.release(m0)
        iqT = A.bf([128, 8, S])
        m0 = A.mark()
        tabsi = load_tables64()
        alloc_ropetmp()
        new_wslots(16 * 512 * 2)
        wt, wk = load_w_cols(win, OFF["idx_k"], 80, 16, wk_in)
        proj_F(wt, wk, 0, 64, lambda pa, pk, tb: rope64(pa, pk, [ikT[0:64, tb * 512:(tb + 1) * 512], ikT[64:128, tb * 512:(tb + 1) * 512]], "ikT", tb, tabsi))
        proj_T(wt, wk, 64, 16, lambda pa, pk, t0, g: act(iw[:, t0:t0 + g, :], pa.rearrange("p (g c) -> p g c", c=16), AF.Copy, [pk], ["iw"], scale=1.0 / 32.0), group=16)
        for half in range(2):
            wt, wk = load_w_cols(win, OFF["idx_q"] + half * 512, 512, 16, wk_in)
            for hh in range(8):
                h = half * 8 + hh
                proj_F(wt, wk, hh * 64, 64, lambda pa, pk, tb, h=h: rope64(
                    pa, pk, [iqT[64 * (h % 2):64 * (h % 2) + 64, h // 2, tb * 512:(tb + 1) * 512]], "iqT", tb, tabsi))
        P.barrier()
        A.release(m0)
        scA = A.f32([128, S])
        scB = A.f32([128, S])
        rel = [A.f32([128, 512]) for _ in range(2)]
        negm = [A.bf([128, S]) for _ in range(2)]
        m8 = A.f32([128, 8])
        bis = A.f32([128, 8])
        pts = [(A.bf([128, 384]), ("pT", i)) for i in range(2)]
        odb = A.bf([128, 6, 128])
        oTd = [A.bf([128, 6, 128]) for _ in range(2)]
        rec = A.f32([128, 6])
        def dsa_prep(qt):
            qsl = slice(qt * 128, (qt + 1) * 128)
            nk_ = (qt + 1) * 128
            nm = negm[qt % 2]
            nmk = ("negm", qt % 2)
            if qt >= 2:
                for kb in range(0, nk_, 512):
                    n = min(512, nk_ - kb)
                    for h in range(16):
                        b = nextps(0, 2)
                        pb_ = 64 * (h % 2)
                        mm(ps[b][:, 0:n], iqT[pb_:pb_ + 64, h // 2, qsl], ikT[pb_:pb_ + 64, kb:kb + n], True, True, ["iqT", "ikT"], PS(b))
                        r_ = rel[h % 2]
                        act(r_[:, 0:n], ps[b][:, 0:n], AF.Relu, [PS(b)], [("rel", h % 2)])
                        if h == 0:
                            ts(scA[:, kb:kb + n], r_[:, 0:n], iw[:, qt, h:h + 1], None, ALU.mult, None, [("rel", h % 2), "iw"], [("scA", kb)])
                        else:
                            stt(scA[:, kb:kb + n], r_[:, 0:n], iw[:, qt, h:h + 1], scA[:, kb:kb + n], ALU.mult, ALU.add,
                                [("rel", h % 2), "iw", ("scA", kb)], [("scA", kb)])
                sck = [("scA", kb) for kb in range(0, nk_, 512)]
                P.add("dve", lambda e, nk_=nk_: e.tensor_reduce(out=bis[:, 0:1], in_=scA[:, 0:nk_], axis=mybir.AxisListType.X, op=ALU.max),
                      reads=sck, writes=["bis_hi"])
                P.add("dve", lambda e, nk_=nk_: e.tensor_reduce(out=bis[:, 1:2], in_=scA[:, 0:nk_], axis=mybir.AxisListType.X, op=ALU.min),
                      reads=sck, writes=["bis_lo"])
                tt(bis[:, 2:3], bis[:, 0:1], bis[:, 1:2], ALU.subtract, ["bis_hi", "bis_lo"], ["bis_w"])
                tt(scA[:, qt * 128:nk_], scA[:, qt * 128:nk_], negC[:, :], ALU.add, sck + ["negC", "bis_hi", "bis_lo"], sck)
                for it in range(22):
                    ck = float(2.0 ** -(it + 1))
                    ts(bis[:, 3:4], bis[:, 2:3], ck, bis[:, 1:2], ALU.mult, ALU.add, ["bis_w", "bis_lo"], ["bis_mid"])
                    P.add("dve", lambda e, nk_=nk_: e.tensor_scalar(out=scB[:, 0:nk_], in0=scA[:, 0:nk_], scalar1=bis[:, 3:4], scalar2=None,
                                                                    op0=ALU.is_ge, op1=ALU.add, accum_out=bis[:, 4:5]),
                          reads=sck + ["bis_mid"], writes=["scB", "bis_cnt"])
                    ts(bis[:, 5:6], bis[:, 4:5], 256.0, bis[:, 2:3], ALU.is_ge, ALU.mult, ["bis_cnt", "bis_w"], ["bis_tmp"])
                    stt(bis[:, 1:2], bis[:, 5:6], ck, bis[:, 1:2], ALU.mult, ALU.add, ["bis_tmp", "bis_lo"], ["bis_lo"])
                ts(nm[:, 0:nk_], scA[:, 0:nk_], bis[:, 1:2], NEG, ALU.is_lt, ALU.mult, sck + ["bis_lo"], [nmk])

        def dsa_attn(qt):
            qsl = slice(qt * 128, (qt + 1) * 128)
            nk_ = (qt + 1) * 128
            nm = negm[qt % 2]
            nmk = ("negm", qt % 2)
            kts = []
            for kt in range(qt + 1):
                masks = []
                if qt >= 2:
                    masks.append((nm[:, kt * 128:(kt + 1) * 128], ident[:], [nmk, "ident"]))
                if kt == qt:
                    masks.append((ident[:], maskC[:], ["ident", "maskC"]))
                kts.append(dict(kT=(kdT[:, kt * 128:(kt + 1) * 128], "kdT"), ns=128, V=(vd[:, kt, 0:129], ["vd", "vd1"]), masks=masks))
            for gi, hs in enumerate(((0, 1, 2), (3, 4, 5))):
                res = attn_qtile(list(hs), lambda h, qsl=qsl: (qTd[:, h, qsl], "qTd"), kts, 129, pts, (4, 5), 2 + gi)
                for hi, h in enumerate(hs):
                    oap, opk = res[hi]
                    P.add("dve", lambda e, h=h, oap=oap: e.reciprocal(out=rec[:, h:h + 1], in_=oap[:, 128:129]), reads=[opk, "vd1"], writes=[("rec", h)])
                    ts(odb[:, h, :], oap[:, 0:128], rec[:, h:h + 1], None, ALU.mult, None, [opk, ("rec", h)], [("odb", h)])
            ot = oTd[qt % 2]
            otk = ("oTd", qt % 2)
            for half in range(2):
                bnk = 6 + half
                pb = ps[bnk][:].bitcast(BF16)
                for j in range(3):
                    h = half * 3 + j
                    tr(pb[:, j * 128:(j + 1) * 128], odb[:, h, :], ident[:], [("odb", h), "ident"], PS(bnk))
                evac(ot[:, half * 3:half * 3 + 3, :], pb[:, 0:384].rearrange("p (h c) -> p h c", c=128), [PS(bnk)], [otk])
            P.dma("sp", oT_dram[10 * 128:16 * 128, qsl].rearrange("(h d) t -> d h t", d=128), ot[:, :, :], reads=[otk],
                  writes=[("dram", "oT", 10 + qt * 0)])
            pump_cast(1, [otk])

        dsa_prep(0)
        for qt in range(16):
            if qt + 1 < 16:
                dsa_prep(qt + 1)
            dsa_attn(qt)

    def phase_merge(l):
        P.barrier()
        A.release(0)
        win = wb["w_in"][l]
        wk_in = wkeys("w_in", l)
        oT = A.bf([128, 16, S])
        new_wslots(16 * 512 * 2, n=3)
        sg = [A.f32([128, 512]) for _ in range(3)]
        macc = [A.f32([128, 512]) for _ in range(1)] * 2
        mix = [A.bf([128, S]) for _ in range(1)] * 2
        for h in range(16):
            P.dma("sp", oT[:, h, :], oT_dram[h * 128:(h + 1) * 128, :], reads=[("dram", "oT", min(h, 10))], writes=["oT"])
        brs = (("w_br_fox", 0, 6), ("w_br_nsa", 6, 4), ("w_br_dsa", 10, 6))
        def load_merge(c):
            buf, wk = wslot()
            wt = buf[:, 0:16 * 512].rearrange("p (k c) -> p k c", c=512)
            for bi, (wn, h0, nh) in enumerate(brs):
                P.dma("sp", wt[:, h0:h0 + nh, 0:128], wb[wn][l][:, c * 128:(c + 1) * 128].rearrange("(k p) c -> p k c", p=128),
                      reads=wkeys(wn, l), writes=[wk])
                g0 = OFF["gate"] + bi * 2048 + c * 128
                P.dma("sp", wt[:, :, 128 * (bi + 1):128 * (bi + 2)], win[:, g0:g0 + 128].rearrange("(k p) c -> p k c", p=128),
                      reads=wk_in, writes=[wk])
            return wt, wk

        pend = [load_merge(0), load_merge(1)]
        for c in range(16):
            wt, wk = pend.pop(0)
            if c + 2 < 16:
                pend.append(load_merge(c + 2))
            mx = mix[0]
            mxk = ("mix", 0)
            for tb in range(4):
                tsl = slice(tb * 512, (tb + 1) * 512)
                mk = ("macc", 0)
                ma = macc[0]
                for bi, (wn, h0, nh) in enumerate(brs):
                    bg = nextps(0, 8)
                    for kc in range(16):
                        mm(ps[bg][:, :], wt[:, kc, 128 * (bi + 1):128 * (bi + 2)], hT[:, kc, tsl], kc == 0, kc == 15, [wk] + hkeys(tb * 4, 4), PS(bg))
                    act(sg[bi][:, :], ps[bg][:, :], AF.Sigmoid, [PS(bg)], [("sg", bi)])
                    bo = nextps(0, 8)
                    for j in range(nh):
                        mm(ps[bo][:, :], wt[:, h0 + j, 0:128], oT[:, h0 + j, tsl], j == 0, j == nh - 1, [wk, "oT"], PS(bo))
                    if bi == 0:
                        tt(ma[:, :], ps[bo][:, :], sg[bi][:, :], ALU.mult, [PS(bo), ("sg", bi)], [mk])
                    else:
                        tt(sg[bi][:, :], ps[bo][:, :], sg[bi][:, :], ALU.mult, [PS(bo), ("sg", bi)], [("sg", bi)])
                        if bi == 1:
                            tt(ma[:, :], ma[:, :], sg[bi][:, :], ALU.add, [mk, ("sg", bi)], [mk])
                        else:
                            tt(mx[:, tsl], ma[:, :], sg[bi][:, :], ALU.add, [mk, ("sg", bi)], [mxk])
            P.dma("sp", mixT_dram[c * 128:(c + 1) * 128, :], mx[:, :], reads=[mxk], writes=[("dram", "mixT", c)])
            pump_cast(1, [mxk])

    def gemm_resid(actT, akey, nk, wsrc2d, wkl, ncg, cgw, tiles, hsrc):
        hin = [A.f32([128, cgw]) for _ in range(2)]
        rout = [A.f32([128, cgw]) for _ in range(2)]
        k_ = 0
        depth_ = len(wslots["v"]) - 1
        pend = [load_w_cols(wsrc2d, g * cgw, cgw, nk, wkl) for g in range(min(depth_, ncg))]
        for cg in range(ncg):
            wt, wk = pend.pop(0)
            if cg + depth_ < ncg:
                pend.append(load_w_cols(wsrc2d, (cg + depth_) * cgw, cgw, nk, wkl))
            for t_ in tiles:
                i2 = k_ % 2
                k_ += 1
                P.dma("sp", hin[i2][:, :], hsrc[t_ * 128:(t_ + 1) * 128, cg * cgw:(cg + 1) * cgw], reads=[("dram", "h", t_)], writes=[("hin", i2)])
                b = nextps(0, 8)
                for kc in range(nk):
                    mm(ps[b][:, 0:cgw], actT(kc, t_), wt[:, kc, :], kc == 0, kc == nk - 1, [wk, akey], PS(b))
                stt(rout[i2][:, :], hin[i2][:, :], ALPHA, ps[b][:, 0:cgw], ALU.mult, ALU.add, [("hin", i2), PS(b)], [("rout", i2)])
                P.dma("sp", r_dram[t_ * 128:(t_ + 1) * 128, cg * cgw:(cg + 1) * cgw], rout[i2][:, :], reads=[("rout", i2)], writes=[("dram", "r", t_)])

    def layernorm_pass(l, gname, bname, tiles, hdst, write_y=False):
        gB = A.f32([128, D])
        bB = A.f32([128, D])
        rt = [A.f32([128, D]) for _ in range(2)]
        hb = [A.bf([128, D]) for _ in range(2)]
        st = A.f32([128, 2, 4, 6])
        mv = A.f32([128, 2, 2])
        P.dma("sp", gB[:, :], small[gname][l:l + 1, :].to_broadcast([128, D]), writes=["lnG"])
        P.dma("sp", bB[:, :], small[bname][l:l + 1, :].to_broadcast([128, D]), writes=["lnB"])
        for k_, t_ in enumerate(tiles):
            i2 = k_ % 2
            r_ = rt[i2]
            rk = ("rt", i2)
            P.dma("sp", r_[:, :], r_dram[t_ * 128:(t_ + 1) * 128, :], reads=[("dram", "r", t_)], writes=[rk])
            for c in range(4):
                P.add("dve", lambda e, c=c, r_=r_, i2=i2: e.bn_stats(out=st[:, i2, c, :], in_=r_[:, c * 512:(c + 1) * 512]), reads=[rk], writes=[("st", i2)])
            P.add("dve", lambda e, i2=i2: e.bn_aggr(out=mv[:, i2, :], in_=st[:, i2, :, :].rearrange("p a b -> p (a b)")), reads=[("st", i2)], writes=[("mv", i2)])
            ts(mv[:, i2, 1:2], mv[:, i2, 1:2], 1e-5, None, ALU.add, None, [("mv", i2)], [("mv", i2)])
            act(mv[:, i2, 1:2], mv[:, i2, 1:2], AF.Sqrt, [("mv", i2)], [("mv", i2)])
            P.add("dve", lambda e, i2=i2: e.reciprocal(out=mv[:, i2, 1:2], in_=mv[:, i2, 1:2]), reads=[("mv", i2)], writes=[("mv", i2)])
            ts(r_[:, :], r_[:, :], mv[:, i2, 0:1], mv[:, i2, 1:2], ALU.subtract, ALU.mult, [rk, ("mv", i2)], [rk])
            tt(r_[:, :], r_[:, :], gB[:, :], ALU.mult, [rk, "lnG"], [rk])
            tt(r_[:, :], r_[:, :], bB[:, :], ALU.add, [rk, "lnB"], [rk])
            P.dma("sp", hdst[t_ * 128:(t_ + 1) * 128, :], r_[:, :], reads=[rk], writes=[("dram", "h", t_)])
            act(hb[i2][:, :], r_[:, :], AF.Copy, [rk], [("hb", i2)])
            for q4 in range(4):
                bnk = 4 + (k_ * 4 + q4) % 4
                pb = ps[bnk][:].bitcast(BF16)
                for j in range(4):
                    kc = q4 * 4 + j
                    tr(pb[:, j * 128:(j + 1) * 128], hb[i2][:, kc * 128:(kc + 1) * 128], ident[:], [("hb", i2), "ident"], PS(bnk))
                evac(hT[:, q4 * 4:q4 * 4 + 4, t_ * 128:(t_ + 1) * 128], pb[:, 0:512].rearrange("p (k t) -> p k t", t=128), [PS(bnk)], [("hT", t_)])

    def phase_outproj(l):
        P.barrier()
        A.release(0)
        mixT = A.bf([128, 16, S])
        new_wslots(16 * 512 * 2)
        for c in range(16):
            P.dma("sp", mixT[:, c, :], mixT_dram[c * 128:(c + 1) * 128, :], reads=[("dram", "mixT", c)], writes=["mixT"])
        gemm_resid(lambda kc, t_: mixT[:, kc, t_ * 128:(t_ + 1) * 128], "mixT", 16, wb["w_out"][l], wkeys("w_out", l), 4, 512, range(NT), h_dram)
        P.barrier()
        A.release(0)
        layernorm_pass(l, "ln1_g", "ln1_b", range(NT), h_dram)

    def phase_ffn(l):
        w1 = wb["w_ffn_in"][l]
        w2 = wb["w_ffn_out"][l]
        k1 = wkeys("w_ffn_in", l)
        k2 = wkeys("w_ffn_out", l)
        for tb in range(4):
            P.barrier()
            A.release(0)
            uT = A.bf([128, 44, 512])
            new_wslots(44 * 256 * 2, n=3)
            sa = [A.f32([128, 512]) for _ in range(2)]
            tsl = slice(tb * 512, (tb + 1) * 512)

            def load_ab(c):
                buf, wk = wslot()
                wt = buf[:, 0:16 * 256].rearrange("p (k c) -> p k c", c=256)
                P.dma("sp", wt[:, :, 0:128], w1[:, c * 128:(c + 1) * 128].rearrange("(k p) c -> p k c", p=128), reads=k1, writes=[wk])
                P.dma("sp", wt[:, :, 128:256], w1[:, DFF + c * 128:DFF + (c + 1) * 128].rearrange("(k p) c -> p k c", p=128), reads=k1, writes=[wk])
                return wt, wk

            pend = [load_ab(0), load_ab(1)]
            for c in range(44):
                wt, wk = pend.pop(0)
                if c + 2 < 44:
                    pend.append(load_ab(c + 2))
                ba = nextps(0, 8)
                for kc in range(16):
                    mm(ps[ba][:, :], wt[:, kc, 0:128], hT[:, kc, tsl], kc == 0, kc == 15, [wk] + hkeys(tb * 4, 4), PS(ba))
                bb = nextps(0, 8)
                for kc in range(16):
                    mm(ps[bb][:, :], wt[:, kc, 128:256], hT[:, kc, tsl], kc == 0, kc == 15, [wk] + hkeys(tb * 4, 4), PS(bb))
                s_ = sa[c % 2]
                act(s_[:, :], ps[ba][:, :], AF.Silu, [PS(ba)], [("sa", c % 2)])
                tt(uT[:, c, :], s_[:, :], ps[bb][:, :], ALU.mult, [("sa", c % 2), PS(bb)], [("uT", c)])
            P.add("dve", lambda e: e.memset(sa[0][:, 0:1], 0.0), reads=[("uT", c) for c in range(44)] + [("sa", 0)], writes=["uTall", ("sa", 0)])
            gemm_resid(lambda kc, t_: uT[:, kc, (t_ % 4) * 128:(t_ % 4 + 1) * 128], "uTall", 44, w2, k2, 8, 256,
                       range(tb * 4, tb * 4 + 4), h_dram)
        P.barrier()
        A.release(0)
        layernorm_pass(l, "ln2_g", "ln2_b", range(NT), h_dram)

    def to_hT_tile(src, skey, t_, hbuf, hkey, k_):
        act(hbuf[:, :], src, AF.Copy, [skey], [hkey])
        for q4 in range(4):
            bnk = 4 + (k_ * 4 + q4) % 4
            pb = ps[bnk][:].bitcast(BF16)
            for j in range(4):
                kc = q4 * 4 + j
                tr(pb[:, j * 128:(j + 1) * 128], hbuf[:, kc * 128:(kc + 1) * 128], ident[:], [hkey, "ident"], PS(bnk))
            evac(hT[:, q4 * 4:q4 * 4 + 4, t_ * 128:(t_ + 1) * 128], pb[:, 0:512].rearrange("p (k t) -> p k t", t=128),
                 [PS(bnk)], [("hT", t_)])

    def phase_ple(l, last):
        P.barrier()
        A.release(0)
        new_wslots(16 * 512 * 2)
        pT = A.bf([128, 2, S])
        pin = [A.f32([128, 256]) for _ in range(2)]
        pbf = [A.bf([128, 256]) for _ in range(2)]
        wpl = A.bf([128, 2, D])
        hin = [A.f32([128, 512]) for _ in range(2)]
        sg = [A.f32([128, 512]) for _ in range(2)]
        P.dma("sp", wpl[:, :, :], wb["w_ple_in"][l].rearrange("(k p) c -> p k c", p=128), reads=wkeys("w_ple_in", l), writes=["wpl"])
        for t_ in range(NT):
            i2 = t_ % 2
            P.dma("sp", pin[i2][:, :], p_d[l, t_ * 128:(t_ + 1) * 128, :], writes=[("pin", i2)])
            vcopy(pbf[i2][:, :], pin[i2][:, :], [("pin", i2)], [("pbf", i2)])
            pb = ps[6 + i2][:].bitcast(BF16)
            for c in range(2):
                tr(pb[:, c * 128:(c + 1) * 128], pbf[i2][:, c * 128:(c + 1) * 128], ident[:], [("pbf", i2), "ident"], PS(6 + i2))
            evac(pT[:, :, t_ * 128:(t_ + 1) * 128], pb[:, 0:256].rearrange("p (c t) -> p c t", t=128), [PS(6 + i2)], ["pT"])
        wg = wb["w_ple_gate"][l]
        kg = wkeys("w_ple_gate", l)
        k_ = 0
        nxt = load_w_cols(wg, 0, 512, 16, kg)
        for cg in range(4):
            wt, wk = nxt
            if cg + 1 < 4:
                nxt = load_w_cols(wg, (cg + 1) * 512, 512, 16, kg)
            for t_ in range(NT):
                i2 = k_ % 2
                k_ += 1
                P.dma("sp", hin[i2][:, :], h_dram[t_ * 128:(t_ + 1) * 128, cg * 512:(cg + 1) * 512], reads=[("dram", "h", t_)], writes=[("hin", i2)])
                bg = nextps(0, 4)
                for kc in range(16):
                    mm(ps[bg][:, :], hT[:, kc, t_ * 128:(t_ + 1) * 128], wt[:, kc, :], kc == 0, kc == 15, [wk, ("hT", t_)], PS(bg))
                be = nextps(0, 4)
                for kc in range(2):
                    mm(ps[be][:, :], pT[:, kc, t_ * 128:(t_ + 1) * 128], wpl[:, kc, cg * 512:(cg + 1) * 512], kc == 0, kc == 1, ["pT", "wpl"], PS(be))
                act(sg[i2][:, :], ps[bg][:, :], AF.Sigmoid, [PS(bg)], [("sg", i2)])
                tt(sg[i2][:, :], sg[i2][:, :], ps[be][:, :], ALU.mult, [("sg", i2), PS(be)], [("sg", i2)])
                tt(hin[i2][:, :], hin[i2][:, :], sg[i2][:, :], ALU.add, [("hin", i2), ("sg", i2)], [("hin", i2)])
                if last:
                    finals.append(P.dma("sp", y_d[t_ * 128:(t_ + 1) * 128, cg * 512:(cg + 1) * 512], hin[i2][:, :],
                                        reads=[("hin", i2)], writes=[("dram", "y", t_, cg)]))
                else:
                    P.dma("sp", r_dram[t_ * 128:(t_ + 1) * 128, cg * 512:(cg + 1) * 512], hin[i2][:, :], reads=[("hin", i2)],
                          writes=[("dram", "r", t_)])
        if not last:
            P.barrier()
            A.release(0)
            rt = [A.f32([128, D]) for _ in range(2)]
            hb = [A.bf([128, D]) for _ in range(2)]
            for t_ in range(NT):
                i2 = t_ % 2
                P.dma("sp", rt[i2][:, :], r_dram[t_ * 128:(t_ + 1) * 128, :], reads=[("dram", "r", t_)], writes=[("rt", i2)])
                P.dma("sp", h_dram[t_ * 128:(t_ + 1) * 128, :], rt[i2][:, :], reads=[("rt", i2)], writes=[("dram", "h", t_)])
                to_hT_tile(rt[i2][:, :], ("rt", i2), t_, hb[i2], ("hb", i2), t_)

    finals = []
    mark_id = [0]

    def mark(name):
        if not marks:
            return
        mark_id[0] += 1
        n = 1 + mark_id[0]
        marks_out.append((name, n))
        mm(ps[7][0:1, 0:n], ident[0:1, 0:1], ident[0:1, 0:n], True, True, ["ident"], PS(7))

    cast_layer(0)
    pump_cast(10 ** 9)
    A.release(0)
    xin = [A.f32([128, D]) for _ in range(2)]
    xb = [A.bf([128, D]) for _ in range(2)]
    for t_ in range(NT):
        i2 = t_ % 2
        P.dma("sp", xin[i2][:, :], x_d[t_ * 128:(t_ + 1) * 128, :], writes=[("xin", i2)])
        P.dma("sp", h_dram[t_ * 128:(t_ + 1) * 128, :], xin[i2][:, :], reads=[("xin", i2)], writes=[("dram", "h", t_)])
        to_hT_tile(xin[i2][:, :], ("xin", i2), t_, xb[i2], ("xb", i2), t_)

    def dbg_dump(name, src_ap, keys):
        if name in dbg_d:
            P.barrier()
            finals.append(P.dma("sp", dbg_d[name], src_ap, reads=keys, writes=[("dram", "dbg", name)]))

    for l in range(depth):
        if l + 1 < depth:
            cast_layer(l + 1)
        mark("fox%d" % l)
        phase_fox(l)
        mark("nsa%d" % l)
        phase_nsa(l)
        mark("dsa%d" % l)
        phase_dsa(l)
        mark("merge%d" % l)
        if l == 0:
            dbg_dump("oT", oT_dram, [("dram", "oT", h) for h in range(11)])
        phase_merge(l)
        if l == 0:
            dbg_dump("mixT", mixT_dram, [("dram", "mixT", c) for c in range(16)])
        mark("outproj%d" % l)
        pump_cast(10 ** 9)
        phase_outproj(l)
        mark("ffn%d" % l)
        if l == 0:
            dbg_dump("h1", h_dram, [("dram", "h", t) for t in range(NT)])
        phase_ffn(l)
        if l == 0:
            dbg_dump("h2", h_dram, [("dram", "h", t) for t in range(NT)])
        pump_cast(10 ** 9)
        mark("ple%d" % l)
        phase_ple(l, l == depth - 1)
        mark("end%d" % l)
    P.emit(nc, ctx, final_wait_ops=finals)
    return nc, ctx


_CACHE = {}


def kernel(**inputs):
    n_cores = 8
    if "nc" not in _CACHE:
        _CACHE["nc"] = build(DEPTH)
    nc, _ctx = _CACHE["nc"]
    consts = make_consts()
    shared = {}
    for n in WSHAPES:
        shared[n] = np.ascontiguousarray(np.asarray(inputs[n], dtype=np.float32))
    for n in SMALL:
        shared[n] = np.ascontiguousarray(np.asarray(inputs[n], dtype=np.float32))
    for n, v in consts.items():
        shared["c_" + n] = np.ascontiguousarray(v)
    x = np.asarray(inputs["x"], dtype=np.float32)
    p = np.asarray(inputs["p"], dtype=np.float32)
    zeros = {k: np.zeros_like(v) for k, v in shared.items()}
    zx = np.zeros_like(np.ascontiguousarray(x[0]))
    zp = np.zeros_like(np.ascontiguousarray(p[:, 0]))
    in_maps = []
    for c in range(n_cores):
        if c % 2 == 0:
            b = c // 2
            m = dict(shared)
            m["x"] = np.ascontiguousarray(x[b])
            m["p"] = np.ascontiguousarray(p[:, b])
        else:
            m = dict(zeros)
            m["x"] = zx
            m["p"] = zp
        in_maps.append(m)
    res = run_bass_kernel_spmd(nc, in_maps, core_ids=list(range(n_cores)))
    out = np.stack([np.asarray(res.results[2 * b]["y"], dtype=np.float32) for b in range(4)], axis=0)
    return out
```
